# Optimizing a Trainium2 kernel written in Bass

```python
import math
import jax, jax.numpy as jnp
from jax import lax
import numpy as np

D_MODEL = 1024
BATCH = 4
SEQ = 4096
DEPTH = 1

A_HEADS = 8
A_HEAD_DIM = 64
A_V_DIM = 2 * A_HEAD_DIM
A_QK_W = A_HEADS * 2 * A_HEAD_DIM
A_V_W = A_HEADS * A_V_DIM
LAMBDA_INIT_STD = 0.1
Q_BLOCK = 128
B_HEADS = 8
B_HEAD_DIM = 128
B_W = B_HEADS * B_HEAD_DIM
CONV_WIDTH = 4
CHUNK = 64
D_FF = 2816
FFN_CONV_WIDTH = 3
ROPE_THETA = 10000.0
NORM_EPS = 1e-6
IN_SIZES = (A_QK_W, A_QK_W, A_V_W, B_W, B_W, B_W, B_W, B_HEADS, B_HEADS, D_MODEL, D_MODEL)
D_IN = A_QK_W * 2 + A_V_W + B_W * 4 + B_HEADS * 2 + D_MODEL * 2

kernel_name = "hybrid_diffattn_gdn_convffn_block"


def rms_norm(x, w):
    xf = x.astype(jnp.float32)
    y = xf * lax.rsqrt(jnp.mean(xf * xf, axis=-1, keepdims=True) + NORM_EPS)
    return (y * w.astype(jnp.float32)).astype(x.dtype)


def l2_normalize(x):
    xf = x.astype(jnp.float32)
    return xf * lax.rsqrt(jnp.sum(xf * xf, axis=-1, keepdims=True) + NORM_EPS)


def causal_dwconv(x, w):
    k_w, c = w.shape
    return lax.conv_general_dilated(
        x, w[:, None, :].astype(x.dtype), window_strides=(1,), padding=[(k_w - 1, 0)],
        dimension_numbers=("NWC", "WIO", "NWC"), feature_group_count=c)


def rotary(x, positions):
    d = x.shape[-1]
    inv_freq = ROPE_THETA ** (-jnp.arange(0, d, 2, dtype=jnp.float32) / d)
    ang = positions.astype(jnp.float32)[..., None] * inv_freq
    cos = jnp.cos(ang)[:, :, None, :]
    sin = jnp.sin(ang)[:, :, None, :]
    xf = x.astype(jnp.float32)
    x1, x2 = xf[..., : d // 2], xf[..., d // 2:]
    return jnp.concatenate([x1 * cos - x2 * sin, x2 * cos + x1 * sin], axis=-1).astype(x.dtype)


def diff_attention(q, k, v, lam):
    b, s, h, _, d = q.shape
    nb = s // Q_BLOCK
    qb = q.transpose(0, 2, 3, 1, 4).reshape(b, h, 2, nb, Q_BLOCK, d).transpose(3, 0, 1, 2, 4, 5)
    kt = k.transpose(0, 2, 3, 1, 4)
    vt = v.transpose(0, 2, 1, 3)
    key_pos = jnp.arange(s)
    scale = d ** -0.5

    def block(args):
        q_blk, blk = args
        sc = jnp.einsum("bhmqd,bhmkd->bhmqk", q_blk, kt).astype(jnp.float32) * scale
        q_pos = blk * Q_BLOCK + jnp.arange(Q_BLOCK)
        causal = key_pos[None, :] <= q_pos[:, None]
        sc = jnp.where(causal, sc, -jnp.inf)
        p = jax.nn.softmax(sc, axis=-1)
        a = (p[:, :, 0] - lam * p[:, :, 1]).astype(vt.dtype)
        return jnp.einsum("bhqk,bhkd->bhqd", a, vt)

    o = lax.map(block, (qb, jnp.arange(nb)))
    return o.transpose(1, 0, 3, 2, 4).reshape(b, s, h, 2 * d)


def gated_delta_rule(q, k, v, beta, g):
    b, h, s, dk = q.shape
    dv = v.shape[-1]
    n = s // CHUNK
    kb = k * beta[..., None]
    vb = v * beta[..., None]
    ch = lambda t: t.reshape(b, h, n, CHUNK, t.shape[-1])
    q, k, kb, vb = ch(q), ch(k), ch(kb), ch(vb)
    gc = jnp.cumsum(g.reshape(b, h, n, CHUNK), axis=-1)
    tri = jnp.tril(jnp.ones((CHUNK, CHUNK), dtype=bool))
    strict = jnp.tril(jnp.ones((CHUNK, CHUNK), dtype=bool), k=-1)
    diff = gc[..., :, None] - gc[..., None, :]
    decay = jnp.where(tri, jnp.exp(jnp.where(tri, diff, 0.0)), 0.0)
    a_mat = jnp.where(strict, jnp.einsum("bhnid,bhnjd->bhnij", kb, k) * decay, 0.0)
    eye = jnp.eye(CHUNK, dtype=jnp.float32)
    rhs = jnp.concatenate([vb, kb * jnp.exp(gc)[..., None]], axis=-1)
    sol = lax.linalg.triangular_solve(eye + a_mat, rhs, left_side=True, lower=True, unit_diagonal=True)
    u, w = sol[..., :dv], sol[..., dv:]
    qk = jnp.where(tri, jnp.einsum("bhnid,bhnjd->bhnij", q, k) * decay, 0.0)

    def step(state, inp):
        q_i, k_i, u_i, w_i, qk_i, g_i = inp
        v_new = u_i - jnp.einsum("bhcd,bhde->bhce", w_i, state)
        o = (jnp.einsum("bhcd,bhde->bhce", q_i * jnp.exp(g_i)[..., None], state)
             + jnp.einsum("bhcj,bhje->bhce", qk_i, v_new))
        g_last = g_i[..., -1]
        state = (state * jnp.exp(g_last)[..., None, None]
                 + jnp.einsum("bhcd,bhce->bhde", k_i * jnp.exp(g_last[..., None] - g_i)[..., None], v_new))
        return state, o

    mv = lambda t: jnp.moveaxis(t, 2, 0)
    state0 = jnp.zeros((b, h, dk, dv), jnp.float32)
    _, o = lax.scan(step, state0, (mv(q), mv(k), mv(u), mv(w), mv(qk), mv(gc)))
    return jnp.moveaxis(o, 0, 2).reshape(b, h, s, dv)


def setup_inputs(seed: int = 0) -> dict:
    key = jax.random.key(seed)
    ks = jax.random.split(key, 24)
    f32 = jnp.float32
    nrm = lambda k, shape, fan_in: jax.random.normal(k, shape, f32) * fan_in ** -0.5
    gain = lambda k, shape: 1.0 + 0.02 * jax.random.normal(k, shape, f32)
    x = jax.random.normal(ks[0], (BATCH, SEQ, D_MODEL), f32)
    offsets = jax.random.randint(ks[1], (BATCH, 1), 0, 1024, dtype=jnp.int32)
    positions = (jnp.arange(SEQ, dtype=jnp.int32)[None, :] + offsets).astype(jnp.int32)
    return {
        "x": x,
        "positions": positions,
        "norm_mix": gain(ks[2], (DEPTH, D_MODEL)),
        "w_in": nrm(ks[3], (DEPTH, D_MODEL, D_IN), D_MODEL),
        "lambda_q1": LAMBDA_INIT_STD * jax.random.normal(ks[4], (DEPTH, A_HEAD_DIM), f32),
        "lambda_k1": LAMBDA_INIT_STD * jax.random.normal(ks[5], (DEPTH, A_HEAD_DIM), f32),
        "lambda_q2": LAMBDA_INIT_STD * jax.random.normal(ks[6], (DEPTH, A_HEAD_DIM), f32),
        "lambda_k2": LAMBDA_INIT_STD * jax.random.normal(ks[7], (DEPTH, A_HEAD_DIM), f32),
        "a_subln": gain(ks[8], (DEPTH, A_V_DIM)),
        "w_a_out": nrm(ks[9], (DEPTH, A_V_W, D_MODEL), A_V_W),
        "conv_qkv": nrm(ks[10], (DEPTH, CONV_WIDTH, 3 * B_W), CONV_WIDTH),
        "a_log": jnp.log(jax.random.uniform(ks[11], (DEPTH, B_HEADS), f32, 1.0, 16.0)),
        "dt_bias": jnp.log(jnp.expm1(jax.random.uniform(ks[12], (DEPTH, B_HEADS), f32, 1e-3, 0.1))),
        "b_onorm": gain(ks[13], (DEPTH, B_HEAD_DIM)),
        "w_b_out": nrm(ks[14], (DEPTH, B_W, D_MODEL), B_W),
        "w_o": nrm(ks[15], (DEPTH, D_MODEL, D_MODEL), D_MODEL),
        "norm_ffn": gain(ks[16], (DEPTH, D_MODEL)),
        "w_up": nrm(ks[17], (DEPTH, D_MODEL, 2 * D_FF), D_MODEL),
        "ffn_conv": nrm(ks[18], (DEPTH, FFN_CONV_WIDTH, 2 * D_FF), FFN_CONV_WIDTH),
        "ffn_conv_bias": 0.02 * jax.random.normal(ks[19], (DEPTH, 2 * D_FF), f32),
        "w_down": nrm(ks[20], (DEPTH, D_FF, D_MODEL), D_FF),
        "norm_final": gain(ks[21], (D_MODEL,)),
    }


def reference(x, positions, norm_mix, w_in, lambda_q1, lambda_k1, lambda_q2, lambda_k2, a_subln,
              w_a_out, conv_qkv, a_log, dt_bias, b_onorm, w_b_out, w_o, norm_ffn, w_up, ffn_conv,
              ffn_conv_bias, w_down, norm_final):
    b, s, _ = x.shape
    split_points = [int(v) for v in np.cumsum(IN_SIZES)[:-1]]
    for layer in range(DEPTH):
        h = rms_norm(x, norm_mix[layer])
        proj = jnp.einsum("bsd,de->bse", h, w_in[layer])
        qa, ka, va, qb, kb, vb, zb, beta_in, a_in, gate_a, gate_b = jnp.split(proj, split_points, axis=-1)

        qa = rotary(qa.reshape(b, s, A_HEADS * 2, A_HEAD_DIM), positions).reshape(b, s, A_HEADS, 2, A_HEAD_DIM)
        ka = rotary(ka.reshape(b, s, A_HEADS * 2, A_HEAD_DIM), positions).reshape(b, s, A_HEADS, 2, A_HEAD_DIM)
        va = va.reshape(b, s, A_HEADS, A_V_DIM)
        lambda_init = 0.8 - 0.6 * math.exp(-0.3 * layer)
        lam = (jnp.exp(jnp.sum(lambda_q1[layer].astype(jnp.float32) * lambda_k1[layer].astype(jnp.float32)))
               - jnp.exp(jnp.sum(lambda_q2[layer].astype(jnp.float32) * lambda_k2[layer].astype(jnp.float32)))
               + lambda_init)
        oa = diff_attention(qa, ka, va, lam)
        oa = rms_norm(oa, a_subln[layer]) * (1.0 - lambda_init)
        ya = jnp.einsum("bse,ed->bsd", oa.reshape(b, s, A_V_W).astype(x.dtype), w_a_out[layer])

        qkv_b = jax.nn.silu(causal_dwconv(jnp.concatenate([qb, kb, vb], axis=-1), conv_qkv[layer]))
        qb, kb, vb = jnp.split(qkv_b, [B_W, 2 * B_W], axis=-1)
        heads = lambda t: t.reshape(b, s, B_HEADS, B_HEAD_DIM).transpose(0, 2, 1, 3)
        q_d = l2_normalize(heads(qb)) * (B_HEAD_DIM ** -0.5)
        k_d = l2_normalize(heads(kb))
        v_d = heads(vb).astype(jnp.float32)
        beta = jax.nn.sigmoid(beta_in.astype(jnp.float32)).transpose(0, 2, 1)
        g = (-jnp.exp(a_log[layer].astype(jnp.float32))
             * jax.nn.softplus(a_in.astype(jnp.float32) + dt_bias[layer].astype(jnp.float32))).transpose(0, 2, 1)
        ob = gated_delta_rule(q_d, k_d, v_d, beta, g).transpose(0, 2, 1, 3)
        ob = rms_norm(ob, b_onorm[layer]) * jax.nn.silu(zb.reshape(b, s, B_HEADS, B_HEAD_DIM).astype(jnp.float32))
        yb = jnp.einsum("bse,ed->bsd", ob.reshape(b, s, B_W).astype(x.dtype), w_b_out[layer])

        merged = jax.nn.sigmoid(gate_a) * ya + jax.nn.sigmoid(gate_b) * yb
        x = x + jnp.einsum("bsd,de->bse", merged, w_o[layer])

        h2 = rms_norm(x, norm_ffn[layer])
        u = jnp.einsum("bsd,df->bsf", h2, w_up[layer])
        u = causal_dwconv(u, ffn_conv[layer]) + ffn_conv_bias[layer]
        gate, up = jnp.split(u, 2, axis=-1)
        x = x + jnp.einsum("bsf,fd->bsd", jax.nn.silu(gate) * up, w_down[layer])
    return rms_norm(x, norm_final)
```

```python
import bisect
from contextlib import ExitStack
import numpy as np
import ml_dtypes
import concourse.bass as bass
import concourse.mybir as mybir
from concourse.bass_utils import run_bass_kernel_spmd

F32 = mybir.dt.float32
BF16 = mybir.dt.bfloat16
I32 = mybir.dt.int32
ALU = mybir.AluOpType
AF = mybir.ActivationFunctionType
AX = mybir.AxisListType

ENGS = ("pe", "act", "dve", "pool", "sp")
_KNOB = {}
BLK = {"pe": "tensor", "act": "scalar", "dve": "vector", "pool": "gpsimd", "sp": "sync"}

T = 4096
NB = 8
QW = 258
MYT = NB * QW
EPS = 1e-6
NEG = -30000.0
NSLOT = 24


class Op:
    __slots__ = ("eng", "fn", "stream", "pos", "deps", "flag", "val")

    def __init__(self, eng, fn, stream, pos):
        self.eng, self.fn, self.stream, self.pos = eng, fn, stream, pos
        self.deps = []
        self.flag = False
        self.val = 0


class Sched:
    def __init__(self, nc, sems):
        self.nc = nc
        self.sems = sems
        self.pending = {e: [] for e in ENGS}
        self.streams = {}
        self.cum = {}
        self.emitted = {}
        self.flagpos = {}
        self.lastw = {}
        self.readers = {}
        self.waited = {e: {} for e in ENGS}
        self.nops = 0
        self.dmacount = {}

    def _resolve(self, d):
        if d.pos < self.emitted.get(d.stream, 0) and not d.flag:
            fp = self.flagpos[d.stream]
            i = bisect.bisect_left(fp, d.pos)
            return self.streams[d.stream][fp[i]]
        return d

    def op(self, eng, fn, reads=(), writes=(), dma=False):
        if dma:
            k = self.dmacount.get(eng, 0)
            self.dmacount[eng] = k + 1
            st = "d_%s_%d" % (eng, k % NSLOT)
        else:
            st = "c_" + eng
        lst = self.streams.setdefault(st, [])
        o = Op(eng, fn, st, len(lst))
        deps = []
        if dma:
            o.flag = True
            if lst:
                deps.append(lst[-1])
        lst.append(o)
        self.nops += 1
        for b in reads:
            w = self.lastw.get(b)
            if w is not None:
                deps.append(w)
        for b in writes:
            w = self.lastw.get(b)
            if w is not None:
                deps.append(w)
            deps.extend(self.readers.get(b, ()))
        wt = self.waited[eng]
        need = {}
        for d in deps:
            if d is o:
                continue
            if d.stream == "c_" + eng and not dma and eng == "pe":
                continue
            if wt.get(d.stream, -1) >= d.pos:
                continue
            if d.stream not in need or need[d.stream].pos < d.pos:
                need[d.stream] = d
        for s, d in need.items():
            d = self._resolve(d)
            wt[s] = max(wt.get(s, -1), d.pos)
            d.flag = True
            o.deps.append(d)
        for b in reads:
            self.readers.setdefault(b, []).append(o)
        for b in writes:
            self.lastw[b] = o
            self.readers[b] = []
        self.pending[eng].append(o)
        return o

    def barrier(self):
        lasts = [lst[-1] for lst in self.streams.values() if lst]
        for e in ENGS:
            o = Op(e, None, None, -1)
            wt = self.waited[e]
            for d in lasts:
                if d.stream == "c_" + e and e == "pe":
                    continue
                if wt.get(d.stream, -1) >= d.pos:
                    continue
                d = self._resolve(d)
                wt[d.stream] = max(wt.get(d.stream, -1), d.pos)
                d.flag = True
                o.deps.append(d)
            self.pending[e].append(o)

    def flush(self):
        for st, lst in self.streams.items():
            inc = 16 if st.startswith("d_") else 1
            c = self.cum.get(st, 0)
            e0 = self.emitted.get(st, 0)
            if len(lst) > e0:
                lst[-1].flag = True
            fp = self.flagpos.setdefault(st, [])
            for o in lst[e0:]:
                if o.flag:
                    c += inc
                    o.val = c
                    fp.append(o.pos)
            self.cum[st] = c
        nc, sems = self.nc, self.sems
        with nc.Block() as block:
            for e in ENGS:
                ops = self.pending[e]
                if not ops:
                    continue

                def body(eng, ops=ops):
                    for o in ops:
                        for d in o.deps:
                            assert d.val > 0, (d.stream, d.pos)
                            eng.wait_ge(sems[d.stream], d.val)
                        if o.fn is None:
                            continue
                        ins = o.fn(eng)
                        if o.flag:
                            ins.then_inc(sems[o.stream], 16 if o.stream.startswith("d_") else 1)

                getattr(block, BLK[e])(body)
        for st, lst in self.streams.items():
            self.emitted[st] = len(lst)
        self.pending = {e: [] for e in ENGS}


def build_nc(stop_after=None, dbg=False):
    nc = bass.Bass("TRN2", target_bir_lowering=False)
    din = lambda n, shp, dt=F32: nc.dram_tensor(n, shp, dt, kind="ExternalInput").ap()
    xs = din("xs", [T, 1024])
    pos = din("pos", [1, T], I32)
    w_in = din("w_in", [1024, 9232])
    w_a = din("w_a", [1024, 1024])
    w_b = din("w_b", [1024, 1024])
    w_o = din("w_o", [1024, 1024])
    w_up = din("w_up", [1024, 5632])
    w_dn = din("w_dn", [2816, 1024])
    gmix = din("gmix", [128, 8])
    gffn = din("gffn", [128, 8])
    gfin = din("gfin", [1, 1024])
    lam4 = din("lam4", [1, 256])
    asub = din("asub", [128, 1])
    bon = din("bon", [128, 1])
    cw = din("cw", [128, 24 * 4])
    alog = din("alog", [8, 1])
    dtb = din("dtb", [8, 1])
    fcw = din("fcw", [128, 44 * 3])
    fcb = din("fcb", [128, 44])
    kbias = din("kbias", [128, 32])
    c_identb = din("c_identb", [128, 128], BF16)
    c_identf = din("c_identf", [128, 128])
    c_sel = din("c_sel", [8, 8 * 128])
    c_sel16 = din("c_sel16", [16, 8 * 128], BF16)
    c_rope = din("c_rope", [128, 4])
    c_amm = din("c_amm", [128, 2 * 256], BF16)
    c_amh = din("c_amh", [128, 30 * 16], BF16)
    c_dmU = din("c_dmU", [128, 128])
    c_dmL = din("c_dmL", [128, 128])
    y = nc.dram_tensor("y", [NB * 256, 1024], F32, kind="ExternalOutput").ap()
    dscr = lambda n, shp, dt: (nc.dram_tensor(n, shp, dt, kind="ExternalOutput") if dbg else nc.dram_tensor(n, shp, dt)).ap()
    kT_s = dscr("kT_s", [8, 128, T], BF16)
    v_s = dscr("v_s", [T, 1024], BF16)
    qT_s = dscr("qT_s", [8, 128, MYT], BF16)
    dq_s = dscr("dq_s", [8, 128, T], BF16)
    dk_s = dscr("dk_s", [8, 128, T], BF16)
    dktm_s = dscr("dktm_s", [8, T, 128], BF16)
    dvtm_s = dscr("dvtm_s", [8, T, 128], BF16)
    z_s = dscr("z_s", [8, 128, MYT], BF16)
    gate_s = dscr("gate_s", [16, 128, MYT], BF16)
    bg_s = dscr("bg_s", [2, 8, T], F32)
    bg16_s = dscr("bg16_s", [24, T], BF16)
    xmid_s = dscr("xmid_s", [NB, QW, 1024], F32)
    h2T_s = dscr("h2T_s", [128, 8, MYT], BF16)
    dbg_out = {}

    top = ExitStack()
    with top:
        sems = {}
        for st in ["c_pe", "c_act", "c_dve", "c_pool", "c_sp"] + ["d_%s_%d" % (e, k) for e in ("sp", "pool") for k in range(NSLOT)]:
            sems[st] = top.enter_context(nc.semaphore(st))
        S = Sched(nc, sems)
        O = S.op

        def SB(es, n, shp, dt):
            return es.enter_context(nc.sbuf_tensor(n, shp, dt))

        def PS(es, n, shp, dt=F32):
            return es.enter_context(nc.psum_tensor(n, shp, dt))

        def dma(out, in_, r, w, eng="sp", **kw):
            return O(eng, lambda e: e.dma_start(out=out, in_=in_, **kw), reads=r, writes=w, dma=True)

        identb = SB(top, "identb", [128, 128], BF16)
        identf = SB(top, "identf", [128, 128], F32)
        onesb = SB(top, "onesb", [128, 128], BF16)
        cst = SB(top, "cst", [128, 8], F32)
        dma(identb[:], c_identb, [], ["identb"])
        dma(identf[:], c_identf, [], ["identf"])
        O("pool", lambda e: e.memset(onesb[:], 1.0), writes=["onesb"])
        O("pool", lambda e: e.memset(cst[:, 0:1], EPS), writes=["cst"])
        O("pool", lambda e: e.memset(cst[:, 1:2], 1e-30), writes=["cst"])
        O("pool", lambda e: e.memset(cst[:, 2:3], 0.0), writes=["cst"])
        O("pool", lambda e: e.memset(cst[:, 3:4], 1.0), writes=["cst"])
        eps_ap = cst[:, 0:1]

        def rsqrt_cols(src_ap, dst_ap, scale, r, w, tmp_ap, tmpname):
            if dst_ap.shape[-1] > 8:
                O("act", lambda e: e.activation(out=tmp_ap, in_=src_ap, func=AF.Ln, scale=scale, bias=cst[:tmp_ap.shape[0], 0:1]),
                  reads=r + ["cst"], writes=[tmpname])
                O("act", lambda e: e.activation(out=dst_ap, in_=tmp_ap, func=AF.Exp, scale=-0.5), reads=[tmpname], writes=w)
                return
            O("act", lambda e: e.activation(out=tmp_ap, in_=src_ap, func=AF.Sqrt, scale=scale, bias=cst[:tmp_ap.shape[0], 0:1]),
              reads=r + ["cst"], writes=[tmpname])
            O("dve", lambda e: e.reciprocal(out=dst_ap, in_=tmp_ap), reads=[tmpname], writes=w)

        esAB = ExitStack()
        with esAB:
            hT = SB(esAB, "hT", [128, 8, T], BF16)
            esA = ExitStack()
            with esA:
                gm = SB(esA, "gm", [128, 8], F32)
                dma(gm[:], gmix, [], ["gm"])
                xt = [SB(esA, f"xt{i}", [128, 1024], F32) for i in range(2)]
                xn = [SB(esA, f"xn{i}", [128, 1024], BF16) for i in range(2)]
                junk = SB(esA, "junkA", [128, 1024], BF16)
                stA = SB(esA, "stA", [128, 3 * 32], F32)
                psT = [PS(esA, f"psT{i}", [128, 8, 128], BF16) for i in range(2)]
                O("dve", lambda e: e.memset(stA[:], 0.0), writes=["stA"])
                for tt in range(32):
                    s = tt % 2
                    dma(xt[s][:], xs[tt * 128:(tt + 1) * 128, :], [], [f"xt{s}"])
                    O("act", lambda e, s=s, tt=tt: e.activation(out=junk[:], in_=xt[s][:], func=AF.Square, accum_out=stA[:, tt:tt + 1]),
                      reads=[f"xt{s}", "stA"], writes=["junkA", "stA"])
                    rsqrt_cols(stA[:, tt:tt + 1], stA[:, 64 + tt:65 + tt], 1.0 / 1024, ["stA"], ["stA"], stA[:, 32 + tt:33 + tt], "stA")
                    O("dve", lambda e, s=s, tt=tt: e.tensor_scalar(out=xn[s][:], in0=xt[s][:], scalar1=stA[:, 64 + tt:65 + tt], scalar2=None, op0=ALU.mult),
                      reads=[f"xt{s}", "stA"], writes=[f"xn{s}"])
                    for kc in range(8):
                        O("pe", lambda e, s=s, kc=kc: e.transpose(psT[s][:, kc, :], xn[s][:, kc * 128:(kc + 1) * 128], identb[:]),
                          reads=[f"xn{s}", "identb"], writes=[f"psT{s}"])
                    O("dve", lambda e, s=s, tt=tt: e.tensor_tensor(out=hT[:, :, tt * 128:(tt + 1) * 128], in0=psT[s][:],
                                                                  in1=gm[:].unsqueeze(2).to_broadcast([128, 8, 128]), op=ALU.mult),
                      reads=[f"psT{s}", "gm"], writes=[("hT", tt // 4)])
                S.barrier()
                S.flush()
            esB = ExitStack()
            with esB:
                wc = [SB(esB, f"wc{i}", [128, 8, 128], BF16) for i in range(3)]
                wr = [SB(esB, f"wr{i}", [128, 8, 128], BF16) for i in range(2)]
                pA = [PS(esB, f"pA{i}", [128, 512]) for i in range(3)]
                pB = [PS(esB, f"pB{i}", [128, 512]) for i in range(2)]
                pTm = [PS(esB, f"pTm{i}", [128, 4, 128], BF16) for i in range(2)]
                w_in_v = w_in.rearrange("(kc p) c -> p kc c", p=128)
                cnt = {"wc": 0, "wr": 0, "pA": 0, "pB": 0, "pTm": 0}

                def load_wc(c0, ncols=128):
                    s = cnt["wc"] % 3
                    cnt["wc"] += 1
                    dma(wc[s][:, :, 0:ncols], w_in_v[:, :, c0:c0 + ncols], [], [f"wc{s}"], eng="pool")
                    return s

                def load_wr(c0):
                    s = cnt["wr"] % 2
                    cnt["wr"] += 1
                    src = w_in_v[:, :, c0:c0 + 128].rearrange("p k (m t j) -> p k m t j", m=2, t=2, j=32)
                    dst = wr[s][:].rearrange("p k (m t j) -> p k m t j", m=2, t=2, j=32)
                    for m in range(2):
                        dma(dst[:, :, m, 0, :], src[:, :, m, 1, :], [], [f"wr{s}"], eng="pool")
                        dma(dst[:, :, m, 1, :], src[:, :, m, 0, :], [], [f"wr{s}"], eng="pool")
                    return s

                def proj(wt, wname, t0, n, M=128, wcols=(0, 128)):
                    s = cnt["pA"] % 3
                    cnt["pA"] += 1
                    for kc in range(8):
                        O("pe", lambda e, s=s, kc=kc: e.matmul(pA[s][0:M, 0:n], lhsT=wt[:, kc, wcols[0]:wcols[1]], rhs=hT[:, kc, t0:t0 + n],
                                                             start=(kc == 0), stop=(kc == 7)),
                          reads=[wname, ("hT", t0 // 512)], writes=[f"pA{s}"])
                    return s

                def projB(wt, wname, t0, n):
                    s = cnt["pB"] % 2
                    cnt["pB"] += 1
                    for kc in range(8):
                        O("pe", lambda e, s=s, kc=kc: e.matmul(pB[s][:, 0:n], lhsT=wt[:, kc, :], rhs=hT[:, kc, t0:t0 + n],
                                                             start=(kc == 0), stop=(kc == 7)),
                          reads=[wname, ("hT", t0 // 512)], writes=[f"pB{s}"])
                    return s

                all_tiles = [(tt * 512, 512, tt * 512) for tt in range(8)]
                my_tiles = [(512 * i + 254, QW, i * QW) for i in range(NB)]

                esR = ExitStack()
                with esR:
                    cosT = SB(esR, "cosT", [128, T], F32)
                    sinT = SB(esR, "sinT", [128, T], F32)
                    esRt = ExitStack()
                    with esRt:
                        rc = SB(esRt, "rc", [128, 4], F32)
                        dma(rc[:], c_rope, [], ["rc"])
                        posi = SB(esRt, "posi", [128, T], I32)
                        posf = SB(esRt, "posf", [128, T], F32)
                        t1 = SB(esRt, "t1", [128, T], F32)
                        t2 = SB(esRt, "t2", [128, T], F32)
                        dma(posi[:], pos.partition_broadcast(128), [], ["posi"])
                        O("dve", lambda e: e.tensor_copy(out=posf[:], in_=posi[:]), reads=["posi"], writes=["posf"])
                        O("dve", lambda e: e.tensor_scalar(out=t1[:], in0=posf[:], scalar1=rc[:, 0:1], scalar2=None, op0=ALU.mult),
                          reads=["posf", "rc"], writes=["t1"])
                        for which, dst, dname in ((0, sinT, "sinT"), (1, cosT, "cosT")):
                            if which == 1:
                                O("dve", lambda e: e.tensor_scalar(out=t1[:], in0=t1[:], scalar1=0.25, scalar2=None, op0=ALU.add),
                                  reads=["t1"], writes=["t1"])
                            O("dve", lambda e: e.tensor_copy(out=posi[:], in_=t1[:]), reads=["t1"], writes=["posi"])
                            O("dve", lambda e: e.tensor_copy(out=posf[:], in_=posi[:]), reads=["posi"], writes=["posf"])
                            O("dve", lambda e: e.tensor_tensor(out=t2[:], in0=t1[:], in1=posf[:], op=ALU.subtract), reads=["t1", "posf"], writes=["t2"])
                            O("dve", lambda e: e.tensor_scalar(out=posf[:], in0=t2[:], scalar1=0.5, scalar2=-1.0, op0=ALU.is_gt, op1=ALU.mult),
                              reads=["t2"], writes=["posf"])
                            O("dve", lambda e: e.tensor_tensor(out=t2[:], in0=t2[:], in1=posf[:], op=ALU.add), reads=["t2", "posf"], writes=["t2"])
                            sc_ap = rc[:, 1:2] if which == 0 else rc[:, 2:3]
                            O("act", lambda e, dst=dst, sc_ap=sc_ap: e.activation(out=dst[:], in_=t2[:], func=AF.Sin, scale=sc_ap),
                              reads=["t2", "rc"], writes=[dname])

                        S.barrier()
                    rt1 = [SB(esR, f"rt1{i}", [128, 512], F32) for i in range(2)]
                    rt2 = [SB(esR, f"rt2{i}", [128, 512], F32) for i in range(2)]
                    ko = [SB(esR, f"ko{i}", [128, T], BF16) for i in range(2)]
                    it = 0
                    for kind, cbase, tiles, dst_s in (("k", 1024, all_tiles, kT_s), ("q", 0, my_tiles, qT_s)):
                        for h in range(8):
                            sw = load_wc(cbase + h * 128)
                            sr = load_wr(cbase + h * 128)
                            so = (it // 1) % 2
                            it += 1
                            for (t0, n, o0) in tiles:
                                a = proj(wc[sw], f"wc{sw}", t0, n)
                                b = projB(wr[sr], f"wr{sr}", t0, n)
                                u = cnt["pB"] % 2
                                O("dve", lambda e, a=a, u=u, t0=t0, n=n: e.tensor_tensor(out=rt1[u][:, 0:n], in0=pA[a][:, 0:n], in1=cosT[:, t0:t0 + n], op=ALU.mult),
                                  reads=[f"pA{a}", "cosT"], writes=[f"rt1{u}"])
                                O("dve", lambda e, b=b, u=u, t0=t0, n=n: e.tensor_tensor(out=rt2[u][:, 0:n], in0=pB[b][:, 0:n], in1=sinT[:, t0:t0 + n], op=ALU.mult),
                                  reads=[f"pB{b}", "sinT"], writes=[f"rt2{u}"])
                                O("dve", lambda e, u=u, so=so, n=n, o0=o0: e.tensor_tensor(out=ko[so][:, o0:o0 + n], in0=rt1[u][:, 0:n], in1=rt2[u][:, 0:n], op=ALU.add),
                                  reads=[f"rt1{u}", f"rt2{u}"], writes=[f"ko{so}"])
                            ntot = tiles[-1][2] + tiles[-1][1]
                            dma(dst_s[h, :, :], ko[so][:, 0:ntot], [f"ko{so}"], [(kind + "T_s", h)])
                S.barrier()
                esV = ExitStack()
                with esV:
                    wv = SB(esV, "wv", [128, 8, 1024], BF16)
                    vt = [SB(esV, f"vt{i}", [128, 1024], BF16) for i in range(2)]
                    dma(wv[:], w_in_v[:, :, 2048:3072], [], ["wv"], eng="pool")
                    for tt in range(32):
                        s2 = tt % 2
                        for half in range(2):
                            s = cnt["pA"] % 3
                            cnt["pA"] += 1
                            for kc in range(8):
                                O("pe", lambda e, s=s, kc=kc, tt=tt, half=half: e.matmul(pA[s][:, :], lhsT=hT[:, kc, tt * 128:(tt + 1) * 128],
                                                                                       rhs=wv[:, kc, half * 512:(half + 1) * 512], start=(kc == 0), stop=(kc == 7)),
                                  reads=["wv", ("hT", tt // 4)], writes=[f"pA{s}"])
                            O("act", lambda e, s=s, s2=s2, half=half: e.activation(out=vt[s2][:, half * 512:(half + 1) * 512], in_=pA[s][:, :], func=AF.Copy),
                              reads=[f"pA{s}"], writes=[f"vt{s2}"])
                        dma(v_s[tt * 128:(tt + 1) * 128, :], vt[s2][:], [f"vt{s2}"], ["v_s"])
                S.barrier()
                esZ = ExitStack()
                with esZ:
                    zo = [SB(esZ, f"zo{i}", [128, MYT], BF16) for i in range(2)]
                    for j in range(24):
                        if j < 8:
                            c0, fn, dst = 6144 + j * 128, AF.Silu, z_s[j, :, :]
                        else:
                            c0, fn, dst = 7184 + (j - 8) * 128, AF.Sigmoid, gate_s[j - 8, :, :]
                        sw = load_wc(c0)
                        so = j % 2
                        for (t0, n, o0) in my_tiles:
                            a = proj(wc[sw], f"wc{sw}", t0, n)
                            O("act", lambda e, a=a, so=so, n=n, o0=o0, fn=fn: e.activation(out=zo[so][:, o0:o0 + n], in_=pA[a][:, 0:n], func=fn),
                              reads=[f"pA{a}"], writes=[f"zo{so}"])
                        dma(dst, zo[so][:], [f"zo{so}"], [("zg_s", j)])
                S.barrier()
                esG = ExitStack()
                with esG:
                    bT = SB(esG, "bT", [8, T], F32)
                    gA = SB(esG, "gA", [8, T], F32)
                    gB = SB(esG, "gB", [8, T], F32)
                    sm = SB(esG, "sm", [8, 4], F32)
                    dma(sm[:, 0:1], alog, [], ["sm"])
                    dma(sm[:, 1:2], dtb, [], ["sm"])
                    O("act", lambda e: e.activation(out=sm[:, 2:3], in_=sm[:, 0:1], func=AF.Exp), reads=["sm"], writes=["sm"])
                    O("dve", lambda e: e.tensor_scalar(out=sm[:, 3:4], in0=sm[:, 2:3], scalar1=-1.0, scalar2=None, op0=ALU.mult), reads=["sm"], writes=["sm"])
                    sw = load_wc(7168, 16)
                    for (t0, n, o0) in all_tiles:
                        a = proj(wc[sw], f"wc{sw}", t0, n, M=8, wcols=(0, 8))
                        O("act", lambda e, a=a, t0=t0: e.activation(out=bT[:, t0:t0 + 512], in_=pA[a][0:8, :], func=AF.Sigmoid),
                          reads=[f"pA{a}"], writes=["bT"])
                        a = proj(wc[sw], f"wc{sw}", t0, n, M=8, wcols=(8, 16))
                        O("act", lambda e, a=a, t0=t0: e.activation(out=gA[:, t0:t0 + 512], in_=pA[a][0:8, :], func=AF.Exp, bias=sm[:, 1:2]),
                          reads=[f"pA{a}", "sm"], writes=["gA"])
                    O("act", lambda e: e.activation(out=gB[:], in_=gA[:], func=AF.Ln, bias=cst[0:8, 3:4]), reads=["gA", "cst"], writes=["gB"])
                    O("dve", lambda e: e.tensor_scalar(out=gA[:], in0=gB[:], scalar1=sm[:, 3:4], scalar2=None, op0=ALU.mult), reads=["gB", "sm"], writes=["gA"])
                    cur, oth, cn, on = gA, gB, "gA", "gB"
                    sh = 1
                    while sh < 128:
                        c3 = cur[:].rearrange("p (n j) -> p n j", j=128)
                        o3 = oth[:].rearrange("p (n j) -> p n j", j=128)
                        O("dve", lambda e, c3=c3, o3=o3, sh=sh: e.tensor_tensor(out=o3[:, :, sh:], in0=c3[:, :, sh:], in1=c3[:, :, :128 - sh], op=ALU.add),
                          reads=[cn], writes=[on])
                        O("pool", lambda e, c3=c3, o3=o3, sh=sh: e.tensor_copy(out=o3[:, :, :sh], in_=c3[:, :, :sh]), reads=[cn], writes=[on])
                        cur, oth, cn, on = oth, cur, on, cn
                        sh *= 2
                    dma(bg_s[0, :, :], bT[:], ["bT"], ["bg_s"])
                    dma(bg_s[1, :, :], cur[:], [cn], ["bg_s"])
                    h16 = SB(esG, "h16", [8, T], BF16)
                    l16 = SB(esG, "l16", [8, T], BF16)
                    b16 = SB(esG, "b16", [8, T], BF16)
                    O("dve", lambda e, cur=cur: e.tensor_copy(out=h16[:], in_=cur[:]), reads=[cn], writes=["h16"])
                    O("dve", lambda e, oth=oth: e.tensor_copy(out=oth[:], in_=h16[:]), reads=["h16"], writes=[on])
                    O("dve", lambda e, cur=cur, oth=oth: e.tensor_tensor(out=oth[:], in0=cur[:], in1=oth[:], op=ALU.subtract), reads=[cn, on], writes=[on])
                    O("dve", lambda e, oth=oth: e.tensor_copy(out=l16[:], in_=oth[:]), reads=[on], writes=["l16"])
                    O("dve", lambda e: e.tensor_copy(out=b16[:], in_=bT[:]), reads=["bT"], writes=["b16"])
                    dma(bg16_s[0:8, :], h16[:], ["h16"], ["bg16_s"])
                    dma(bg16_s[8:16, :], l16[:], ["l16"], ["bg16_s"])
                    dma(bg16_s[16:24, :], b16[:], ["b16"], ["bg16_s"])
                S.barrier()
                esD = ExitStack()
                with esD:
                    cwt = SB(esD, "cwt", [128, 96], F32)
                    dma(cwt[:], cw, [], ["cwt"])
                    pc = [SB(esD, f"pc{i}", [128, T + 3], F32) for i in range(2)]
                    cvs = [SB(esD, f"cv{i}", [128, T], F32) for i in range(2)]
                    sqt = [SB(esD, f"sqt{i}", [128, 512], BF16) for i in range(2)]
                    nbs = [SB(esD, f"nbD{i}", [128, T], BF16) for i in range(2)]
                    rs = [SB(esD, f"rsD{i}", [128, 512], F32) for i in range(2)]
                    sd = [SB(esD, f"sdD{i}", [128, 512], F32) for i in range(2)]
                    tm = SB(esD, "tmD", [128, 32, 128], BF16)
                    for i in range(2):
                        O("pool", lambda e, i=i: e.memset(pc[i][:, 0:3], 0.0), writes=[f"pc{i}"])

                    for j in range(24):
                        kind, h = j // 8, j % 8
                        sw = load_wc(3072 + j * 128)
                        sp_ = j % 2
                        cv, cvn = cvs[sp_], f"cv{sp_}"
                        nb, nbn = nbs[sp_], f"nbD{sp_}"
                        scl = (128.0 ** -0.5) if kind == 0 else 1.0
                        for ti, (t0, n, o0) in enumerate(all_tiles):
                            a = proj(wc[sw], f"wc{sw}", t0, n)
                            pcw = (f"pc{sp_}", ti)
                            pcr = [(f"pc{sp_}", ti), (f"pc{sp_}", ti - 1)] if ti > 0 else [(f"pc{sp_}", ti), f"pc{sp_}"]
                            cvt = (cvn, ti)
                            if ti % 2 == 0:
                                O("act", lambda e, a=a, sp_=sp_, t0=t0: e.activation(out=pc[sp_][:, 3 + t0:3 + t0 + 512], in_=pA[a][:, :], func=AF.Copy),
                                  reads=[f"pA{a}"], writes=[pcw])
                            else:
                                O("dve", lambda e, a=a, sp_=sp_, t0=t0: e.tensor_copy(out=pc[sp_][:, 3 + t0:3 + t0 + 512], in_=pA[a][:, :]),
                                  reads=[f"pA{a}"], writes=[pcw])
                            O("act", lambda e, cv=cv, sp_=sp_, j=j, t0=t0: e.activation(out=cv[:, t0:t0 + 512], in_=pc[sp_][:, 3 + t0:3 + t0 + 512], func=AF.Copy, scale=cwt[:, j * 4 + 3:j * 4 + 4]),
                              reads=pcr + ["cwt"], writes=[cvt])
                            for k in (2, 1, 0):
                                O("dve", lambda e, cv=cv, sp_=sp_, j=j, k=k, t0=t0: e.scalar_tensor_tensor(out=cv[:, t0:t0 + 512], in0=pc[sp_][:, k + t0:k + t0 + 512], scalar=cwt[:, j * 4 + k:j * 4 + k + 1],
                                                                                                 in1=cv[:, t0:t0 + 512], op0=ALU.mult, op1=ALU.add),
                                  reads=pcr + ["cwt", cvt], writes=[cvt])
                        for ti, (t0, n, o0) in enumerate(all_tiles):
                            cvt = (cvn, ti)
                            if kind < 2:
                                O("act", lambda e, cv=cv, t0=t0: e.activation(out=cv[:, t0:t0 + 512], in_=cv[:, t0:t0 + 512], func=AF.Silu), reads=[cvt], writes=[cvt])
                            else:
                                O("act", lambda e, cv=cv, nb=nb, t0=t0: e.activation(out=nb[:, t0:t0 + 512], in_=cv[:, t0:t0 + 512], func=AF.Silu), reads=[cvt], writes=[(nbn, ti)])
                        if kind < 2:
                            for ti, (t0, n, o0) in enumerate(all_tiles):
                                cvt = (cvn, ti)
                                sqs = cnt["pB"] % 2
                                O("pool", lambda e, cv=cv, t0=t0, sqs=sqs: e.tensor_tensor(out=sqt[sqs][:], in0=cv[:, t0:t0 + 512], in1=cv[:, t0:t0 + 512], op=ALU.mult), reads=[cvt], writes=[f"sqt{sqs}"])
                                s = cnt["pB"] % 2
                                cnt["pB"] += 1
                                O("pe", lambda e, s=s, sqs=sqs: e.matmul(pB[s][:, :], lhsT=onesb[:], rhs=sqt[sqs][:], start=True, stop=True),
                                  reads=["onesb", f"sqt{sqs}"], writes=[f"pB{s}"])
                                rsqrt_cols(pB[s][:, :], rs[s][:], 1.0, [f"pB{s}"], [f"rsD{s}"], sd[s][:], f"sdD{s}")
                                O("dve", lambda e, cv=cv, nb=nb, s=s, t0=t0, scl=scl: e.scalar_tensor_tensor(out=nb[:, t0:t0 + 512], in0=cv[:, t0:t0 + 512], scalar=scl, in1=rs[s][:],
                                                                                             op0=ALU.mult, op1=ALU.mult),
                                  reads=[cvt, f"rsD{s}"], writes=[(nbn, ti)])
                        nball = [(nbn, ti) for ti in range(8)]
                        if kind < 2:
                            dma((dq_s if kind == 0 else dk_s)[h, :, :], nb[:], nball, [("dqk_s", kind, h)])
                        if kind >= 1:
                            for g in range(8):
                                s = cnt["pTm"] % 2
                                cnt["pTm"] += 1
                                for q in range(4):
                                    tt = g * 4 + q
                                    O("pe", lambda e, cv=cv, nb=nb, s=s, q=q, tt=tt: e.transpose(pTm[s][:, q, :], nb[:, tt * 128:(tt + 1) * 128], identb[:]),
                                      reads=[(nbn, tt // 4), "identb"], writes=[f"pTm{s}"])
                                eng = "act" if g % 2 == 0 else "dve"
                                if eng == "act":
                                    O("act", lambda e, cv=cv, nb=nb, s=s, g=g: e.activation(out=tm[:, g * 4:(g + 1) * 4, :], in_=pTm[s][:], func=AF.Copy), reads=[f"pTm{s}"], writes=["tmD"])
                                else:
                                    O("dve", lambda e, cv=cv, nb=nb, s=s, g=g: e.tensor_copy(out=tm[:, g * 4:(g + 1) * 4, :], in_=pTm[s][:]), reads=[f"pTm{s}"], writes=["tmD"])
                            dst = (dktm_s if kind == 1 else dvtm_s)[h].rearrange("(n p) d -> p n d", p=128)
                            dma(dst, tm[:], ["tmD"], [("dtm_s", kind, h)])
                S.barrier()
                S.flush()
        if stop_after == "B":
            return finish(nc, S, top, y, dbg_out)

        esCE = ExitStack()
        with esCE:
            oaT = SB(esCE, "oaT", [128, 8, MYT], BF16)
            obT = SB(esCE, "obT", [128, 8, MYT], BF16)
            esC = ExitStack()
            with esC:
                kT = [SB(esC, f"kT{i}", [128, T], BF16) for i in range(2)]
                vh = [SB(esC, f"vh{i}", [128, 32, 128], BF16) for i in range(2)]
                qT = [SB(esC, f"qT{i}", [128, MYT], BF16) for i in range(2)]
                kb_t = SB(esC, "kb_t", [128, 32], F32)
                amm = SB(esC, "amm", [128, 2 * 256], BF16)
                amh = SB(esC, "amh", [128, 30 * 16], BF16)
                pP = [SB(esC, f"pP{i}", [128, 512], BF16) for i in range(3)]
                pPh = [SB(esC, f"pPh{i}", [128, 32], BF16) for i in range(2)]
                qh = SB(esC, "qh", [128, 2, 16], BF16)
                O("dve", lambda e: e.memset(qh[:], 0.0), writes=["qh"])
                lamt = SB(esC, "lamt", [128, 256], F32)
                lt = SB(esC, "lt", [128, 128], F32)
                ls = SB(esC, "ls", [128, 8], F32)
                asb = SB(esC, "asb", [128, 2], F32)
                rrt = SB(esC, "rrt", [128, 512], F32)
                ttm = SB(esC, "ttm", [128, 512], F32)
                osb = SB(esC, "osb", [128, 256], F32)
                sqc = SB(esC, "sqc", [128, 256], BF16)
                sdc = SB(esC, "sdc", [128, 256], F32)
                rsc = SB(esC, "rsc", [128, 256], F32)
                rrh = SB(esC, "rrh", [128, 32], F32)
                tth = SB(esC, "tth", [128, 32], F32)
                osh = SB(esC, "osh", [128, 16], F32)
                sqh = SB(esC, "sqh", [128, 16], BF16)
                sdh = SB(esC, "sdh", [128, 16], F32)
                rsh = SB(esC, "rsh", [128, 16], F32)
                pS = [PS(esC, f"pS{i}", [128, 2, 512]) for i in range(2)]
                pO = [PS(esC, f"pO{i}", [128, 512]) for i in range(1)]
                pL = [PS(esC, f"pL{i}", [128, 512]) for i in range(1)]
                pH1 = PS(esC, "pH1", [128, 512])
                pH2 = PS(esC, "pH2", [128, 512])
                dma(kb_t[:], kbias, [], ["kb_t"])
                dma(amm[:], c_amm, [], ["amm"])
                dma(amh[:], c_amh, [], ["amh"])
                dma(lamt[:], lam4.partition_broadcast(128), [], ["lamt"])
                dma(asb[:, 0:1], asub, [], ["asb"])
                O("dve", lambda e: e.tensor_scalar(out=asb[:, 1:2], in0=asb[:, 0:1], scalar1=0.8, scalar2=None, op0=ALU.mult), reads=["asb"], writes=["asb"])
                O("dve", lambda e: e.tensor_tensor(out=lt[:, 0:64], in0=lamt[:, 0:64], in1=lamt[:, 64:128], op=ALU.mult), reads=["lamt"], writes=["lt"])
                O("dve", lambda e: e.tensor_tensor(out=lt[:, 64:128], in0=lamt[:, 128:192], in1=lamt[:, 192:256], op=ALU.mult), reads=["lamt"], writes=["lt"])
                O("dve", lambda e: e.reduce_sum(out=ls[:, 0:2], in_=lt[:].rearrange("p (a b) -> p a b", a=2), axis=AX.X), reads=["lt"], writes=["ls"])
                O("act", lambda e: e.activation(out=ls[:, 2:4], in_=ls[:, 0:2], func=AF.Exp), reads=["ls"], writes=["ls"])
                O("dve", lambda e: e.tensor_tensor(out=ls[:, 4:5], in0=ls[:, 3:4], in1=ls[:, 2:3], op=ALU.subtract), reads=["ls"], writes=["ls"])
                O("dve", lambda e: e.tensor_scalar(out=ls[:, 5:6], in0=ls[:, 4:5], scalar1=-0.2, scalar2=None, op0=ALU.add), reads=["ls"], writes=["ls"])
                nlam = ls[:, 5:6]
                pcount = 0
                scount = 0
                bcount = 0
                for h in range(8):
                    hs = h % 2
                    dma(kT[hs][:], kT_s[h, :, :], [("kT_s", h)], [f"kT{hs}"])
                    dma(vh[hs][:], v_s[:, h * 128:(h + 1) * 128].rearrange("(n p) d -> p n d", p=128), ["v_s"], [f"vh{hs}"])
                    dma(qT[hs][:], qT_s[h, :, :], [("qT_s", h)], [f"qT{hs}"])
                    qT3 = qT[hs][:].rearrange("p (a b) -> p a b", b=QW)
                    for m in range(2):
                        O("dve", lambda e, qT3=qT3, m=m: e.tensor_copy(out=qh[m * 64:(m + 1) * 64, m, :].rearrange("p (a b) -> p a b", b=2), in_=qT3[m * 64:(m + 1) * 64, :, 0:2]),
                          reads=[f"qT{hs}"], writes=["qh"])

                    def halo_S(kb, hs=hs):
                        r = kb % 2
                        for m in range(2):
                            O("pe", lambda e, m=m, r=r, kb=kb, hs=hs: e.matmul(pH1[:, r * 32 + m * 16:r * 32 + (m + 1) * 16], lhsT=kT[hs][:, kb * 128:(kb + 1) * 128],
                                                                              rhs=qh[:, m, :], start=True, stop=True),
                              reads=[f"kT{hs}", "qh"], writes=[f"pHs{r}"])
                        O("act", lambda e, r=r, kb=kb: e.activation(out=pPh[r][:], in_=pH1[:, r * 32:(r + 1) * 32], func=AF.Exp, scale=0.125, bias=kb_t[:, kb:kb + 1]),
                          reads=[f"pHs{r}", "kb_t"], writes=[f"pPh{r}"])
                        O("dve", lambda e, r=r, kb=kb: e.tensor_tensor(out=pPh[r][:].rearrange("p (m c) -> p m c", m=2), in0=pPh[r][:].rearrange("p (m c) -> p m c", m=2),
                                                                       in1=amh[:, kb * 16:(kb + 1) * 16].unsqueeze(1).to_broadcast([128, 2, 16]), op=ALU.mult),
                          reads=[f"pPh{r}", "amh"], writes=[f"pPh{r}"])

                    def halo_PV(kb, hs=hs):
                        r = kb % 2
                        O("pe", lambda e, r=r, kb=kb, hs=hs: e.matmul(pH2[:, 0:32], lhsT=vh[hs][:, kb, :], rhs=pPh[r][:], start=(kb == 0), stop=False),
                          reads=[f"vh{hs}", f"pPh{r}"], writes=["pHo"])
                        O("pe", lambda e, r=r, kb=kb: e.matmul(pH2[:, 64:96], lhsT=onesb[:], rhs=pPh[r][:], start=False, stop=(kb == 29)),
                          reads=["onesb", f"pPh{r}"], writes=["pHl"])

                    def halo_step(j):
                        if _KNOB.get("nohalo"):
                            return
                        if j < 30:
                            halo_S(j)
                        if 1 <= j <= 30:
                            halo_PV(j - 1)

                    hstep = 0
                    mstep = 0
                    for i in range(NB):
                        q0m = i * QW + 2
                        nkb = 4 * i + 4
                        ob = 0
                        bcount += 1

                        def emit_S(kb, hs=hs, q0m=q0m):
                            nonlocal_sc = scount_box[0]
                            ss = nonlocal_sc % 2
                            scount_box[0] += 1
                            for m in range(2):
                                O("pe", lambda e, ss=ss, m=m, hs=hs, kb=kb, q0m=q0m: e.matmul(pS[ss][:, m, 0:256], lhsT=kT[hs][m * 64:(m + 1) * 64, kb * 128:(kb + 1) * 128],
                                                                                             rhs=qT[hs][m * 64:(m + 1) * 64, q0m:q0m + 256], start=True, stop=True),
                                  reads=[f"kT{hs}", f"qT{hs}"], writes=[f"pS{ss}"])
                            return ss
                        if i == 0:
                            scount_box = [scount]
                        ss_next = emit_S(0)
                        for kb in range(nkb):
                            ss = ss_next
                            pp = pcount % 3
                            pcount += 1
                            if kb + 1 < nkb:
                                ss_next = emit_S(kb + 1)
                            O("act", lambda e, ss=ss, pp=pp, kb=kb: e.activation(out=pP[pp][:].rearrange("p (m c) -> p m c", m=2), in_=pS[ss][:, :, 0:256], func=AF.Exp, scale=0.125, bias=kb_t[:, kb:kb + 1]),
                              reads=[f"pS{ss}", "kb_t"], writes=[f"pP{pp}"])
                            r = kb - (4 * i + 2)
                            if r >= 0:
                                O("dve", lambda e, pp=pp, r=r: e.tensor_tensor(out=pP[pp][:].rearrange("p (m c) -> p m c", m=2), in0=pP[pp][:].rearrange("p (m c) -> p m c", m=2),
                                                                               in1=amm[:, r * 256:(r + 1) * 256].unsqueeze(1).to_broadcast([128, 2, 256]), op=ALU.mult),
                                  reads=[f"pP{pp}", "amm"], writes=[f"pP{pp}"])
                            O("pe", lambda e, pp=pp, hs=hs, kb=kb, nkb=nkb, ob=ob: e.matmul(pO[ob][:, :], lhsT=vh[hs][:, kb, :], rhs=pP[pp][:], start=(kb == 0), stop=(kb == nkb - 1)),
                              reads=[f"vh{hs}", f"pP{pp}"], writes=[f"pO{ob}"])
                            O("pe", lambda e, pp=pp, kb=kb, nkb=nkb, ob=ob: e.matmul(pL[ob][:, :], lhsT=onesb[:], rhs=pP[pp][:], start=(kb == 0), stop=(kb == nkb - 1)),
                              reads=["onesb", f"pP{pp}"], writes=[f"pL{ob}"])
                            mstep += 1
                            if mstep % 4 == 0 and hstep <= 30:
                                halo_step(hstep)
                                hstep += 1
                        O("act", lambda e, ob=ob: e.activation(out=rrt[:], in_=pL[ob][:, :], func=AF.Ln, bias=cst[:, 1:2]), reads=[f"pL{ob}", "cst"], writes=["rrt"])
                        O("act", lambda e: e.activation(out=rrt[:], in_=rrt[:], func=AF.Exp, scale=-1.0), reads=["rrt"], writes=["rrt"])
                        O("dve", lambda e, ob=ob: e.tensor_tensor(out=ttm[:], in0=pO[ob][:, :], in1=rrt[:], op=ALU.mult), reads=[f"pO{ob}", "rrt"], writes=["ttm"])
                        O("dve", lambda e: e.scalar_tensor_tensor(out=osb[:], in0=ttm[:, 256:512], scalar=nlam, in1=ttm[:, 0:256], op0=ALU.mult, op1=ALU.add),
                          reads=["ttm", "ls"], writes=["osb"])
                        O("act", lambda e: e.activation(out=sqc[:], in_=osb[:], func=AF.Square), reads=["osb"], writes=["sqc"])
                        O("pe", lambda e: e.matmul(pH1[:, 256:512], lhsT=onesb[:], rhs=sqc[:], start=True, stop=True), reads=["onesb", "sqc"], writes=["pSQ"])
                        rsqrt_cols(pH1[:, 256:512], rsc[:], 1.0 / 128, ["pSQ"], ["rsc"], sdc[:], "sdc")
                        O("dve", lambda e, h=h, q0m=q0m: e.scalar_tensor_tensor(out=oaT[:, h, q0m:q0m + 256], in0=osb[:], scalar=asb[:, 1:2], in1=rsc[:], op0=ALU.mult, op1=ALU.mult),
                          reads=["osb", "asb", "rsc"], writes=[("oaT", i)])
                    scount = scount_box[0]
                    while hstep <= 30:
                        halo_step(hstep)
                        hstep += 1
                    if _KNOB.get("nohalo"):
                        continue
                    O("act", lambda e: e.activation(out=rrh[:], in_=pH2[:, 64:96], func=AF.Ln, bias=cst[:, 1:2]), reads=["pHl", "cst"], writes=["rrh"])
                    O("act", lambda e: e.activation(out=rrh[:], in_=rrh[:], func=AF.Exp, scale=-1.0), reads=["rrh"], writes=["rrh"])
                    O("dve", lambda e: e.tensor_tensor(out=tth[:], in0=pH2[:, 0:32], in1=rrh[:], op=ALU.mult), reads=["pHo", "rrh"], writes=["tth"])
                    O("dve", lambda e: e.scalar_tensor_tensor(out=osh[:], in0=tth[:, 16:32], scalar=nlam, in1=tth[:, 0:16], op0=ALU.mult, op1=ALU.add),
                      reads=["tth", "ls"], writes=["osh"])
                    O("act", lambda e: e.activation(out=sqh[:], in_=osh[:], func=AF.Square), reads=["osh"], writes=["sqh"])
                    O("pe", lambda e: e.matmul(pH1[:, 64:80], lhsT=onesb[:], rhs=sqh[:], start=True, stop=True), reads=["onesb", "sqh"], writes=["pHq"])
                    rsqrt_cols(pH1[:, 64:80], rsh[:], 1.0 / 128, ["pHq"], ["rsh"], sdh[:], "sdh")
                    oa3 = oaT[:, h, :].rearrange("p (a b) -> p a b", b=QW)
                    O("dve", lambda e, oa3=oa3: e.scalar_tensor_tensor(out=oa3[:, :, 0:2], in0=osh[:].rearrange("p (a b) -> p a b", b=2), scalar=asb[:, 1:2],
                                                                      in1=rsh[:].rearrange("p (a b) -> p a b", b=2), op0=ALU.mult, op1=ALU.mult),
                      reads=["osh", "asb", "rsh"], writes=[("oaT", i) for i in range(NB)])
                S.barrier()
                S.flush()
            if stop_after == "C":
                if dbg:
                    dbg_out["oaT"] = nc.dram_tensor("dbg_oaT", [128, 8 * MYT], BF16, kind="ExternalOutput").ap()
                    dma(dbg_out["oaT"], oaT[:].rearrange("p h t -> p (h t)"), [("oaT", i) for i in range(NB)], ["dbg"])
                return finish(nc, S, top, y, dbg_out)
            build_delta(nc, S, O, SB, PS, dma, cst, identb, identf, onesb, rsqrt_cols, obT,
                        dq_s, dk_s, dktm_s, dvtm_s, z_s, bg_s, bon, c_sel, c_dmU, c_dmL, bg16_s, c_sel16)
            if stop_after == "D":
                if dbg:
                    dbg_out["obT"] = nc.dram_tensor("dbg_obT", [128, 8 * MYT], BF16, kind="ExternalOutput").ap()
                    dma(dbg_out["obT"], obT[:].rearrange("p h t -> p (h t)"), [("obT", i) for i in range(NB)], ["dbg"])
                return finish(nc, S, top, y, dbg_out)
            esE = ExitStack()
            with esE:
                h2T = SB(esE, "h2T", [128, 8, MYT], BF16)
                wa = SB(esE, "wa", [128, 8, 1024], BF16)
                wb_ = SB(esE, "wb_", [128, 8, 1024], BF16)
                wo = SB(esE, "wo", [128, 8, 1024], BF16)
                gf = SB(esE, "gf", [128, 8], F32)
                dma(wa[:], w_a.rearrange("(k p) c -> p k c", p=128), [], ["wa"], eng="pool")
                dma(wb_[:], w_b.rearrange("(k p) c -> p k c", p=128), [], ["wb_"], eng="pool")
                dma(wo[:], w_o.rearrange("(k p) c -> p k c", p=128), [], ["wo"], eng="pool")
                dma(gf[:], gffn, [], ["gf"])
                gab = [SB(esE, f"gab{i}", [128, 2, QW], BF16) for i in range(2)]
                me1 = [SB(esE, f"me1{i}", [128, QW], F32) for i in range(2)]
                me2 = [SB(esE, f"me2{i}", [128, QW], F32) for i in range(2)]
                mT = SB(esE, "mT", [128, 8, QW], BF16)
                xin = [SB(esE, f"xin{i}", [128, 1024], F32) for i in range(2)]
                xm = [SB(esE, f"xm{i}", [128, 1024], F32) for i in range(2)]
                xn2 = [SB(esE, f"xn2{i}", [128, 1024], BF16) for i in range(2)]
                junkE = SB(esE, "junkE", [128, 1024], BF16)
                stE = SB(esE, "stE", [128, 3 * 32], F32)
                pY = [[PS(esE, f"pY{i}{m}", [128, 512]) for m in range(2)] for i in range(2)]
                pD = [PS(esE, f"pD{m}", [128, 512]) for m in range(2)]
                pT2 = [PS(esE, f"pT2{i}", [128, 8, 128], BF16) for i in range(2)]
                O("dve", lambda e: e.memset(stE[:], 0.0), writes=["stE"])
                n_sub = 0
                for i in range(NB):
                    q0 = i * QW
                    for c in range(8):
                        sy = c % 2
                        sg = c % 2
                        dma(gab[sg][:, 0, :], gate_s[c, :, q0:q0 + QW], [("zg_s", 8 + c)], [f"gab{sg}"])
                        dma(gab[sg][:, 1, :], gate_s[8 + c, :, q0:q0 + QW], [("zg_s", 16 + c)], [f"gab{sg}"])
                        for (m, wt, wn, src, sn) in ((0, wa, "wa", oaT, "oaT"), (1, wb_, "wb_", obT, "obT")):
                            for hh in range(8):
                                O("pe", lambda e, sy=sy, m=m, wt=wt, src=src, hh=hh, c=c, q0=q0: e.matmul(pY[sy][m][:, 0:QW], lhsT=wt[:, hh, c * 128:(c + 1) * 128],
                                                                                                       rhs=src[:, hh, q0:q0 + QW], start=(hh == 0), stop=(hh == 7)),
                                  reads=[wn, (sn, i)], writes=[f"pY{sy}{m}"])
                        O("dve", lambda e, sy=sy, sg=sg: e.tensor_tensor(out=me1[sy][:], in0=pY[sy][0][:, 0:QW], in1=gab[sg][:, 0, :], op=ALU.mult),
                          reads=[f"pY{sy}0", f"gab{sg}"], writes=[f"me1{sy}"])
                        O("dve", lambda e, sy=sy, sg=sg: e.tensor_tensor(out=me2[sy][:], in0=pY[sy][1][:, 0:QW], in1=gab[sg][:, 1, :], op=ALU.mult),
                          reads=[f"pY{sy}1", f"gab{sg}"], writes=[f"me2{sy}"])
                        O("pool", lambda e, sy=sy, c=c: e.tensor_tensor(out=mT[:, c, :], in0=me1[sy][:], in1=me2[sy][:], op=ALU.add),
                          reads=[f"me1{sy}", f"me2{sy}"], writes=["mT"])
                    for (a0, M) in ((0, 2), (2, 128), (130, 128)):
                        sx = n_sub % 2
                        col = n_sub
                        n_sub += 1
                        tok0 = 512 * i + 254 + a0
                        dma(xin[sx][0:M, :], xs[tok0:tok0 + M, :], [], [f"xin{sx}"])
                        for half in range(2):
                            for c in range(8):
                                O("pe", lambda e, half=half, c=c, a0=a0, M=M: e.matmul(pD[half][0:M, :], lhsT=mT[:, c, a0:a0 + M], rhs=wo[:, c, half * 512:(half + 1) * 512],
                                                                                      start=(c == 0), stop=(c == 7)),
                                  reads=["mT", "wo"], writes=[f"pD{half}"])
                            O("dve", lambda e, half=half, sx=sx, M=M: e.tensor_tensor(out=xm[sx][0:M, half * 512:(half + 1) * 512], in0=pD[half][0:M, :],
                                                                                     in1=xin[sx][0:M, half * 512:(half + 1) * 512], op=ALU.add),
                              reads=[f"pD{half}", f"xin{sx}"], writes=[f"xm{sx}"])
                        dma(xmid_s[i, a0:a0 + M, :], xm[sx][0:M, :], [f"xm{sx}"], [("xmid_s", i)])
                        O("act", lambda e, sx=sx, M=M, col=col: e.activation(out=junkE[0:M, :], in_=xm[sx][0:M, :], func=AF.Square, accum_out=stE[0:M, col:col + 1]),
                          reads=[f"xm{sx}", "stE"], writes=["junkE", "stE"])
                        rsqrt_cols(stE[0:M, col:col + 1], stE[0:M, 64 + col:65 + col], 1.0 / 1024, ["stE"], ["stE"], stE[0:M, 32 + col:33 + col], "stE")
                        O("dve", lambda e, sx=sx, M=M, col=col: e.tensor_scalar(out=xn2[sx][0:M, :], in0=xm[sx][0:M, :], scalar1=stE[0:M, 64 + col:65 + col], scalar2=None, op0=ALU.mult),
                          reads=[f"xm{sx}", "stE"], writes=[f"xn2{sx}"])
                        for kc in range(8):
                            O("pe", lambda e, sx=sx, kc=kc, M=M: e.transpose(pT2[sx][:, kc, 0:M], xn2[sx][0:M, kc * 128:(kc + 1) * 128], identb[0:M, 0:M]),
                              reads=[f"xn2{sx}", "identb"], writes=[f"pT2{sx}"])
                        O("dve", lambda e, sx=sx, M=M, q0=q0, a0=a0: e.tensor_tensor(out=h2T[:, :, q0 + a0:q0 + a0 + M], in0=pT2[sx][:, :, 0:M],
                                                                                    in1=gf[:].unsqueeze(2).to_broadcast([128, 8, M]), op=ALU.mult),
                          reads=[f"pT2{sx}", "gf"], writes=[("h2T", i)])
                    dma(h2T_s[:, :, q0:q0 + QW], h2T[:, :, q0:q0 + QW], [("h2T", i)], [("h2T_s", i)])
                S.barrier()
                S.flush()
        if stop_after == "E":
            return finish(nc, S, top, y, dbg_out)
        build_ffn(nc, S, O, SB, PS, dma, cst, rsqrt_cols, h2T_s, xmid_s, w_up, w_dn, fcw, fcb, gfin, y)
        return finish(nc, S, top, y, dbg_out)


def finish(nc, S, top, y, dbg_out):
    S.barrier()
    S.flush()
    return nc


def build_delta(nc, S, O, SB, PS, dma, cst, identb, identf, onesb, rsqrt_cols, obT,
                dq_s, dk_s, dktm_s, dvtm_s, z_s, bg_s, bon, c_sel, c_dmU, c_dmL, bg16_s, c_sel16):
    es = ExitStack()
    with es:
        sel = SB(es, "sel", [8, 8 * 128], F32)
        dmU = SB(es, "dmU", [128, 128], F32)
        dmL = SB(es, "dmL", [128, 128], F32)
        bont = SB(es, "bont", [128, 1], F32)
        gct = [SB(es, f"gct{i}", [8, 512], F32) for i in range(2)]
        bet = [SB(es, f"bet{i}", [8, 512], F32) for i in range(2)]
        glast = SB(es, "glast", [8, 32], F32)
        gctm = SB(es, "gctm", [128, 32, 8], F32)
        betm = SB(es, "betm", [128, 32, 8], F32)
        gch = SB(es, "gch", [128, 32], F32)
        beh = SB(es, "beh", [128, 32], F32)
        cb1 = SB(es, "cb1", [128, 32], F32)
        cb2 = SB(es, "cb2", [128, 32], F32)
        egl = SB(es, "egl", [128, 32], F32)
        qTd = SB(es, "qTd", [128, T], BF16)
        kTd = SB(es, "kTd", [128, T], BF16)
        ktm = SB(es, "ktm", [128, 32, 128], BF16)
        vtm = SB(es, "vtm", [128, 32, 128], BF16)
        qgT = SB(es, "qgT", [128, T], BF16)
        kbT = SB(es, "kbT", [128, T], BF16)
        vb = SB(es, "vb", [128, 32, 128], BF16)
        kbg = SB(es, "kbg", [128, 32, 128], BF16)
        kg = SB(es, "kg", [128, 32, 128], BF16)
        oTm = SB(es, "oTm", [128, NB, QW], F32)
        eg = SB(es, "eg", [128, 512], F32)
        e1 = SB(es, "e1", [128, 4, 128], F32)
        e2 = SB(es, "e2", [128, 4, 128], F32)
        DmI = SB(es, "DmI", [128, 4, 128], F32)
        DmS = SB(es, "DmS", [128, 4, 128], F32)
        Dp = SB(es, "Dp", [128, 4, 128], F32)
        qkT = [SB(es, f"qkT{i}", [128, 4, 128], BF16) for i in range(2)]
        Bb = [SB(es, f"Bb{i}", [128, 4, 128], BF16) for i in range(2)]
        BTb = [SB(es, f"BTb{i}", [128, 4, 128], BF16) for i in range(2)]
        Xb = [SB(es, f"Xb{i}", [128, 4, 128], BF16) for i in range(2)]
        u_sb = [SB(es, f"u_sb{i}", [128, 4, 128], F32) for i in range(2)]
        wT_sb = [SB(es, f"wT_sb{i}", [128, 4, 128], BF16) for i in range(2)]
        S_f = SB(es, "S_f", [128, 128], F32)
        S_b = SB(es, "S_b", [128, 128], BF16)
        vnew = [SB(es, f"vnew{i}", [128, 128], BF16) for i in range(2)]
        sqd = SB(es, "sqd", [128, QW], BF16)
        sdd = SB(es, "sdd", [128, QW], F32)
        rsd = SB(es, "rsd", [128, QW], F32)
        ztl = [SB(es, f"ztl{i}", [128, QW], BF16) for i in range(2)]
        tno = SB(es, "tno", [128, QW], F32)
        pGR = PS(es, "pGR", [128, 512])
        pBR = PS(es, "pBR", [128, 512])
        pG1 = PS(es, "pG1", [128, 4, 128])
        pG2 = PS(es, "pG2", [128, 4, 128])
        pG3 = PS(es, "pG3", [128, 4, 128])
        pWS = PS(es, "pWS", [128, 512])
        pOT = PS(es, "pOT", [128, 512])
        pSU = PS(es, "pSU", [128, 512])
        dma(sel[:], c_sel, [], ["sel"])
        sel16 = SB(es, "sel16", [16, 8 * 128], BF16)
        dma(sel16[:], c_sel16, [], ["sel16"])
        g16 = [SB(es, f"g16{i}", [16, 512], BF16) for i in range(2)]
        be16 = [SB(es, f"be16{i}", [8, 512], BF16) for i in range(2)]
        dma(dmU[:], c_dmU, [], ["dmU"])
        dma(dmL[:], c_dmL, [], ["dmL"])
        dma(bont[:], bon, [], ["bont"])
        dma(glast[:], bg_s[1].rearrange("h (n j) -> h n j", j=128)[:, :, 127], ["bg_s"], ["glast"], allow_slow_non_contiguous=True)
        for g in range(8):
            s = g % 2
            dma(gct[s][:], bg_s[1, :, g * 512:(g + 1) * 512], ["bg_s"], [f"gct{s}"])
            dma(bet[s][:], bg_s[0, :, g * 512:(g + 1) * 512], ["bg_s"], [f"bet{s}"])
            for q in range(4):
                n = g * 4 + q
                O("pe", lambda e, s=s, q=q, n=n: e.matmul(pGR[:, n * 8:(n + 1) * 8], lhsT=gct[s][:, q * 128:(q + 1) * 128], rhs=identf[0:8, 0:8], start=True, stop=True),
                  reads=[f"gct{s}", "identf"], writes=["pGR"])
                O("pe", lambda e, s=s, q=q, n=n: e.matmul(pBR[:, n * 8:(n + 1) * 8], lhsT=bet[s][:, q * 128:(q + 1) * 128], rhs=identf[0:8, 0:8], start=True, stop=True),
                  reads=[f"bet{s}", "identf"], writes=["pBR"])
        O("dve", lambda e: e.tensor_copy(out=gctm[:].rearrange("p n h -> p (n h)"), in_=pGR[:, 0:256]), reads=["pGR"], writes=["gctm"])
        O("dve", lambda e: e.tensor_copy(out=betm[:].rearrange("p n h -> p (n h)"), in_=pBR[:, 0:256]), reads=["pBR"], writes=["betm"])
        for h in range(8):
            selh = sel[:, h * 128:(h + 1) * 128]
            dma(qTd[:], dq_s[h, :, :], [("dqk_s", 0, h)], ["qTd"])
            dma(kTd[:], dk_s[h, :, :], [("dqk_s", 1, h)], ["kTd"])
            dma(ktm[:], dktm_s[h].rearrange("(n p) d -> p n d", p=128), [("dtm_s", 1, h)], ["ktm"])
            dma(vtm[:], dvtm_s[h].rearrange("(n p) d -> p n d", p=128), [("dtm_s", 2, h)], ["vtm"])
            O("dve", lambda e, h=h: e.tensor_copy(out=gch[:], in_=gctm[:, :, h]), reads=["gctm"], writes=["gch"])
            O("dve", lambda e, h=h: e.tensor_copy(out=beh[:], in_=betm[:, :, h]), reads=["betm"], writes=["beh"])
            O("pe", lambda e, selh=selh: e.matmul(pWS[:, 0:32], lhsT=selh, rhs=glast[:], start=True, stop=True), reads=["sel", "glast"], writes=["pWS"])
            O("act", lambda e: e.activation(out=egl[:], in_=pWS[:, 0:32], func=AF.Exp), reads=["pWS"], writes=["egl"])
            O("dve", lambda e: e.tensor_tensor(out=cb2[:], in0=pWS[:, 0:32], in1=gch[:], op=ALU.subtract), reads=["pWS", "gch"], writes=["cb2"])
            O("act", lambda e: e.activation(out=cb2[:], in_=cb2[:], func=AF.Exp), reads=["cb2"], writes=["cb2"])
            O("act", lambda e: e.activation(out=cb1[:], in_=gch[:], func=AF.Exp), reads=["gch"], writes=["cb1"])
            O("dve", lambda e: e.tensor_tensor(out=cb1[:], in0=cb1[:], in1=beh[:], op=ALU.mult), reads=["cb1", "beh"], writes=["cb1"])
            O("dve", lambda e: e.tensor_tensor(out=vb[:], in0=vtm[:], in1=beh[:].unsqueeze(2).to_broadcast([128, 32, 128]), op=ALU.mult), reads=["vtm", "beh"], writes=["vb"])
            O("dve", lambda e: e.tensor_tensor(out=kbg[:], in0=ktm[:], in1=cb1[:].unsqueeze(2).to_broadcast([128, 32, 128]), op=ALU.mult), reads=["ktm", "cb1"], writes=["kbg"])
            O("dve", lambda e: e.tensor_tensor(out=kg[:], in0=ktm[:], in1=cb2[:].unsqueeze(2).to_broadcast([128, 32, 128]), op=ALU.mult), reads=["ktm", "cb2"], writes=["kg"])
            O("dve", lambda e: e.memset(S_f[:], 0.0), writes=["S_f"])
            O("pool", lambda e: e.memset(S_b[:], 0.0), writes=["S_b"])
            def gen_inv(g, h=h, selh=selh):
                s = g % 2
                gs = g % 2
                t0 = g * 512
                dma(g16[s][:], bg16_s[0:16, t0:t0 + 512], ["bg16_s"], [f"g16{s}"])
                dma(be16[s][:], bg16_s[16:24, t0:t0 + 512], ["bg16_s"], [f"be16{s}"])
                O("pe", lambda e, s=s, h=h: e.matmul(pGR[:, :], lhsT=sel16[:, h * 128:(h + 1) * 128], rhs=g16[s][:], start=True, stop=True), reads=["sel16", f"g16{s}"], writes=["pGR"])
                O("pe", lambda e, s=s, h=h: e.matmul(pBR[:, :], lhsT=sel16[0:8, h * 128:(h + 1) * 128], rhs=be16[s][:], start=True, stop=True), reads=["sel16", f"be16{s}"], writes=["pBR"])
                yield
                O("act", lambda e: e.activation(out=eg[:], in_=pGR[:, :], func=AF.Exp), reads=["pGR"], writes=["eg"])
                O("dve", lambda e, t0=t0: e.tensor_tensor(out=kbT[:, t0:t0 + 512], in0=kTd[:, t0:t0 + 512], in1=pBR[:, :], op=ALU.mult), reads=["kTd", "pBR"], writes=[("kbT", g)])
                for c in range(4):
                    n = g * 4 + c
                    O("dve", lambda e, c=c, n=n: e.scalar_tensor_tensor(out=e1[:, c, :], in0=pGR[:, c * 128:(c + 1) * 128], scalar=gch[:, n:n + 1], in1=dmU[:], op0=ALU.subtract, op1=ALU.add),
                      reads=["pGR", "gch", "dmU"], writes=["e1"])
                    O("dve", lambda e, c=c, n=n: e.scalar_tensor_tensor(out=e2[:, c, :], in0=pGR[:, c * 128:(c + 1) * 128], scalar=gch[:, n:n + 1], in1=dmL[:], op0=ALU.subtract, op1=ALU.add),
                      reads=["pGR", "gch", "dmL"], writes=["e2"])
                yield
                O("act", lambda e: e.activation(out=DmI[:], in_=e1[:], func=AF.Exp), reads=["e1"], writes=["DmI"])
                O("act", lambda e: e.activation(out=Dp[:], in_=e2[:], func=AF.Exp, scale=-1.0), reads=["e2"], writes=["Dp"])
                O("dve", lambda e, t0=t0: e.tensor_tensor(out=qgT[:, t0:t0 + 512], in0=qTd[:, t0:t0 + 512], in1=eg[:], op=ALU.mult), reads=["qTd", "eg"], writes=[("qgT", g)])
                O("pool", lambda e: e.tensor_tensor(out=DmS[:], in0=DmI[:], in1=identf[:].unsqueeze(1).to_broadcast([128, 4, 128]), op=ALU.subtract), reads=["DmI", "identf"], writes=["DmS"])
                for c in range(4):
                    cs = slice(t0 + c * 128, t0 + (c + 1) * 128)
                    O("pe", lambda e, c=c, cs=cs: e.matmul(pG1[:, c, :], lhsT=kTd[:, cs], rhs=kbT[:, cs], start=True, stop=True), reads=["kTd", ("kbT", g)], writes=["pG1"])
                    O("pe", lambda e, c=c, cs=cs: e.matmul(pG2[:, c, :], lhsT=kbT[:, cs], rhs=kTd[:, cs], start=True, stop=True), reads=["kTd", ("kbT", g)], writes=["pG2"])
                    O("pe", lambda e, c=c, cs=cs: e.matmul(pG3[:, c, :], lhsT=kTd[:, cs], rhs=qTd[:, cs], start=True, stop=True), reads=["kTd", "qTd"], writes=["pG3"])
                yield
                O("dve", lambda e: e.scalar_tensor_tensor(out=Bb[0][:], in0=pG1[:], scalar=-1.0, in1=DmS[:], op0=ALU.mult, op1=ALU.mult), reads=["pG1", "DmS"], writes=["Bb0"])
                O("dve", lambda e: e.scalar_tensor_tensor(out=BTb[0][:], in0=pG2[:], scalar=-1.0, in1=Dp[:], op0=ALU.mult, op1=ALU.mult), reads=["pG2", "Dp"], writes=["BTb0"])
                O("dve", lambda e, gs=gs: e.tensor_tensor(out=qkT[gs][:], in0=pG3[:], in1=DmI[:], op=ALU.mult), reads=["pG3", "DmI"], writes=[f"qkT{gs}"])
                O("pool", lambda e: e.tensor_tensor(out=Xb[0][:], in0=Bb[0][:], in1=identf[:].unsqueeze(1).to_broadcast([128, 4, 128]), op=ALU.add), reads=["Bb0", "identf"], writes=["Xb0"])
                yield
                cur = 0
                for k in range(1, 7):
                    nx = 1 - cur
                    for c in range(4):
                        if k < 6:
                            O("pe", lambda e, c=c, cur=cur: e.matmul(pG1[:, c, :], lhsT=BTb[cur][:, c, :], rhs=Bb[cur][:, c, :], start=True, stop=True),
                              reads=[f"BTb{cur}", f"Bb{cur}"], writes=["pG1"])
                        O("pe", lambda e, c=c, cur=cur: e.matmul(pG2[:, c, :], lhsT=Bb[cur][:, c, :], rhs=BTb[cur][:, c, :], start=True, stop=True),
                          reads=[f"BTb{cur}", f"Bb{cur}"], writes=["pG2"])
                    yield
                    if k < 6:
                        O("act", lambda e, nx=nx: e.activation(out=Bb[nx][:], in_=pG1[:], func=AF.Copy), reads=["pG1"], writes=[f"Bb{nx}"])
                    O("act", lambda e, nx=nx: e.activation(out=BTb[nx][:], in_=pG2[:], func=AF.Copy), reads=["pG2"], writes=[f"BTb{nx}"])
                    for c in range(4):
                        O("pe", lambda e, c=c, cur=cur, nx=nx: e.matmul(pG3[:, c, :], lhsT=BTb[nx][:, c, :], rhs=Xb[cur][:, c, :], start=True, stop=True),
                          reads=[f"BTb{nx}", f"Xb{cur}"], writes=["pG3"])
                    yield
                    O("dve", lambda e, nx=nx, cur=cur: e.tensor_tensor(out=Xb[nx][:], in0=pG3[:], in1=Xb[cur][:], op=ALU.add), reads=["pG3", f"Xb{cur}"], writes=[f"Xb{nx}"])
                    cur = nx
                for c in range(4):
                    n = g * 4 + c
                    O("pe", lambda e, c=c, n=n, cur=cur: e.matmul(pG1[:, c, :], lhsT=Xb[cur][:, c, :], rhs=vb[:, n, :], start=True, stop=True), reads=[f"Xb{cur}", "vb"], writes=["pG1"])
                    O("pe", lambda e, c=c, n=n, cur=cur: e.matmul(pG2[:, c, :], lhsT=kbg[:, n, :], rhs=Xb[cur][:, c, :], start=True, stop=True), reads=[f"Xb{cur}", "kbg"], writes=["pG2"])
                yield
                O("act", lambda e, gs=gs: e.activation(out=u_sb[gs][:], in_=pG1[:], func=AF.Copy), reads=["pG1"], writes=[f"u_sb{gs}"])
                O("dve", lambda e, gs=gs: e.tensor_copy(out=wT_sb[gs][:], in_=pG2[:]), reads=["pG2"], writes=[f"wT_sb{gs}"])
                yield

            def gen_scan(g, h=h):
                gs = g % 2
                t0 = g * 512
                for c in range(4):
                    n = g * 4 + c
                    vs = n % 2
                    cs = slice(t0 + c * 128, t0 + (c + 1) * 128)
                    O("pe", lambda e, c=c, gs=gs: e.matmul(pWS[:, 0:128], lhsT=wT_sb[gs][:, c, :], rhs=S_b[:], start=True, stop=True), reads=[f"wT_sb{gs}", "S_b"], writes=["pWS"])
                    yield
                    O("dve", lambda e, c=c, vs=vs, gs=gs: e.tensor_tensor(out=vnew[vs][:], in0=u_sb[gs][:, c, :], in1=pWS[:, 0:128], op=ALU.subtract), reads=[f"u_sb{gs}", "pWS"], writes=[f"vnew{vs}"])
                    if n % 4 != 0:
                        O("pe", lambda e, cs=cs: e.matmul(pOT[:, 0:128], lhsT=S_b[:], rhs=qgT[:, cs], start=True, stop=False), reads=["S_b", ("qgT", g)], writes=["pOT"])
                    yield
                    O("pe", lambda e, n=n, vs=vs: e.matmul(pSU[:, 0:128], lhsT=kg[:, n, :], rhs=vnew[vs][:], start=True, stop=True), reads=["kg", f"vnew{vs}"], writes=["pSU"])
                    if n % 4 != 0:
                        O("pe", lambda e, c=c, vs=vs, gs=gs: e.matmul(pOT[:, 0:128], lhsT=vnew[vs][:], rhs=qkT[gs][:, c, :], start=False, stop=True), reads=[f"vnew{vs}", f"qkT{gs}"], writes=["pOT"])
                        blk = n // 4
                        if n % 4 == 1:
                            dsl, ssl = slice(0, 2), slice(126, 128)
                        elif n % 4 == 2:
                            dsl, ssl = slice(2, 130), slice(0, 128)
                        else:
                            dsl, ssl = slice(130, 258), slice(0, 128)
                    yield
                    O("dve", lambda e, n=n: e.scalar_tensor_tensor(out=S_f[:], in0=S_f[:], scalar=egl[:, n:n + 1], in1=pSU[:, 0:128], op0=ALU.mult, op1=ALU.add),
                      reads=["S_f", "egl", "pSU"], writes=["S_f"])
                    O("act", lambda e: e.activation(out=S_b[:], in_=S_f[:], func=AF.Copy), reads=["S_f"], writes=["S_b"])
                    if n % 4 != 0:
                        O("act", lambda e, blk=blk, dsl=dsl, ssl=ssl: e.activation(out=oTm[:, blk, dsl], in_=pOT[:, ssl], func=AF.Copy), reads=["pOT"], writes=["oTm"])
                    yield

            for _ in gen_inv(0):
                pass
            for g in range(8):
                gi = gen_inv(g + 1) if g + 1 < 8 else iter(())
                gsn = gen_scan(g)
                done_i = done_s = False
                while not (done_i and done_s):
                    if not done_s:
                        try:
                            next(gsn)
                        except StopIteration:
                            done_s = True
                    if not done_i:
                        try:
                            next(gi)
                        except StopIteration:
                            done_i = True
            for blk in range(NB):
                q0 = blk * QW
                zs = blk % 2
                dma(ztl[zs][:], z_s[h, :, q0:q0 + QW], [("zg_s", h)], [f"ztl{zs}"])
                O("act", lambda e, blk=blk: e.activation(out=sqd[:], in_=oTm[:, blk, :], func=AF.Square), reads=["oTm"], writes=["sqd"])
                O("pe", lambda e: e.matmul(pGR[:, 0:QW], lhsT=onesb[:], rhs=sqd[:], start=True, stop=True), reads=["onesb", "sqd"], writes=["pGR"])
                rsqrt_cols(pGR[:, 0:QW], rsd[:], 1.0 / 128, ["pGR"], ["rsd"], sdd[:], "sdd")
                O("dve", lambda e, blk=blk: e.scalar_tensor_tensor(out=tno[:], in0=oTm[:, blk, :], scalar=bont[:, 0:1], in1=rsd[:], op0=ALU.mult, op1=ALU.mult),
                  reads=["oTm", "bont", "rsd"], writes=["tno"])
                O("dve", lambda e, h=h, q0=q0, zs=zs: e.tensor_tensor(out=obT[:, h, q0:q0 + QW], in0=tno[:], in1=ztl[zs][:], op=ALU.mult),
                  reads=["tno", f"ztl{zs}"], writes=[("obT", blk)])
        S.barrier()
        S.flush()


def build_ffn(nc, S, O, SB, PS, dma, cst, rsqrt_cols, h2T_s, xmid_s, w_up, w_dn, fcw, fcb, gfin, y):
    es = ExitStack()
    with es:
        wup = SB(es, "wup", [128, 8, 5632], BF16)
        wdn = SB(es, "wdn", [128, 22, 1024], BF16)
        fcwt = SB(es, "fcwt", [128, 132], F32)
        fcbt = SB(es, "fcbt", [128, 44], F32)
        gfr = SB(es, "gfr", [128, 1024], F32)
        h2 = [SB(es, f"h2b{i}", [128, 8, QW], BF16) for i in range(2)]
        actT = SB(es, "actT", [128, 22, 256], BF16)
        cg = [SB(es, f"cg{i}", [128, 256], F32) for i in range(2)]
        cu = [SB(es, f"cu{i}", [128, 256], F32) for i in range(2)]
        ctmp = [SB(es, f"ctmp{i}", [128, 256], F32) for i in range(2)]
        sg = [SB(es, f"sg{i}", [128, 256], F32) for i in range(2)]
        uu = [SB(es, f"uu{i}", [128, QW], F32) for i in range(2)]
        xmt = [SB(es, f"xmt{i}", [128, 1024], F32) for i in range(2)]
        junkF = SB(es, "junkF", [128, 1024], BF16)
        stF = SB(es, "stF", [128, 3 * 16], F32)
        pUg = [PS(es, f"pUg{i}", [128, 512]) for i in range(2)]
        pUu = [PS(es, f"pUu{i}", [128, 512]) for i in range(2)]
        pDn = [PS(es, f"pDn{i}", [128, 512]) for i in range(2)]
        w_up_v = w_up.rearrange("(k p) c -> p k c", p=128)
        for kc in range(8):
            dma(wup[:, kc, :], w_up_v[:, kc, :], [], ["wup"], eng="pool")
        dma(wdn[:], w_dn.rearrange("(f p) c -> p f c", p=128), [], ["wdn"], eng="pool")
        dma(fcwt[:], fcw, [], ["fcwt"])
        dma(fcbt[:], fcb, [], ["fcbt"])
        dma(gfr[:], gfin.partition_broadcast(128), [], ["gfr"])
        O("dve", lambda e: e.memset(stF[:], 0.0), writes=["stF"])
        col = 0
        for i in range(NB):
            q0 = i * QW
            hs = i % 2
            dma(h2[hs][:], h2T_s[:, :, q0:q0 + QW], [("h2T_s", i)], [f"h2b{hs}"])
            for fc in range(22):
                s = fc % 2
                for (ps_, cb, nm) in ((pUg, fc * 128, "pUg"), (pUu, 2816 + fc * 128, "pUu")):
                    for kc in range(8):
                        O("pe", lambda e, ps_=ps_, s=s, kc=kc, cb=cb, hs=hs: e.matmul(ps_[s][:, 0:QW], lhsT=wup[:, kc, cb:cb + 128], rhs=h2[hs][:, kc, :], start=(kc == 0), stop=(kc == 7)),
                          reads=["wup", f"h2b{hs}"], writes=[f"{nm}{s}"])
                wg = lambda k: fcwt[:, fc * 3 + k:fc * 3 + k + 1]
                wu = lambda k: fcwt[:, (22 + fc) * 3 + k:(22 + fc) * 3 + k + 1]
                O("act", lambda e, s=s, w=wg(0): e.activation(out=cg[s][:], in_=pUg[s][:, 0:256], func=AF.Copy, scale=w), reads=[f"pUg{s}", "fcwt"], writes=[f"cg{s}"])
                for k in (1, 2):
                    O("dve", lambda e, s=s, k=k, w=wg(k): e.scalar_tensor_tensor(out=cg[s][:], in0=pUg[s][:, k:k + 256], scalar=w, in1=cg[s][:], op0=ALU.mult, op1=ALU.add),
                      reads=[f"pUg{s}", "fcwt", f"cg{s}"], writes=[f"cg{s}"])
                O("act", lambda e, s=s, fc=fc: e.activation(out=sg[s][:], in_=cg[s][:], func=AF.Silu, bias=fcbt[:, fc:fc + 1]), reads=[f"cg{s}", "fcbt"], writes=[f"sg{s}"])
                O("act", lambda e, s=s, w=wu(0): e.activation(out=cu[s][:], in_=pUu[s][:, 0:256], func=AF.Copy, scale=w), reads=[f"pUu{s}", "fcwt"], writes=[f"cu{s}"])
                for k in (1, 2):
                    O("dve", lambda e, s=s, k=k, w=wu(k): e.scalar_tensor_tensor(out=cu[s][:], in0=pUu[s][:, k:k + 256], scalar=w, in1=cu[s][:], op0=ALU.mult, op1=ALU.add),
                      reads=[f"pUu{s}", "fcwt", f"cu{s}"], writes=[f"cu{s}"])
                O("dve", lambda e, s=s, fc=fc: e.scalar_tensor_tensor(out=actT[:, fc, :], in0=cu[s][:], scalar=fcbt[:, 22 + fc:23 + fc], in1=sg[s][:], op0=ALU.add, op1=ALU.mult),
                  reads=[f"cu{s}", f"sg{s}", "fcbt"], writes=["actT"])
            for sub in range(2):
                xsl = (i * 2 + sub) % 2
                dma(xmt[xsl][:], xmid_s[i, 2 + sub * 128:2 + (sub + 1) * 128, :], [("xmid_s", i)], [f"xmt{xsl}"])
                for half in range(2):
                    for fc in range(22):
                        O("pe", lambda e, half=half, fc=fc, sub=sub: e.matmul(pDn[half][:, :], lhsT=actT[:, fc, sub * 128:(sub + 1) * 128], rhs=wdn[:, fc, half * 512:(half + 1) * 512],
                                                                            start=(fc == 0), stop=(fc == 21)),
                          reads=["actT", "wdn"], writes=[f"pDn{half}"])
                    O("dve", lambda e, half=half, xsl=xsl: e.tensor_tensor(out=xmt[xsl][:, half * 512:(half + 1) * 512], in0=xmt[xsl][:, half * 512:(half + 1) * 512], in1=pDn[half][:, :], op=ALU.add),
                      reads=[f"xmt{xsl}", f"pDn{half}"], writes=[f"xmt{xsl}"])
                O("act", lambda e, xsl=xsl, col=col: e.activation(out=junkF[:], in_=xmt[xsl][:], func=AF.Square, accum_out=stF[:, col:col + 1]),
                  reads=[f"xmt{xsl}", "stF"], writes=["junkF", "stF"])
                rsqrt_cols(stF[:, col:col + 1], stF[:, 32 + col:33 + col], 1.0 / 1024, ["stF"], ["stF"], stF[:, 16 + col:17 + col], "stF")
                O("dve", lambda e, xsl=xsl, col=col: e.scalar_tensor_tensor(out=xmt[xsl][:], in0=xmt[xsl][:], scalar=stF[:, 32 + col:33 + col], in1=gfr[:], op0=ALU.mult, op1=ALU.mult),
                  reads=[f"xmt{xsl}", "stF", "gfr"], writes=[f"xmt{xsl}"])
                r0 = i * 256 + sub * 128
                dma(y[r0:r0 + 128, :], xmt[xsl][:], [f"xmt{xsl}"], ["y"])
                col += 1
        S.barrier()
        S.flush()


def _host_consts():
    bf = ml_dtypes.bfloat16
    c = {}
    c["c_identb"] = np.eye(128, dtype=np.float32).astype(bf)
    c["c_identf"] = np.eye(128, dtype=np.float32)
    sel = np.zeros((8, 8, 128), np.float32)
    for h in range(8):
        sel[h, h, :] = 1.0
    c["c_sel"] = sel.reshape(8, 1024)
    c["c_sel16"] = np.concatenate([sel, sel], axis=0).reshape(16, 1024).astype(bf)
    p = np.arange(128)
    invf = (10000.0 ** (-(np.arange(0, 64, 2, dtype=np.float32)) / 64.0)).astype(np.float32)
    rope = np.zeros((128, 4), np.float32)
    rope[:, 0] = (invf[p % 32].astype(np.float64) / (2 * np.pi)).astype(np.float32)
    sgn = np.where((p % 64) < 32, -1.0, 1.0)
    rope[:, 1] = (2 * np.pi * sgn).astype(np.float32)
    rope[:, 2] = np.float32(2 * np.pi)
    c["c_rope"] = rope
    col = np.arange(256)
    am = np.zeros((128, 2, 256), np.float32)
    for r in range(2):
        am[:, r, :] = (p[:, None] + 128 * r <= col[None, :]).astype(np.float32)
    c["c_amm"] = am.reshape(128, 512).astype(bf)
    ah = np.zeros((128, 30, 8, 2), np.float32)
    for kb_ in range(30):
        for i_ in range(8):
            for t_ in range(2):
                ah[:, kb_, i_, t_] = (128 * kb_ + p <= 512 * i_ + 254 + t_).astype(np.float32)
    c["c_amh"] = ah.reshape(128, 480).astype(bf)
    i = np.arange(128)
    c["c_dmU"] = np.where(i[None, :] >= p[:, None], 0.0, NEG).astype(np.float32)
    c["c_dmL"] = np.where(i[None, :] < p[:, None], 0.0, -NEG).astype(np.float32)
    return c


_NC_CACHE = {}


def _core_inputs(b, c, x, positions, shared):
    d = dict(shared)
    if c == 0:
        xs = np.concatenate([np.zeros((256, 1024), np.float32), x[b, :3840]], axis=0)
        ps = np.concatenate([np.zeros((256,), np.int32), positions[b, :3840]], axis=0)
        kb = np.zeros((128, 32), np.float32)
        kb[:, 0:2] = NEG
    else:
        xs = x[b]
        ps = positions[b]
        kb = np.zeros((128, 32), np.float32)
    d["xs"] = np.ascontiguousarray(xs)
    d["pos"] = np.ascontiguousarray(ps.reshape(1, T).astype(np.int32))
    d["kbias"] = kb
    return d


def _shared_inputs(norm_mix, w_in, lambda_q1, lambda_k1, lambda_q2, lambda_k2, a_subln, w_a_out, conv_qkv, a_log, dt_bias,
                   b_onorm, w_b_out, w_o, norm_ffn, w_up, ffn_conv, ffn_conv_bias, w_down, norm_final):
    f = lambda a: np.ascontiguousarray(np.asarray(a, dtype=np.float32))
    d = _host_consts()
    d["w_in"] = f(w_in[0]); d["w_a"] = f(w_a_out[0]); d["w_b"] = f(w_b_out[0]); d["w_o"] = f(w_o[0])
    d["w_up"] = f(w_up[0]); d["w_dn"] = f(w_down[0])
    d["gmix"] = f(np.asarray(norm_mix[0]).reshape(8, 128).T)
    d["gffn"] = f(np.asarray(norm_ffn[0]).reshape(8, 128).T)
    d["gfin"] = f(np.asarray(norm_final).reshape(1, 1024))
    d["lam4"] = f(np.concatenate([np.asarray(lambda_q1[0]), np.asarray(lambda_k1[0]), np.asarray(lambda_q2[0]), np.asarray(lambda_k2[0])]).reshape(1, 256))
    d["asub"] = f(np.asarray(a_subln[0]).reshape(128, 1))
    d["bon"] = f(np.asarray(b_onorm[0]).reshape(128, 1))
    d["cw"] = f(np.asarray(conv_qkv[0]).reshape(4, 24, 128).transpose(2, 1, 0).reshape(128, 96))
    d["alog"] = f(np.asarray(a_log[0]).reshape(8, 1))
    d["dtb"] = f(np.asarray(dt_bias[0]).reshape(8, 1))
    d["fcw"] = f(np.asarray(ffn_conv[0]).reshape(3, 44, 128).transpose(2, 1, 0).reshape(128, 132))
    d["fcb"] = f(np.asarray(ffn_conv_bias[0]).reshape(44, 128).T)
    return d


def kernel(x, positions, norm_mix, w_in, lambda_q1, lambda_k1, lambda_q2, lambda_k2, a_subln, w_a_out, conv_qkv, a_log,
           dt_bias, b_onorm, w_b_out, w_o, norm_ffn, w_up, ffn_conv, ffn_conv_bias, w_down, norm_final):
    x = np.asarray(x, dtype=np.float32)
    positions = np.asarray(positions, dtype=np.int32)
    shared = _shared_inputs(norm_mix, w_in, lambda_q1, lambda_k1, lambda_q2, lambda_k2, a_subln, w_a_out, conv_qkv, a_log,
                            dt_bias, b_onorm, w_b_out, w_o, norm_ffn, w_up, ffn_conv, ffn_conv_bias, w_down, norm_final)
    nc = build_nc()
    in_maps = []
    for core in range(8):
        b, c = core // 2, core % 2
        in_maps.append(_core_inputs(b, c, x, positions, shared))
    res = run_bass_kernel_spmd(nc, in_maps, core_ids=list(range(8)))
    out = np.zeros((4, 4096, 1024), np.float32)
    for core in range(8):
        b, c = core // 2, core % 2
        yc = np.asarray(res.results[core]["y"]).reshape(NB, 256, 1024)
        for i in range(NB):
            g = 2 * i + c
            out[b, g * 256:(g + 1) * 256] = yc[i]
    return out
```

```python
import bisect
from contextlib import ExitStack
import numpy as np
import ml_dtypes
import concourse.bass as bass
import concourse.mybir as mybir
from concourse.bass_utils import run_bass_kernel_spmd

F32 = mybir.dt.float32
BF16 = mybir.dt.bfloat16
I32 = mybir.dt.int32
ALU = mybir.AluOpType
AF = mybir.ActivationFunctionType
AX = mybir.AxisListType

ENGS = ("pe", "act", "dve", "pool", "sp")
BLK = {"pe": "tensor", "act": "scalar", "dve": "vector", "pool": "gpsimd", "sp": "sync"}

T = 4096
NB = 8
QW = 258
MYT = NB * QW
EPS = 1e-6
NEG = -30000.0
NSLOT = 24


class Op:
    __slots__ = ("eng", "fn", "stream", "pos", "deps", "flag", "val")

    def __init__(self, eng, fn, stream, pos):
        self.eng, self.fn, self.stream, self.pos = eng, fn, stream, pos
        self.deps = []
        self.flag = False
        self.val = 0


class Sched:
    def __init__(self, nc, sems):
        self.nc = nc
        self.sems = sems
        self.pending = {e: [] for e in ENGS}
        self.streams = {}
        self.cum = {}
        self.emitted = {}
        self.flagpos = {}
        self.lastw = {}
        self.readers = {}
        self.waited = {e: {} for e in ENGS}
        self.nops = 0
        self.dmacount = {}

    def _resolve(self, d):
        if d.pos < self.emitted.get(d.stream, 0) and not d.flag:
            fp = self.flagpos[d.stream]
            i = bisect.bisect_left(fp, d.pos)
            return self.streams[d.stream][fp[i]]
        return d

    def op(self, eng, fn, reads=(), writes=(), dma=False):
        if dma:
            k = self.dmacount.get(eng, 0)
            self.dmacount[eng] = k + 1
            st = "d_%s_%d" % (eng, k % NSLOT)
        else:
            st = "c_" + eng
        lst = self.streams.setdefault(st, [])
        o = Op(eng, fn, st, len(lst))
        deps = []
        if dma:
            o.flag = True
            if lst:
                deps.append(lst[-1])
        lst.append(o)
        self.nops += 1
        for b in reads:
            w = self.lastw.get(b)
            if w is not None:
                deps.append(w)
        for b in writes:
            w = self.lastw.get(b)
            if w is not None:
                deps.append(w)
            deps.extend(self.readers.get(b, ()))
        wt = self.waited[eng]
        need = {}
        for d in deps:
            if d is o:
                continue
            if d.stream == "c_" + eng and not dma and eng == "pe":
                continue
            if wt.get(d.stream, -1) >= d.pos:
                continue
            if d.stream not in need or need[d.stream].pos < d.pos:
                need[d.stream] = d
        for s, d in need.items():
            d = self._resolve(d)
            wt[s] = max(wt.get(s, -1), d.pos)
            d.flag = True
            o.deps.append(d)
        for b in reads:
            self.readers.setdefault(b, []).append(o)
        for b in writes:
            self.lastw[b] = o
            self.readers[b] = []
        self.pending[eng].append(o)
        return o

    def barrier(self):
        lasts = [lst[-1] for lst in self.streams.values() if lst]
        for e in ENGS:
            o = Op(e, None, None, -1)
            wt = self.waited[e]
            for d in lasts:
                if d.stream == "c_" + e and e == "pe":
                    continue
                if wt.get(d.stream, -1) >= d.pos:
                    continue
                d = self._resolve(d)
                wt[d.stream] = max(wt.get(d.stream, -1), d.pos)
                d.flag = True
                o.deps.append(d)
            self.pending[e].append(o)

    def flush(self):
        for st, lst in self.streams.items():
            inc = 16 if st.startswith("d_") else 1
            c = self.cum.get(st, 0)
            e0 = self.emitted.get(st, 0)
            if len(lst) > e0:
                lst[-1].flag = True
            fp = self.flagpos.setdefault(st, [])
            for o in lst[e0:]:
                if o.flag:
                    c += inc
                    o.val = c
                    fp.append(o.pos)
            self.cum[st] = c
        nc, sems = self.nc, self.sems
        with nc.Block() as block:
            for e in ENGS:
                ops = self.pending[e]
                if not ops:
                    continue

                def body(eng, ops=ops):
                    for o in ops:
                        for d in o.deps:
                            assert d.val > 0, (d.stream, d.pos)
                            eng.wait_ge(sems[d.stream], d.val)
                        if o.fn is None:
                            continue
                        ins = o.fn(eng)
                        if o.flag:
                            ins.then_inc(sems[o.stream], 16 if o.stream.startswith("d_") else 1)

                getattr(block, BLK[e])(body)
        for st, lst in self.streams.items():
            self.emitted[st] = len(lst)
        self.pending = {e: [] for e in ENGS}


def build_nc(stop_after=None, dbg=False):
    nc = bass.Bass("TRN2", target_bir_lowering=False)
    din = lambda n, shp, dt=F32: nc.dram_tensor(n, shp, dt, kind="ExternalInput").ap()
    xs = din("xs", [T, 1024])
    pos = din("pos", [1, T], I32)
    w_in = din("w_in", [1024, 9232])
    w_a = din("w_a", [1024, 1024])
    w_b = din("w_b", [1024, 1024])
    w_o = din("w_o", [1024, 1024])
    w_up = din("w_up", [1024, 5632])
    w_dn = din("w_dn", [2816, 1024])
    gmix = din("gmix", [128, 8])
    gffn = din("gffn", [128, 8])
    gfin = din("gfin", [1, 1024])
    lam4 = din("lam4", [1, 256])
    asub = din("asub", [128, 1])
    bon = din("bon", [128, 1])
    cw = din("cw", [128, 24 * 4])
    alog = din("alog", [8, 1])
    dtb = din("dtb", [8, 1])
    fcw = din("fcw", [128, 44 * 3])
    fcb = din("fcb", [128, 44])
    kbias = din("kbias", [128, 32])
    c_identb = din("c_identb", [128, 128], BF16)
    c_identf = din("c_identf", [128, 128])
    c_sel = din("c_sel", [8, 8 * 128])
    c_sel16 = din("c_sel16", [16, 8 * 128], BF16)
    c_rope = din("c_rope", [128, 4])
    c_amask = din("c_amask", [128, 3 * QW], BF16)
    c_dmU = din("c_dmU", [128, 128])
    c_dmL = din("c_dmL", [128, 128])
    y = nc.dram_tensor("y", [NB * 256, 1024], F32, kind="ExternalOutput").ap()
    dscr = lambda n, shp, dt: (nc.dram_tensor(n, shp, dt, kind="ExternalOutput") if dbg else nc.dram_tensor(n, shp, dt)).ap()
    kT_s = dscr("kT_s", [8, 128, T], BF16)
    v_s = dscr("v_s", [T, 1024], BF16)
    qT_s = dscr("qT_s", [8, 128, MYT], BF16)
    dq_s = dscr("dq_s", [8, 128, T], BF16)
    dk_s = dscr("dk_s", [8, 128, T], BF16)
    dktm_s = dscr("dktm_s", [8, T, 128], BF16)
    dvtm_s = dscr("dvtm_s", [8, T, 128], BF16)
    z_s = dscr("z_s", [8, 128, MYT], BF16)
    gate_s = dscr("gate_s", [16, 128, MYT], BF16)
    bg_s = dscr("bg_s", [2, 8, T], F32)
    bg16_s = dscr("bg16_s", [24, T], BF16)
    xmid_s = dscr("xmid_s", [NB, QW, 1024], F32)
    h2T_s = dscr("h2T_s", [128, 8, MYT], BF16)
    dbg_out = {}

    top = ExitStack()
    with top:
        sems = {}
        for st in ["c_pe", "c_act", "c_dve", "c_pool", "c_sp"] + ["d_%s_%d" % (e, k) for e in ("sp", "pool") for k in range(NSLOT)]:
            sems[st] = top.enter_context(nc.semaphore(st))
        S = Sched(nc, sems)
        O = S.op

        def SB(es, n, shp, dt):
            return es.enter_context(nc.sbuf_tensor(n, shp, dt))

        def PS(es, n, shp, dt=F32):
            return es.enter_context(nc.psum_tensor(n, shp, dt))

        def dma(out, in_, r, w, eng="sp", **kw):
            return O(eng, lambda e: e.dma_start(out=out, in_=in_, **kw), reads=r, writes=w, dma=True)

        identb = SB(top, "identb", [128, 128], BF16)
        identf = SB(top, "identf", [128, 128], F32)
        onesb = SB(top, "onesb", [128, 128], BF16)
        cst = SB(top, "cst", [128, 8], F32)
        dma(identb[:], c_identb, [], ["identb"])
        dma(identf[:], c_identf, [], ["identf"])
        O("pool", lambda e: e.memset(onesb[:], 1.0), writes=["onesb"])
        O("pool", lambda e: e.memset(cst[:, 0:1], EPS), writes=["cst"])
        O("pool", lambda e: e.memset(cst[:, 1:2], 1e-30), writes=["cst"])
        O("pool", lambda e: e.memset(cst[:, 2:3], 0.0), writes=["cst"])
        O("pool", lambda e: e.memset(cst[:, 3:4], 1.0), writes=["cst"])
        eps_ap = cst[:, 0:1]

        def rsqrt_cols(src_ap, dst_ap, scale, r, w, tmp_ap, tmpname):
            if dst_ap.shape[-1] > 8:
                O("act", lambda e: e.activation(out=tmp_ap, in_=src_ap, func=AF.Ln, scale=scale, bias=cst[:tmp_ap.shape[0], 0:1]),
                  reads=r + ["cst"], writes=[tmpname])
                O("act", lambda e: e.activation(out=dst_ap, in_=tmp_ap, func=AF.Exp, scale=-0.5), reads=[tmpname], writes=w)
                return
            O("act", lambda e: e.activation(out=tmp_ap, in_=src_ap, func=AF.Sqrt, scale=scale, bias=cst[:tmp_ap.shape[0], 0:1]),
              reads=r + ["cst"], writes=[tmpname])
            O("dve", lambda e: e.reciprocal(out=dst_ap, in_=tmp_ap), reads=[tmpname], writes=w)

        esAB = ExitStack()
        with esAB:
            hT = SB(esAB, "hT", [128, 8, T], BF16)
            esA = ExitStack()
            with esA:
                gm = SB(esA, "gm", [128, 8], F32)
                dma(gm[:], gmix, [], ["gm"])
                xt = [SB(esA, f"xt{i}", [128, 1024], F32) for i in range(2)]
                xn = [SB(esA, f"xn{i}", [128, 1024], BF16) for i in range(2)]
                junk = SB(esA, "junkA", [128, 1024], BF16)
                stA = SB(esA, "stA", [128, 3 * 32], F32)
                psT = [PS(esA, f"psT{i}", [128, 8, 128], BF16) for i in range(2)]
                O("dve", lambda e: e.memset(stA[:], 0.0), writes=["stA"])
                for tt in range(32):
                    s = tt % 2
                    dma(xt[s][:], xs[tt * 128:(tt + 1) * 128, :], [], [f"xt{s}"])
                    O("act", lambda e, s=s, tt=tt: e.activation(out=junk[:], in_=xt[s][:], func=AF.Square, accum_out=stA[:, tt:tt + 1]),
                      reads=[f"xt{s}", "stA"], writes=["junkA", "stA"])
                    rsqrt_cols(stA[:, tt:tt + 1], stA[:, 64 + tt:65 + tt], 1.0 / 1024, ["stA"], ["stA"], stA[:, 32 + tt:33 + tt], "stA")
                    O("dve", lambda e, s=s, tt=tt: e.tensor_scalar(out=xn[s][:], in0=xt[s][:], scalar1=stA[:, 64 + tt:65 + tt], scalar2=None, op0=ALU.mult),
                      reads=[f"xt{s}", "stA"], writes=[f"xn{s}"])
                    for kc in range(8):
                        O("pe", lambda e, s=s, kc=kc: e.transpose(psT[s][:, kc, :], xn[s][:, kc * 128:(kc + 1) * 128], identb[:]),
                          reads=[f"xn{s}", "identb"], writes=[f"psT{s}"])
                    O("dve", lambda e, s=s, tt=tt: e.tensor_tensor(out=hT[:, :, tt * 128:(tt + 1) * 128], in0=psT[s][:],
                                                                  in1=gm[:].unsqueeze(2).to_broadcast([128, 8, 128]), op=ALU.mult),
                      reads=[f"psT{s}", "gm"], writes=[("hT", tt // 4)])
                S.barrier()
                S.flush()
            esB = ExitStack()
            with esB:
                wc = [SB(esB, f"wc{i}", [128, 8, 128], BF16) for i in range(3)]
                wr = [SB(esB, f"wr{i}", [128, 8, 128], BF16) for i in range(2)]
                pA = [PS(esB, f"pA{i}", [128, 512]) for i in range(3)]
                pB = [PS(esB, f"pB{i}", [128, 512]) for i in range(2)]
                pTm = [PS(esB, f"pTm{i}", [128, 4, 128], BF16) for i in range(2)]
                w_in_v = w_in.rearrange("(kc p) c -> p kc c", p=128)
                cnt = {"wc": 0, "wr": 0, "pA": 0, "pB": 0, "pTm": 0}

                def load_wc(c0, ncols=128):
                    s = cnt["wc"] % 3
                    cnt["wc"] += 1
                    dma(wc[s][:, :, 0:ncols], w_in_v[:, :, c0:c0 + ncols], [], [f"wc{s}"], eng="pool")
                    return s

                def load_wr(c0):
                    s = cnt["wr"] % 2
                    cnt["wr"] += 1
                    src = w_in_v[:, :, c0:c0 + 128].rearrange("p k (m t j) -> p k m t j", m=2, t=2, j=32)
                    dst = wr[s][:].rearrange("p k (m t j) -> p k m t j", m=2, t=2, j=32)
                    for m in range(2):
                        dma(dst[:, :, m, 0, :], src[:, :, m, 1, :], [], [f"wr{s}"], eng="pool")
                        dma(dst[:, :, m, 1, :], src[:, :, m, 0, :], [], [f"wr{s}"], eng="pool")
                    return s

                def proj(wt, wname, t0, n, M=128, wcols=(0, 128)):
                    s = cnt["pA"] % 3
                    cnt["pA"] += 1
                    for kc in range(8):
                        O("pe", lambda e, s=s, kc=kc: e.matmul(pA[s][0:M, 0:n], lhsT=wt[:, kc, wcols[0]:wcols[1]], rhs=hT[:, kc, t0:t0 + n],
                                                             start=(kc == 0), stop=(kc == 7)),
                          reads=[wname, ("hT", t0 // 512)], writes=[f"pA{s}"])
                    return s

                def projB(wt, wname, t0, n):
                    s = cnt["pB"] % 2
                    cnt["pB"] += 1
                    for kc in range(8):
                        O("pe", lambda e, s=s, kc=kc: e.matmul(pB[s][:, 0:n], lhsT=wt[:, kc, :], rhs=hT[:, kc, t0:t0 + n],
                                                             start=(kc == 0), stop=(kc == 7)),
                          reads=[wname, ("hT", t0 // 512)], writes=[f"pB{s}"])
                    return s

                all_tiles = [(tt * 512, 512, tt * 512) for tt in range(8)]
                my_tiles = [(512 * i + 254, QW, i * QW) for i in range(NB)]

                esR = ExitStack()
                with esR:
                    cosT = SB(esR, "cosT", [128, T], F32)
                    sinT = SB(esR, "sinT", [128, T], F32)
                    esRt = ExitStack()
                    with esRt:
                        rc = SB(esRt, "rc", [128, 4], F32)
                        dma(rc[:], c_rope, [], ["rc"])
                        posi = SB(esRt, "posi", [128, T], I32)
                        posf = SB(esRt, "posf", [128, T], F32)
                        t1 = SB(esRt, "t1", [128, T], F32)
                        t2 = SB(esRt, "t2", [128, T], F32)
                        dma(posi[:], pos.partition_broadcast(128), [], ["posi"])
                        O("dve", lambda e: e.tensor_copy(out=posf[:], in_=posi[:]), reads=["posi"], writes=["posf"])
                        O("dve", lambda e: e.tensor_scalar(out=t1[:], in0=posf[:], scalar1=rc[:, 0:1], scalar2=None, op0=ALU.mult),
                          reads=["posf", "rc"], writes=["t1"])
                        for which, dst, dname in ((0, sinT, "sinT"), (1, cosT, "cosT")):
                            if which == 1:
                                O("dve", lambda e: e.tensor_scalar(out=t1[:], in0=t1[:], scalar1=0.25, scalar2=None, op0=ALU.add),
                                  reads=["t1"], writes=["t1"])
                            O("dve", lambda e: e.tensor_copy(out=posi[:], in_=t1[:]), reads=["t1"], writes=["posi"])
                            O("dve", lambda e: e.tensor_copy(out=posf[:], in_=posi[:]), reads=["posi"], writes=["posf"])
                            O("dve", lambda e: e.tensor_tensor(out=t2[:], in0=t1[:], in1=posf[:], op=ALU.subtract), reads=["t1", "posf"], writes=["t2"])
                            O("dve", lambda e: e.tensor_scalar(out=posf[:], in0=t2[:], scalar1=0.5, scalar2=-1.0, op0=ALU.is_gt, op1=ALU.mult),
                              reads=["t2"], writes=["posf"])
                            O("dve", lambda e: e.tensor_tensor(out=t2[:], in0=t2[:], in1=posf[:], op=ALU.add), reads=["t2", "posf"], writes=["t2"])
                            sc_ap = rc[:, 1:2] if which == 0 else rc[:, 2:3]
                            O("act", lambda e, dst=dst, sc_ap=sc_ap: e.activation(out=dst[:], in_=t2[:], func=AF.Sin, scale=sc_ap),
                              reads=["t2", "rc"], writes=[dname])

                        S.barrier()
                    rt1 = [SB(esR, f"rt1{i}", [128, 512], F32) for i in range(2)]
                    rt2 = [SB(esR, f"rt2{i}", [128, 512], F32) for i in range(2)]
                    ko = [SB(esR, f"ko{i}", [128, T], BF16) for i in range(2)]
                    it = 0
                    for kind, cbase, tiles, dst_s in (("k", 1024, all_tiles, kT_s), ("q", 0, my_tiles, qT_s)):
                        for h in range(8):
                            sw = load_wc(cbase + h * 128)
                            sr = load_wr(cbase + h * 128)
                            so = (it // 1) % 2
                            it += 1
                            for (t0, n, o0) in tiles:
                                a = proj(wc[sw], f"wc{sw}", t0, n)
                                b = projB(wr[sr], f"wr{sr}", t0, n)
                                u = cnt["pB"] % 2
                                O("dve", lambda e, a=a, u=u, t0=t0, n=n: e.tensor_tensor(out=rt1[u][:, 0:n], in0=pA[a][:, 0:n], in1=cosT[:, t0:t0 + n], op=ALU.mult),
                                  reads=[f"pA{a}", "cosT"], writes=[f"rt1{u}"])
                                O("dve", lambda e, b=b, u=u, t0=t0, n=n: e.tensor_tensor(out=rt2[u][:, 0:n], in0=pB[b][:, 0:n], in1=sinT[:, t0:t0 + n], op=ALU.mult),
                                  reads=[f"pB{b}", "sinT"], writes=[f"rt2{u}"])
                                O("dve", lambda e, u=u, so=so, n=n, o0=o0: e.tensor_tensor(out=ko[so][:, o0:o0 + n], in0=rt1[u][:, 0:n], in1=rt2[u][:, 0:n], op=ALU.add),
                                  reads=[f"rt1{u}", f"rt2{u}"], writes=[f"ko{so}"])
                            ntot = tiles[-1][2] + tiles[-1][1]
                            dma(dst_s[h, :, :], ko[so][:, 0:ntot], [f"ko{so}"], [(kind + "T_s", h)])
                S.barrier()
                esV = ExitStack()
                with esV:
                    wv = SB(esV, "wv", [128, 8, 1024], BF16)
                    vt = [SB(esV, f"vt{i}", [128, 1024], BF16) for i in range(2)]
                    dma(wv[:], w_in_v[:, :, 2048:3072], [], ["wv"], eng="pool")
                    for tt in range(32):
                        s2 = tt % 2
                        for half in range(2):
                            s = cnt["pA"] % 3
                            cnt["pA"] += 1
                            for kc in range(8):
                                O("pe", lambda e, s=s, kc=kc, tt=tt, half=half: e.matmul(pA[s][:, :], lhsT=hT[:, kc, tt * 128:(tt + 1) * 128],
                                                                                       rhs=wv[:, kc, half * 512:(half + 1) * 512], start=(kc == 0), stop=(kc == 7)),
                                  reads=["wv", ("hT", tt // 4)], writes=[f"pA{s}"])
                            O("act", lambda e, s=s, s2=s2, half=half: e.activation(out=vt[s2][:, half * 512:(half + 1) * 512], in_=pA[s][:, :], func=AF.Copy),
                              reads=[f"pA{s}"], writes=[f"vt{s2}"])
                        dma(v_s[tt * 128:(tt + 1) * 128, :], vt[s2][:], [f"vt{s2}"], ["v_s"])
                S.barrier()
                esZ = ExitStack()
                with esZ:
                    zo = [SB(esZ, f"zo{i}", [128, MYT], BF16) for i in range(2)]
                    for j in range(24):
                        if j < 8:
                            c0, fn, dst = 6144 + j * 128, AF.Silu, z_s[j, :, :]
                        else:
                            c0, fn, dst = 7184 + (j - 8) * 128, AF.Sigmoid, gate_s[j - 8, :, :]
                        sw = load_wc(c0)
                        so = j % 2
                        for (t0, n, o0) in my_tiles:
                            a = proj(wc[sw], f"wc{sw}", t0, n)
                            O("act", lambda e, a=a, so=so, n=n, o0=o0, fn=fn: e.activation(out=zo[so][:, o0:o0 + n], in_=pA[a][:, 0:n], func=fn),
                              reads=[f"pA{a}"], writes=[f"zo{so}"])
                        dma(dst, zo[so][:], [f"zo{so}"], [("zg_s", j)])
                S.barrier()
                esG = ExitStack()
                with esG:
                    bT = SB(esG, "bT", [8, T], F32)
                    gA = SB(esG, "gA", [8, T], F32)
                    gB = SB(esG, "gB", [8, T], F32)
                    sm = SB(esG, "sm", [8, 4], F32)
                    dma(sm[:, 0:1], alog, [], ["sm"])
                    dma(sm[:, 1:2], dtb, [], ["sm"])
                    O("act", lambda e: e.activation(out=sm[:, 2:3], in_=sm[:, 0:1], func=AF.Exp), reads=["sm"], writes=["sm"])
                    O("dve", lambda e: e.tensor_scalar(out=sm[:, 3:4], in0=sm[:, 2:3], scalar1=-1.0, scalar2=None, op0=ALU.mult), reads=["sm"], writes=["sm"])
                    sw = load_wc(7168, 16)
                    for (t0, n, o0) in all_tiles:
                        a = proj(wc[sw], f"wc{sw}", t0, n, M=8, wcols=(0, 8))
                        O("act", lambda e, a=a, t0=t0: e.activation(out=bT[:, t0:t0 + 512], in_=pA[a][0:8, :], func=AF.Sigmoid),
                          reads=[f"pA{a}"], writes=["bT"])
                        a = proj(wc[sw], f"wc{sw}", t0, n, M=8, wcols=(8, 16))
                        O("act", lambda e, a=a, t0=t0: e.activation(out=gA[:, t0:t0 + 512], in_=pA[a][0:8, :], func=AF.Exp, bias=sm[:, 1:2]),
                          reads=[f"pA{a}", "sm"], writes=["gA"])
                    O("act", lambda e: e.activation(out=gB[:], in_=gA[:], func=AF.Ln, bias=cst[0:8, 3:4]), reads=["gA", "cst"], writes=["gB"])
                    O("dve", lambda e: e.tensor_scalar(out=gA[:], in0=gB[:], scalar1=sm[:, 3:4], scalar2=None, op0=ALU.mult), reads=["gB", "sm"], writes=["gA"])
                    cur, oth, cn, on = gA, gB, "gA", "gB"
                    sh = 1
                    while sh < 128:
                        c3 = cur[:].rearrange("p (n j) -> p n j", j=128)
                        o3 = oth[:].rearrange("p (n j) -> p n j", j=128)
                        O("dve", lambda e, c3=c3, o3=o3, sh=sh: e.tensor_tensor(out=o3[:, :, sh:], in0=c3[:, :, sh:], in1=c3[:, :, :128 - sh], op=ALU.add),
                          reads=[cn], writes=[on])
                        O("pool", lambda e, c3=c3, o3=o3, sh=sh: e.tensor_copy(out=o3[:, :, :sh], in_=c3[:, :, :sh]), reads=[cn], writes=[on])
                        cur, oth, cn, on = oth, cur, on, cn
                        sh *= 2
                    dma(bg_s[0, :, :], bT[:], ["bT"], ["bg_s"])
                    dma(bg_s[1, :, :], cur[:], [cn], ["bg_s"])
                    h16 = SB(esG, "h16", [8, T], BF16)
                    l16 = SB(esG, "l16", [8, T], BF16)
                    b16 = SB(esG, "b16", [8, T], BF16)
                    O("dve", lambda e, cur=cur: e.tensor_copy(out=h16[:], in_=cur[:]), reads=[cn], writes=["h16"])
                    O("dve", lambda e, oth=oth: e.tensor_copy(out=oth[:], in_=h16[:]), reads=["h16"], writes=[on])
                    O("dve", lambda e, cur=cur, oth=oth: e.tensor_tensor(out=oth[:], in0=cur[:], in1=oth[:], op=ALU.subtract), reads=[cn, on], writes=[on])
                    O("dve", lambda e, oth=oth: e.tensor_copy(out=l16[:], in_=oth[:]), reads=[on], writes=["l16"])
                    O("dve", lambda e: e.tensor_copy(out=b16[:], in_=bT[:]), reads=["bT"], writes=["b16"])
                    dma(bg16_s[0:8, :], h16[:], ["h16"], ["bg16_s"])
                    dma(bg16_s[8:16, :], l16[:], ["l16"], ["bg16_s"])
                    dma(bg16_s[16:24, :], b16[:], ["b16"], ["bg16_s"])
                S.barrier()
                esD = ExitStack()
                with esD:
                    cwt = SB(esD, "cwt", [128, 96], F32)
                    dma(cwt[:], cw, [], ["cwt"])
                    pc = [SB(esD, f"pc{i}", [128, T + 3], F32) for i in range(2)]
                    cvs = [SB(esD, f"cv{i}", [128, T], F32) for i in range(2)]
                    sqt = [SB(esD, f"sqt{i}", [128, 512], BF16) for i in range(2)]
                    nbs = [SB(esD, f"nbD{i}", [128, T], BF16) for i in range(2)]
                    rs = [SB(esD, f"rsD{i}", [128, 512], F32) for i in range(2)]
                    sd = [SB(esD, f"sdD{i}", [128, 512], F32) for i in range(2)]
                    tm = SB(esD, "tmD", [128, 32, 128], BF16)
                    for i in range(2):
                        O("pool", lambda e, i=i: e.memset(pc[i][:, 0:3], 0.0), writes=[f"pc{i}"])

                    for j in range(24):
                        kind, h = j // 8, j % 8
                        sw = load_wc(3072 + j * 128)
                        sp_ = j % 2
                        cv, cvn = cvs[sp_], f"cv{sp_}"
                        nb, nbn = nbs[sp_], f"nbD{sp_}"
                        scl = (128.0 ** -0.5) if kind == 0 else 1.0
                        for ti, (t0, n, o0) in enumerate(all_tiles):
                            a = proj(wc[sw], f"wc{sw}", t0, n)
                            pcw = (f"pc{sp_}", ti)
                            pcr = [(f"pc{sp_}", ti), (f"pc{sp_}", ti - 1)] if ti > 0 else [(f"pc{sp_}", ti), f"pc{sp_}"]
                            cvt = (cvn, ti)
                            if ti % 2 == 0:
                                O("act", lambda e, a=a, sp_=sp_, t0=t0: e.activation(out=pc[sp_][:, 3 + t0:3 + t0 + 512], in_=pA[a][:, :], func=AF.Copy),
                                  reads=[f"pA{a}"], writes=[pcw])
                            else:
                                O("dve", lambda e, a=a, sp_=sp_, t0=t0: e.tensor_copy(out=pc[sp_][:, 3 + t0:3 + t0 + 512], in_=pA[a][:, :]),
                                  reads=[f"pA{a}"], writes=[pcw])
                            O("act", lambda e, cv=cv, sp_=sp_, j=j, t0=t0: e.activation(out=cv[:, t0:t0 + 512], in_=pc[sp_][:, 3 + t0:3 + t0 + 512], func=AF.Copy, scale=cwt[:, j * 4 + 3:j * 4 + 4]),
                              reads=pcr + ["cwt"], writes=[cvt])
                            for k in (2, 1, 0):
                                O("dve", lambda e, cv=cv, sp_=sp_, j=j, k=k, t0=t0: e.scalar_tensor_tensor(out=cv[:, t0:t0 + 512], in0=pc[sp_][:, k + t0:k + t0 + 512], scalar=cwt[:, j * 4 + k:j * 4 + k + 1],
                                                                                                 in1=cv[:, t0:t0 + 512], op0=ALU.mult, op1=ALU.add),
                                  reads=pcr + ["cwt", cvt], writes=[cvt])
                        for ti, (t0, n, o0) in enumerate(all_tiles):
                            cvt = (cvn, ti)
                            if kind < 2:
                                O("act", lambda e, cv=cv, t0=t0: e.activation(out=cv[:, t0:t0 + 512], in_=cv[:, t0:t0 + 512], func=AF.Silu), reads=[cvt], writes=[cvt])
                            else:
                                O("act", lambda e, cv=cv, nb=nb, t0=t0: e.activation(out=nb[:, t0:t0 + 512], in_=cv[:, t0:t0 + 512], func=AF.Silu), reads=[cvt], writes=[(nbn, ti)])
                        if kind < 2:
                            for ti, (t0, n, o0) in enumerate(all_tiles):
                                cvt = (cvn, ti)
                                sqs = cnt["pB"] % 2
                                O("pool", lambda e, cv=cv, t0=t0, sqs=sqs: e.tensor_tensor(out=sqt[sqs][:], in0=cv[:, t0:t0 + 512], in1=cv[:, t0:t0 + 512], op=ALU.mult), reads=[cvt], writes=[f"sqt{sqs}"])
                                s = cnt["pB"] % 2
                                cnt["pB"] += 1
                                O("pe", lambda e, s=s, sqs=sqs: e.matmul(pB[s][:, :], lhsT=onesb[:], rhs=sqt[sqs][:], start=True, stop=True),
                                  reads=["onesb", f"sqt{sqs}"], writes=[f"pB{s}"])
                                rsqrt_cols(pB[s][:, :], rs[s][:], 1.0, [f"pB{s}"], [f"rsD{s}"], sd[s][:], f"sdD{s}")
                                O("dve", lambda e, cv=cv, nb=nb, s=s, t0=t0, scl=scl: e.scalar_tensor_tensor(out=nb[:, t0:t0 + 512], in0=cv[:, t0:t0 + 512], scalar=scl, in1=rs[s][:],
                                                                                             op0=ALU.mult, op1=ALU.mult),
                                  reads=[cvt, f"rsD{s}"], writes=[(nbn, ti)])
                        nball = [(nbn, ti) for ti in range(8)]
                        if kind < 2:
                            dma((dq_s if kind == 0 else dk_s)[h, :, :], nb[:], nball, [("dqk_s", kind, h)])
                        if kind >= 1:
                            for g in range(8):
                                s = cnt["pTm"] % 2
                                cnt["pTm"] += 1
                                for q in range(4):
                                    tt = g * 4 + q
                                    O("pe", lambda e, cv=cv, nb=nb, s=s, q=q, tt=tt: e.transpose(pTm[s][:, q, :], nb[:, tt * 128:(tt + 1) * 128], identb[:]),
                                      reads=[(nbn, tt // 4), "identb"], writes=[f"pTm{s}"])
                                eng = "act" if g % 2 == 0 else "dve"
                                if eng == "act":
                                    O("act", lambda e, cv=cv, nb=nb, s=s, g=g: e.activation(out=tm[:, g * 4:(g + 1) * 4, :], in_=pTm[s][:], func=AF.Copy), reads=[f"pTm{s}"], writes=["tmD"])
                                else:
                                    O("dve", lambda e, cv=cv, nb=nb, s=s, g=g: e.tensor_copy(out=tm[:, g * 4:(g + 1) * 4, :], in_=pTm[s][:]), reads=[f"pTm{s}"], writes=["tmD"])
                            dst = (dktm_s if kind == 1 else dvtm_s)[h].rearrange("(n p) d -> p n d", p=128)
                            dma(dst, tm[:], ["tmD"], [("dtm_s", kind, h)])
                S.barrier()
                S.flush()
        if stop_after == "B":
            return finish(nc, S, top, y, dbg_out)

        esCE = ExitStack()
        with esCE:
            oaT = SB(esCE, "oaT", [128, 8, MYT], BF16)
            obT = SB(esCE, "obT", [128, 8, MYT], BF16)
            esC = ExitStack()
            with esC:
                kT = [SB(esC, f"kT{i}", [128, T], BF16) for i in range(2)]
                vh = [SB(esC, f"vh{i}", [128, 32, 128], BF16) for i in range(2)]
                qT = [SB(esC, f"qT{i}", [128, MYT], BF16) for i in range(2)]
                kb_t = SB(esC, "kb_t", [128, 32], F32)
                am = SB(esC, "am", [128, 3 * QW], BF16)
                pP = [SB(esC, f"pP{i}", [128, 2, QW], BF16) for i in range(3)]
                lamt = SB(esC, "lamt", [128, 256], F32)
                lt = SB(esC, "lt", [128, 128], F32)
                ls = SB(esC, "ls", [128, 8], F32)
                asb = SB(esC, "asb", [128, 2], F32)
                rr = [SB(esC, f"rr{m}", [128, QW], F32) for m in range(2)]
                tt_ = [SB(esC, f"ttC{m}", [128, QW], F32) for m in range(2)]
                osb = SB(esC, "osb", [128, QW], F32)
                sqc = SB(esC, "sqc", [128, QW], BF16)
                sdc = SB(esC, "sdc", [128, QW], F32)
                rsc = SB(esC, "rsc", [128, QW], F32)
                pS = [PS(esC, f"pS{i}", [128, 2, 512]) for i in range(2)]
                pO = [PS(esC, f"pO{m}", [128, 512]) for m in range(2)]
                pLt = PS(esC, "pLt", [128, 2, 512])
                pL = [pLt[:, m, :] for m in range(2)]
                rrt = SB(esC, "rrt", [128, 2, QW], F32)
                dma(kb_t[:], kbias, [], ["kb_t"])
                dma(am[:], c_amask, [], ["am"])
                dma(lamt[:], lam4.partition_broadcast(128), [], ["lamt"])
                dma(asb[:, 0:1], asub, [], ["asb"])
                O("dve", lambda e: e.tensor_scalar(out=asb[:, 1:2], in0=asb[:, 0:1], scalar1=0.8, scalar2=None, op0=ALU.mult), reads=["asb"], writes=["asb"])
                O("dve", lambda e: e.tensor_tensor(out=lt[:, 0:64], in0=lamt[:, 0:64], in1=lamt[:, 64:128], op=ALU.mult), reads=["lamt"], writes=["lt"])
                O("dve", lambda e: e.tensor_tensor(out=lt[:, 64:128], in0=lamt[:, 128:192], in1=lamt[:, 192:256], op=ALU.mult), reads=["lamt"], writes=["lt"])
                O("dve", lambda e: e.reduce_sum(out=ls[:, 0:2], in_=lt[:].rearrange("p (a b) -> p a b", a=2), axis=AX.X), reads=["lt"], writes=["ls"])
                O("act", lambda e: e.activation(out=ls[:, 2:4], in_=ls[:, 0:2], func=AF.Exp), reads=["ls"], writes=["ls"])
                O("dve", lambda e: e.tensor_tensor(out=ls[:, 4:5], in0=ls[:, 3:4], in1=ls[:, 2:3], op=ALU.subtract), reads=["ls"], writes=["ls"])
                O("dve", lambda e: e.tensor_scalar(out=ls[:, 5:6], in0=ls[:, 4:5], scalar1=-0.2, scalar2=None, op0=ALU.add), reads=["ls"], writes=["ls"])
                nlam = ls[:, 5:6]
                pcount = 0
                for h in range(8):
                    hs = h % 2
                    dma(kT[hs][:], kT_s[h, :, :], [("kT_s", h)], [f"kT{hs}"])
                    dma(vh[hs][:], v_s[:, h * 128:(h + 1) * 128].rearrange("(n p) d -> p n d", p=128), ["v_s"], [f"vh{hs}"])
                    dma(qT[hs][:], qT_s[h, :, :], [("qT_s", h)], [f"qT{hs}"])
                    for i in range(NB):
                        q0 = i * QW
                        nkb = 4 * i + 4
                        def emit_S(kb, hs=hs, q0=q0):
                            ss = kb % 2
                            for m in range(2):
                                O("pe", lambda e, ss=ss, m=m, hs=hs, kb=kb, q0=q0: e.matmul(pS[ss][:, m, 0:QW], lhsT=kT[hs][m * 64:(m + 1) * 64, kb * 128:(kb + 1) * 128],
                                                                                           rhs=qT[hs][m * 64:(m + 1) * 64, q0:q0 + QW], start=True, stop=True),
                                  reads=[f"kT{hs}", f"qT{hs}"], writes=[f"pS{ss}"])
                        emit_S(0)
                        for kb in range(nkb):
                            ss = kb % 2
                            pp = pcount % 3
                            pcount += 1
                            if kb + 1 < nkb:
                                emit_S(kb + 1)
                            O("act", lambda e, ss=ss, pp=pp, kb=kb: e.activation(out=pP[pp][:], in_=pS[ss][:, :, 0:QW], func=AF.Exp, scale=0.125, bias=kb_t[:, kb:kb + 1]),
                              reads=[f"pS{ss}", "kb_t"], writes=[f"pP{pp}"])
                            r = kb - (4 * i + 1)
                            if r >= 0:
                                O("dve", lambda e, pp=pp, r=r: e.tensor_tensor(out=pP[pp][:], in0=pP[pp][:], in1=am[:, r * QW:(r + 1) * QW].unsqueeze(1).to_broadcast([128, 2, QW]), op=ALU.mult),
                                  reads=[f"pP{pp}", "am"], writes=[f"pP{pp}"])
                            for m in range(2):
                                O("pe", lambda e, m=m, pp=pp, hs=hs, kb=kb, nkb=nkb: e.matmul(pO[m][:, 0:QW], lhsT=vh[hs][:, kb, :], rhs=pP[pp][:, m, :], start=(kb == 0), stop=(kb == nkb - 1)),
                                  reads=[f"vh{hs}", f"pP{pp}"], writes=[f"pO{m}"])
                                O("pe", lambda e, m=m, pp=pp, kb=kb, nkb=nkb: e.matmul(pL[m][:, 0:QW], lhsT=onesb[:], rhs=pP[pp][:, m, :], start=(kb == 0), stop=(kb == nkb - 1)),
                                  reads=["onesb", f"pP{pp}"], writes=[f"pL{m}"])
                        O("act", lambda e: e.activation(out=rrt[:], in_=pLt[:, :, 0:QW], func=AF.Ln, bias=cst[:, 1:2]), reads=["pL0", "pL1", "cst"], writes=["rrt"])
                        O("act", lambda e: e.activation(out=rrt[:], in_=rrt[:], func=AF.Exp, scale=-1.0), reads=["rrt"], writes=["rrt"])
                        for m in range(2):
                            O("dve", lambda e, m=m: e.tensor_tensor(out=tt_[m][:], in0=pO[m][:, 0:QW], in1=rrt[:, m, :], op=ALU.mult), reads=[f"pO{m}", "rrt"], writes=[f"ttC{m}"])
                        O("dve", lambda e: e.scalar_tensor_tensor(out=osb[:], in0=tt_[1][:], scalar=nlam, in1=tt_[0][:], op0=ALU.mult, op1=ALU.add),
                          reads=["ttC0", "ttC1", "ls"], writes=["osb"])
                        O("act", lambda e: e.activation(out=sqc[:], in_=osb[:], func=AF.Square), reads=["osb"], writes=["sqc"])
                        O("pe", lambda e: e.matmul(pS[0][:, 0, 0:QW], lhsT=onesb[:], rhs=sqc[:], start=True, stop=True), reads=["onesb", "sqc"], writes=["pS0"])
                        rsqrt_cols(pS[0][:, 0, 0:QW], rsc[:], 1.0 / 128, ["pS0"], ["rsc"], sdc[:], "sdc")
                        O("dve", lambda e, h=h, q0=q0: e.scalar_tensor_tensor(out=oaT[:, h, q0:q0 + QW], in0=osb[:], scalar=asb[:, 1:2], in1=rsc[:], op0=ALU.mult, op1=ALU.mult),
                          reads=["osb", "asb", "rsc"], writes=[("oaT", i)])
                S.barrier()
                S.flush()
            if stop_after == "C":
                if dbg:
                    dbg_out["oaT"] = nc.dram_tensor("dbg_oaT", [128, 8 * MYT], BF16, kind="ExternalOutput").ap()
                    dma(dbg_out["oaT"], oaT[:].rearrange("p h t -> p (h t)"), [("oaT", i) for i in range(NB)], ["dbg"])
                return finish(nc, S, top, y, dbg_out)
            build_delta(nc, S, O, SB, PS, dma, cst, identb, identf, onesb, rsqrt_cols, obT,
                        dq_s, dk_s, dktm_s, dvtm_s, z_s, bg_s, bon, c_sel, c_dmU, c_dmL, bg16_s, c_sel16)
            if stop_after == "D":
                if dbg:
                    dbg_out["obT"] = nc.dram_tensor("dbg_obT", [128, 8 * MYT], BF16, kind="ExternalOutput").ap()
                    dma(dbg_out["obT"], obT[:].rearrange("p h t -> p (h t)"), [("obT", i) for i in range(NB)], ["dbg"])
                return finish(nc, S, top, y, dbg_out)
            esE = ExitStack()
            with esE:
                h2T = SB(esE, "h2T", [128, 8, MYT], BF16)
                wa = SB(esE, "wa", [128, 8, 1024], BF16)
                wb_ = SB(esE, "wb_", [128, 8, 1024], BF16)
                wo = SB(esE, "wo", [128, 8, 1024], BF16)
                gf = SB(esE, "gf", [128, 8], F32)
                dma(wa[:], w_a.rearrange("(k p) c -> p k c", p=128), [], ["wa"], eng="pool")
                dma(wb_[:], w_b.rearrange("(k p) c -> p k c", p=128), [], ["wb_"], eng="pool")
                dma(wo[:], w_o.rearrange("(k p) c -> p k c", p=128), [], ["wo"], eng="pool")
                dma(gf[:], gffn, [], ["gf"])
                gab = [SB(esE, f"gab{i}", [128, 2, QW], BF16) for i in range(2)]
                me1 = [SB(esE, f"me1{i}", [128, QW], F32) for i in range(2)]
                me2 = [SB(esE, f"me2{i}", [128, QW], F32) for i in range(2)]
                mT = SB(esE, "mT", [128, 8, QW], BF16)
                xin = [SB(esE, f"xin{i}", [128, 1024], F32) for i in range(2)]
                xm = [SB(esE, f"xm{i}", [128, 1024], F32) for i in range(2)]
                xn2 = [SB(esE, f"xn2{i}", [128, 1024], BF16) for i in range(2)]
                junkE = SB(esE, "junkE", [128, 1024], BF16)
                stE = SB(esE, "stE", [128, 3 * 32], F32)
                pY = [[PS(esE, f"pY{i}{m}", [128, 512]) for m in range(2)] for i in range(2)]
                pD = [PS(esE, f"pD{m}", [128, 512]) for m in range(2)]
                pT2 = [PS(esE, f"pT2{i}", [128, 8, 128], BF16) for i in range(2)]
                O("dve", lambda e: e.memset(stE[:], 0.0), writes=["stE"])
                n_sub = 0
                for i in range(NB):
                    q0 = i * QW
                    for c in range(8):
                        sy = c % 2
                        sg = c % 2
                        dma(gab[sg][:, 0, :], gate_s[c, :, q0:q0 + QW], [("zg_s", 8 + c)], [f"gab{sg}"])
                        dma(gab[sg][:, 1, :], gate_s[8 + c, :, q0:q0 + QW], [("zg_s", 16 + c)], [f"gab{sg}"])
                        for (m, wt, wn, src, sn) in ((0, wa, "wa", oaT, "oaT"), (1, wb_, "wb_", obT, "obT")):
                            for hh in range(8):
                                O("pe", lambda e, sy=sy, m=m, wt=wt, src=src, hh=hh, c=c, q0=q0: e.matmul(pY[sy][m][:, 0:QW], lhsT=wt[:, hh, c * 128:(c + 1) * 128],
                                                                                                       rhs=src[:, hh, q0:q0 + QW], start=(hh == 0), stop=(hh == 7)),
                                  reads=[wn, (sn, i)], writes=[f"pY{sy}{m}"])
                        O("dve", lambda e, sy=sy, sg=sg: e.tensor_tensor(out=me1[sy][:], in0=pY[sy][0][:, 0:QW], in1=gab[sg][:, 0, :], op=ALU.mult),
                          reads=[f"pY{sy}0", f"gab{sg}"], writes=[f"me1{sy}"])
                        O("dve", lambda e, sy=sy, sg=sg: e.tensor_tensor(out=me2[sy][:], in0=pY[sy][1][:, 0:QW], in1=gab[sg][:, 1, :], op=ALU.mult),
                          reads=[f"pY{sy}1", f"gab{sg}"], writes=[f"me2{sy}"])
                        O("pool", lambda e, sy=sy, c=c: e.tensor_tensor(out=mT[:, c, :], in0=me1[sy][:], in1=me2[sy][:], op=ALU.add),
                          reads=[f"me1{sy}", f"me2{sy}"], writes=["mT"])
                    for (a0, M) in ((0, 2), (2, 128), (130, 128)):
                        sx = n_sub % 2
                        col = n_sub
                        n_sub += 1
                        tok0 = 512 * i + 254 + a0
                        dma(xin[sx][0:M, :], xs[tok0:tok0 + M, :], [], [f"xin{sx}"])
                        for half in range(2):
                            for c in range(8):
                                O("pe", lambda e, half=half, c=c, a0=a0, M=M: e.matmul(pD[half][0:M, :], lhsT=mT[:, c, a0:a0 + M], rhs=wo[:, c, half * 512:(half + 1) * 512],
                                                                                      start=(c == 0), stop=(c == 7)),
                                  reads=["mT", "wo"], writes=[f"pD{half}"])
                            O("dve", lambda e, half=half, sx=sx, M=M: e.tensor_tensor(out=xm[sx][0:M, half * 512:(half + 1) * 512], in0=pD[half][0:M, :],
                                                                                     in1=xin[sx][0:M, half * 512:(half + 1) * 512], op=ALU.add),
                              reads=[f"pD{half}", f"xin{sx}"], writes=[f"xm{sx}"])
                        dma(xmid_s[i, a0:a0 + M, :], xm[sx][0:M, :], [f"xm{sx}"], [("xmid_s", i)])
                        O("act", lambda e, sx=sx, M=M, col=col: e.activation(out=junkE[0:M, :], in_=xm[sx][0:M, :], func=AF.Square, accum_out=stE[0:M, col:col + 1]),
                          reads=[f"xm{sx}", "stE"], writes=["junkE", "stE"])
                        rsqrt_cols(stE[0:M, col:col + 1], stE[0:M, 64 + col:65 + col], 1.0 / 1024, ["stE"], ["stE"], stE[0:M, 32 + col:33 + col], "stE")
                        O("dve", lambda e, sx=sx, M=M, col=col: e.tensor_scalar(out=xn2[sx][0:M, :], in0=xm[sx][0:M, :], scalar1=stE[0:M, 64 + col:65 + col], scalar2=None, op0=ALU.mult),
                          reads=[f"xm{sx}", "stE"], writes=[f"xn2{sx}"])
                        for kc in range(8):
                            O("pe", lambda e, sx=sx, kc=kc, M=M: e.transpose(pT2[sx][:, kc, 0:M], xn2[sx][0:M, kc * 128:(kc + 1) * 128], identb[0:M, 0:M]),
                              reads=[f"xn2{sx}", "identb"], writes=[f"pT2{sx}"])
                        O("dve", lambda e, sx=sx, M=M, q0=q0, a0=a0: e.tensor_tensor(out=h2T[:, :, q0 + a0:q0 + a0 + M], in0=pT2[sx][:, :, 0:M],
                                                                                    in1=gf[:].unsqueeze(2).to_broadcast([128, 8, M]), op=ALU.mult),
                          reads=[f"pT2{sx}", "gf"], writes=[("h2T", i)])
                    dma(h2T_s[:, :, q0:q0 + QW], h2T[:, :, q0:q0 + QW], [("h2T", i)], [("h2T_s", i)])
                S.barrier()
                S.flush()
        if stop_after == "E":
            return finish(nc, S, top, y, dbg_out)
        build_ffn(nc, S, O, SB, PS, dma, cst, rsqrt_cols, h2T_s, xmid_s, w_up, w_dn, fcw, fcb, gfin, y)
        return finish(nc, S, top, y, dbg_out)


def finish(nc, S, top, y, dbg_out):
    S.barrier()
    S.flush()
    return nc


def build_delta(nc, S, O, SB, PS, dma, cst, identb, identf, onesb, rsqrt_cols, obT,
                dq_s, dk_s, dktm_s, dvtm_s, z_s, bg_s, bon, c_sel, c_dmU, c_dmL, bg16_s, c_sel16):
    es = ExitStack()
    with es:
        sel = SB(es, "sel", [8, 8 * 128], F32)
        dmU = SB(es, "dmU", [128, 128], F32)
        dmL = SB(es, "dmL", [128, 128], F32)
        bont = SB(es, "bont", [128, 1], F32)
        gct = [SB(es, f"gct{i}", [8, 512], F32) for i in range(2)]
        bet = [SB(es, f"bet{i}", [8, 512], F32) for i in range(2)]
        glast = SB(es, "glast", [8, 32], F32)
        gctm = SB(es, "gctm", [128, 32, 8], F32)
        betm = SB(es, "betm", [128, 32, 8], F32)
        gch = SB(es, "gch", [128, 32], F32)
        beh = SB(es, "beh", [128, 32], F32)
        cb1 = SB(es, "cb1", [128, 32], F32)
        cb2 = SB(es, "cb2", [128, 32], F32)
        egl = SB(es, "egl", [128, 32], F32)
        qTd = SB(es, "qTd", [128, T], BF16)
        kTd = SB(es, "kTd", [128, T], BF16)
        ktm = SB(es, "ktm", [128, 32, 128], BF16)
        vtm = SB(es, "vtm", [128, 32, 128], BF16)
        qgT = SB(es, "qgT", [128, T], BF16)
        kbT = SB(es, "kbT", [128, T], BF16)
        vb = SB(es, "vb", [128, 32, 128], BF16)
        kbg = SB(es, "kbg", [128, 32, 128], BF16)
        kg = SB(es, "kg", [128, 32, 128], BF16)
        oTm = SB(es, "oTm", [128, NB, QW], F32)
        eg = SB(es, "eg", [128, 512], F32)
        e1 = SB(es, "e1", [128, 4, 128], F32)
        e2 = SB(es, "e2", [128, 4, 128], F32)
        DmI = SB(es, "DmI", [128, 4, 128], F32)
        DmS = SB(es, "DmS", [128, 4, 128], F32)
        Dp = SB(es, "Dp", [128, 4, 128], F32)
        qkT = [SB(es, f"qkT{i}", [128, 4, 128], BF16) for i in range(2)]
        Bb = [SB(es, f"Bb{i}", [128, 4, 128], BF16) for i in range(2)]
        BTb = [SB(es, f"BTb{i}", [128, 4, 128], BF16) for i in range(2)]
        Xb = [SB(es, f"Xb{i}", [128, 4, 128], BF16) for i in range(2)]
        u_sb = [SB(es, f"u_sb{i}", [128, 4, 128], F32) for i in range(2)]
        wT_sb = [SB(es, f"wT_sb{i}", [128, 4, 128], BF16) for i in range(2)]
        S_f = SB(es, "S_f", [128, 128], F32)
        S_b = SB(es, "S_b", [128, 128], BF16)
        vnew = [SB(es, f"vnew{i}", [128, 128], BF16) for i in range(2)]
        sqd = SB(es, "sqd", [128, QW], BF16)
        sdd = SB(es, "sdd", [128, QW], F32)
        rsd = SB(es, "rsd", [128, QW], F32)
        ztl = [SB(es, f"ztl{i}", [128, QW], BF16) for i in range(2)]
        tno = SB(es, "tno", [128, QW], F32)
        pGR = PS(es, "pGR", [128, 512])
        pBR = PS(es, "pBR", [128, 512])
        pG1 = PS(es, "pG1", [128, 4, 128])
        pG2 = PS(es, "pG2", [128, 4, 128])
        pG3 = PS(es, "pG3", [128, 4, 128])
        pWS = PS(es, "pWS", [128, 512])
        pOT = PS(es, "pOT", [128, 512])
        pSU = PS(es, "pSU", [128, 512])
        dma(sel[:], c_sel, [], ["sel"])
        sel16 = SB(es, "sel16", [16, 8 * 128], BF16)
        dma(sel16[:], c_sel16, [], ["sel16"])
        g16 = [SB(es, f"g16{i}", [16, 512], BF16) for i in range(2)]
        be16 = [SB(es, f"be16{i}", [8, 512], BF16) for i in range(2)]
        dma(dmU[:], c_dmU, [], ["dmU"])
        dma(dmL[:], c_dmL, [], ["dmL"])
        dma(bont[:], bon, [], ["bont"])
        dma(glast[:], bg_s[1].rearrange("h (n j) -> h n j", j=128)[:, :, 127], ["bg_s"], ["glast"], allow_slow_non_contiguous=True)
        for g in range(8):
            s = g % 2
            dma(gct[s][:], bg_s[1, :, g * 512:(g + 1) * 512], ["bg_s"], [f"gct{s}"])
            dma(bet[s][:], bg_s[0, :, g * 512:(g + 1) * 512], ["bg_s"], [f"bet{s}"])
            for q in range(4):
                n = g * 4 + q
                O("pe", lambda e, s=s, q=q, n=n: e.matmul(pGR[:, n * 8:(n + 1) * 8], lhsT=gct[s][:, q * 128:(q + 1) * 128], rhs=identf[0:8, 0:8], start=True, stop=True),
                  reads=[f"gct{s}", "identf"], writes=["pGR"])
                O("pe", lambda e, s=s, q=q, n=n: e.matmul(pBR[:, n * 8:(n + 1) * 8], lhsT=bet[s][:, q * 128:(q + 1) * 128], rhs=identf[0:8, 0:8], start=True, stop=True),
                  reads=[f"bet{s}", "identf"], writes=["pBR"])
        O("dve", lambda e: e.tensor_copy(out=gctm[:].rearrange("p n h -> p (n h)"), in_=pGR[:, 0:256]), reads=["pGR"], writes=["gctm"])
        O("dve", lambda e: e.tensor_copy(out=betm[:].rearrange("p n h -> p (n h)"), in_=pBR[:, 0:256]), reads=["pBR"], writes=["betm"])
        for h in range(8):
            selh = sel[:, h * 128:(h + 1) * 128]
            dma(qTd[:], dq_s[h, :, :], [("dqk_s", 0, h)], ["qTd"])
            dma(kTd[:], dk_s[h, :, :], [("dqk_s", 1, h)], ["kTd"])
            dma(ktm[:], dktm_s[h].rearrange("(n p) d -> p n d", p=128), [("dtm_s", 1, h)], ["ktm"])
            dma(vtm[:], dvtm_s[h].rearrange("(n p) d -> p n d", p=128), [("dtm_s", 2, h)], ["vtm"])
            O("dve", lambda e, h=h: e.tensor_copy(out=gch[:], in_=gctm[:, :, h]), reads=["gctm"], writes=["gch"])
            O("dve", lambda e, h=h: e.tensor_copy(out=beh[:], in_=betm[:, :, h]), reads=["betm"], writes=["beh"])
            O("pe", lambda e, selh=selh: e.matmul(pWS[:, 0:32], lhsT=selh, rhs=glast[:], start=True, stop=True), reads=["sel", "glast"], writes=["pWS"])
            O("act", lambda e: e.activation(out=egl[:], in_=pWS[:, 0:32], func=AF.Exp), reads=["pWS"], writes=["egl"])
            O("dve", lambda e: e.tensor_tensor(out=cb2[:], in0=pWS[:, 0:32], in1=gch[:], op=ALU.subtract), reads=["pWS", "gch"], writes=["cb2"])
            O("act", lambda e: e.activation(out=cb2[:], in_=cb2[:], func=AF.Exp), reads=["cb2"], writes=["cb2"])
            O("act", lambda e: e.activation(out=cb1[:], in_=gch[:], func=AF.Exp), reads=["gch"], writes=["cb1"])
            O("dve", lambda e: e.tensor_tensor(out=cb1[:], in0=cb1[:], in1=beh[:], op=ALU.mult), reads=["cb1", "beh"], writes=["cb1"])
            O("dve", lambda e: e.tensor_tensor(out=vb[:], in0=vtm[:], in1=beh[:].unsqueeze(2).to_broadcast([128, 32, 128]), op=ALU.mult), reads=["vtm", "beh"], writes=["vb"])
            O("dve", lambda e: e.tensor_tensor(out=kbg[:], in0=ktm[:], in1=cb1[:].unsqueeze(2).to_broadcast([128, 32, 128]), op=ALU.mult), reads=["ktm", "cb1"], writes=["kbg"])
            O("dve", lambda e: e.tensor_tensor(out=kg[:], in0=ktm[:], in1=cb2[:].unsqueeze(2).to_broadcast([128, 32, 128]), op=ALU.mult), reads=["ktm", "cb2"], writes=["kg"])
            O("dve", lambda e: e.memset(S_f[:], 0.0), writes=["S_f"])
            O("pool", lambda e: e.memset(S_b[:], 0.0), writes=["S_b"])
            def gen_inv(g, h=h, selh=selh):
                s = g % 2
                gs = g % 2
                t0 = g * 512
                dma(g16[s][:], bg16_s[0:16, t0:t0 + 512], ["bg16_s"], [f"g16{s}"])
                dma(be16[s][:], bg16_s[16:24, t0:t0 + 512], ["bg16_s"], [f"be16{s}"])
                O("pe", lambda e, s=s, h=h: e.matmul(pGR[:, :], lhsT=sel16[:, h * 128:(h + 1) * 128], rhs=g16[s][:], start=True, stop=True), reads=["sel16", f"g16{s}"], writes=["pGR"])
                O("pe", lambda e, s=s, h=h: e.matmul(pBR[:, :], lhsT=sel16[0:8, h * 128:(h + 1) * 128], rhs=be16[s][:], start=True, stop=True), reads=["sel16", f"be16{s}"], writes=["pBR"])
                yield
                O("act", lambda e: e.activation(out=eg[:], in_=pGR[:, :], func=AF.Exp), reads=["pGR"], writes=["eg"])
                O("dve", lambda e, t0=t0: e.tensor_tensor(out=kbT[:, t0:t0 + 512], in0=kTd[:, t0:t0 + 512], in1=pBR[:, :], op=ALU.mult), reads=["kTd", "pBR"], writes=[("kbT", g)])
                for c in range(4):
                    n = g * 4 + c
                    O("dve", lambda e, c=c, n=n: e.scalar_tensor_tensor(out=e1[:, c, :], in0=pGR[:, c * 128:(c + 1) * 128], scalar=gch[:, n:n + 1], in1=dmU[:], op0=ALU.subtract, op1=ALU.add),
                      reads=["pGR", "gch", "dmU"], writes=["e1"])
                    O("dve", lambda e, c=c, n=n: e.scalar_tensor_tensor(out=e2[:, c, :], in0=pGR[:, c * 128:(c + 1) * 128], scalar=gch[:, n:n + 1], in1=dmL[:], op0=ALU.subtract, op1=ALU.add),
                      reads=["pGR", "gch", "dmL"], writes=["e2"])
                yield
                O("act", lambda e: e.activation(out=DmI[:], in_=e1[:], func=AF.Exp), reads=["e1"], writes=["DmI"])
                O("act", lambda e: e.activation(out=Dp[:], in_=e2[:], func=AF.Exp, scale=-1.0), reads=["e2"], writes=["Dp"])
                O("dve", lambda e, t0=t0: e.tensor_tensor(out=qgT[:, t0:t0 + 512], in0=qTd[:, t0:t0 + 512], in1=eg[:], op=ALU.mult), reads=["qTd", "eg"], writes=[("qgT", g)])
                O("pool", lambda e: e.tensor_tensor(out=DmS[:], in0=DmI[:], in1=identf[:].unsqueeze(1).to_broadcast([128, 4, 128]), op=ALU.subtract), reads=["DmI", "identf"], writes=["DmS"])
                for c in range(4):
                    cs = slice(t0 + c * 128, t0 + (c + 1) * 128)
                    O("pe", lambda e, c=c, cs=cs: e.matmul(pG1[:, c, :], lhsT=kTd[:, cs], rhs=kbT[:, cs], start=True, stop=True), reads=["kTd", ("kbT", g)], writes=["pG1"])
                    O("pe", lambda e, c=c, cs=cs: e.matmul(pG2[:, c, :], lhsT=kbT[:, cs], rhs=kTd[:, cs], start=True, stop=True), reads=["kTd", ("kbT", g)], writes=["pG2"])
                    O("pe", lambda e, c=c, cs=cs: e.matmul(pG3[:, c, :], lhsT=kTd[:, cs], rhs=qTd[:, cs], start=True, stop=True), reads=["kTd", "qTd"], writes=["pG3"])
                yield
                O("dve", lambda e: e.scalar_tensor_tensor(out=Bb[0][:], in0=pG1[:], scalar=-1.0, in1=DmS[:], op0=ALU.mult, op1=ALU.mult), reads=["pG1", "DmS"], writes=["Bb0"])
                O("dve", lambda e: e.scalar_tensor_tensor(out=BTb[0][:], in0=pG2[:], scalar=-1.0, in1=Dp[:], op0=ALU.mult, op1=ALU.mult), reads=["pG2", "Dp"], writes=["BTb0"])
                O("dve", lambda e, gs=gs: e.tensor_tensor(out=qkT[gs][:], in0=pG3[:], in1=DmI[:], op=ALU.mult), reads=["pG3", "DmI"], writes=[f"qkT{gs}"])
                O("pool", lambda e: e.tensor_tensor(out=Xb[0][:], in0=Bb[0][:], in1=identf[:].unsqueeze(1).to_broadcast([128, 4, 128]), op=ALU.add), reads=["Bb0", "identf"], writes=["Xb0"])
                yield
                cur = 0
                for k in range(1, 7):
                    nx = 1 - cur
                    for c in range(4):
                        if k < 6:
                            O("pe", lambda e, c=c, cur=cur: e.matmul(pG1[:, c, :], lhsT=BTb[cur][:, c, :], rhs=Bb[cur][:, c, :], start=True, stop=True),
                              reads=[f"BTb{cur}", f"Bb{cur}"], writes=["pG1"])
                        O("pe", lambda e, c=c, cur=cur: e.matmul(pG2[:, c, :], lhsT=Bb[cur][:, c, :], rhs=BTb[cur][:, c, :], start=True, stop=True),
                          reads=[f"BTb{cur}", f"Bb{cur}"], writes=["pG2"])
                    yield
                    if k < 6:
                        O("act", lambda e, nx=nx: e.activation(out=Bb[nx][:], in_=pG1[:], func=AF.Copy), reads=["pG1"], writes=[f"Bb{nx}"])
                    O("act", lambda e, nx=nx: e.activation(out=BTb[nx][:], in_=pG2[:], func=AF.Copy), reads=["pG2"], writes=[f"BTb{nx}"])
                    for c in range(4):
                        O("pe", lambda e, c=c, cur=cur, nx=nx: e.matmul(pG3[:, c, :], lhsT=BTb[nx][:, c, :], rhs=Xb[cur][:, c, :], start=True, stop=True),
                          reads=[f"BTb{nx}", f"Xb{cur}"], writes=["pG3"])
                    yield
                    O("dve", lambda e, nx=nx, cur=cur: e.tensor_tensor(out=Xb[nx][:], in0=pG3[:], in1=Xb[cur][:], op=ALU.add), reads=["pG3", f"Xb{cur}"], writes=[f"Xb{nx}"])
                    cur = nx
                for c in range(4):
                    n = g * 4 + c
                    O("pe", lambda e, c=c, n=n, cur=cur: e.matmul(pG1[:, c, :], lhsT=Xb[cur][:, c, :], rhs=vb[:, n, :], start=True, stop=True), reads=[f"Xb{cur}", "vb"], writes=["pG1"])
                    O("pe", lambda e, c=c, n=n, cur=cur: e.matmul(pG2[:, c, :], lhsT=kbg[:, n, :], rhs=Xb[cur][:, c, :], start=True, stop=True), reads=[f"Xb{cur}", "kbg"], writes=["pG2"])
                yield
                O("act", lambda e, gs=gs: e.activation(out=u_sb[gs][:], in_=pG1[:], func=AF.Copy), reads=["pG1"], writes=[f"u_sb{gs}"])
                O("dve", lambda e, gs=gs: e.tensor_copy(out=wT_sb[gs][:], in_=pG2[:]), reads=["pG2"], writes=[f"wT_sb{gs}"])
                yield

            def gen_scan(g, h=h):
                gs = g % 2
                t0 = g * 512
                for c in range(4):
                    n = g * 4 + c
                    vs = n % 2
                    cs = slice(t0 + c * 128, t0 + (c + 1) * 128)
                    O("pe", lambda e, c=c, gs=gs: e.matmul(pWS[:, 0:128], lhsT=wT_sb[gs][:, c, :], rhs=S_b[:], start=True, stop=True), reads=[f"wT_sb{gs}", "S_b"], writes=["pWS"])
                    yield
                    O("dve", lambda e, c=c, vs=vs, gs=gs: e.tensor_tensor(out=vnew[vs][:], in0=u_sb[gs][:, c, :], in1=pWS[:, 0:128], op=ALU.subtract), reads=[f"u_sb{gs}", "pWS"], writes=[f"vnew{vs}"])
                    if n % 4 != 0:
                        O("pe", lambda e, cs=cs: e.matmul(pOT[:, 0:128], lhsT=S_b[:], rhs=qgT[:, cs], start=True, stop=False), reads=["S_b", ("qgT", g)], writes=["pOT"])
                    yield
                    O("pe", lambda e, n=n, vs=vs: e.matmul(pSU[:, 0:128], lhsT=kg[:, n, :], rhs=vnew[vs][:], start=True, stop=True), reads=["kg", f"vnew{vs}"], writes=["pSU"])
                    if n % 4 != 0:
                        O("pe", lambda e, c=c, vs=vs, gs=gs: e.matmul(pOT[:, 0:128], lhsT=vnew[vs][:], rhs=qkT[gs][:, c, :], start=False, stop=True), reads=[f"vnew{vs}", f"qkT{gs}"], writes=["pOT"])
                        blk = n // 4
                        if n % 4 == 1:
                            dsl, ssl = slice(0, 2), slice(126, 128)
                        elif n % 4 == 2:
                            dsl, ssl = slice(2, 130), slice(0, 128)
                        else:
                            dsl, ssl = slice(130, 258), slice(0, 128)
                    yield
                    O("dve", lambda e, n=n: e.scalar_tensor_tensor(out=S_f[:], in0=S_f[:], scalar=egl[:, n:n + 1], in1=pSU[:, 0:128], op0=ALU.mult, op1=ALU.add),
                      reads=["S_f", "egl", "pSU"], writes=["S_f"])
                    O("act", lambda e: e.activation(out=S_b[:], in_=S_f[:], func=AF.Copy), reads=["S_f"], writes=["S_b"])
                    if n % 4 != 0:
                        O("act", lambda e, blk=blk, dsl=dsl, ssl=ssl: e.activation(out=oTm[:, blk, dsl], in_=pOT[:, ssl], func=AF.Copy), reads=["pOT"], writes=["oTm"])
                    yield

            for _ in gen_inv(0):
                pass
            for g in range(8):
                gi = gen_inv(g + 1) if g + 1 < 8 else iter(())
                gsn = gen_scan(g)
                done_i = done_s = False
                while not (done_i and done_s):
                    if not done_s:
                        try:
                            next(gsn)
                        except StopIteration:
                            done_s = True
                    if not done_i:
                        try:
                            next(gi)
                        except StopIteration:
                            done_i = True
            for blk in range(NB):
                q0 = blk * QW
                zs = blk % 2
                dma(ztl[zs][:], z_s[h, :, q0:q0 + QW], [("zg_s", h)], [f"ztl{zs}"])
                O("act", lambda e, blk=blk: e.activation(out=sqd[:], in_=oTm[:, blk, :], func=AF.Square), reads=["oTm"], writes=["sqd"])
                O("pe", lambda e: e.matmul(pGR[:, 0:QW], lhsT=onesb[:], rhs=sqd[:], start=True, stop=True), reads=["onesb", "sqd"], writes=["pGR"])
                rsqrt_cols(pGR[:, 0:QW], rsd[:], 1.0 / 128, ["pGR"], ["rsd"], sdd[:], "sdd")
                O("dve", lambda e, blk=blk: e.scalar_tensor_tensor(out=tno[:], in0=oTm[:, blk, :], scalar=bont[:, 0:1], in1=rsd[:], op0=ALU.mult, op1=ALU.mult),
                  reads=["oTm", "bont", "rsd"], writes=["tno"])
                O("dve", lambda e, h=h, q0=q0, zs=zs: e.tensor_tensor(out=obT[:, h, q0:q0 + QW], in0=tno[:], in1=ztl[zs][:], op=ALU.mult),
                  reads=["tno", f"ztl{zs}"], writes=[("obT", blk)])
        S.barrier()
        S.flush()


def build_ffn(nc, S, O, SB, PS, dma, cst, rsqrt_cols, h2T_s, xmid_s, w_up, w_dn, fcw, fcb, gfin, y):
    es = ExitStack()
    with es:
        wup = SB(es, "wup", [128, 8, 5632], BF16)
        wdn = SB(es, "wdn", [128, 22, 1024], BF16)
        fcwt = SB(es, "fcwt", [128, 132], F32)
        fcbt = SB(es, "fcbt", [128, 44], F32)
        gfr = SB(es, "gfr", [128, 1024], F32)
        h2 = [SB(es, f"h2b{i}", [128, 8, QW], BF16) for i in range(2)]
        actT = SB(es, "actT", [128, 22, 256], BF16)
        cg = [SB(es, f"cg{i}", [128, 256], F32) for i in range(2)]
        cu = [SB(es, f"cu{i}", [128, 256], F32) for i in range(2)]
        ctmp = [SB(es, f"ctmp{i}", [128, 256], F32) for i in range(2)]
        sg = [SB(es, f"sg{i}", [128, 256], F32) for i in range(2)]
        uu = [SB(es, f"uu{i}", [128, QW], F32) for i in range(2)]
        xmt = [SB(es, f"xmt{i}", [128, 1024], F32) for i in range(2)]
        junkF = SB(es, "junkF", [128, 1024], BF16)
        stF = SB(es, "stF", [128, 3 * 16], F32)
        pUg = [PS(es, f"pUg{i}", [128, 512]) for i in range(2)]
        pUu = [PS(es, f"pUu{i}", [128, 512]) for i in range(2)]
        pDn = [PS(es, f"pDn{i}", [128, 512]) for i in range(2)]
        w_up_v = w_up.rearrange("(k p) c -> p k c", p=128)
        for cb in range(6):
            for base in (0, 2816):
                c0 = base + cb * 512
                c1 = min(base + (cb + 1) * 512, base + 2816)
                dma(wup[:, :, c0:c1], w_up_v[:, :, c0:c1], [], [("wup", base, cb)], eng="pool")
        dma(wdn[:], w_dn.rearrange("(f p) c -> p f c", p=128), [], ["wdn"], eng="pool")
        dma(fcwt[:], fcw, [], ["fcwt"])
        dma(fcbt[:], fcb, [], ["fcbt"])
        dma(gfr[:], gfin.partition_broadcast(128), [], ["gfr"])
        O("dve", lambda e: e.memset(stF[:], 0.0), writes=["stF"])
        col = 0
        for i in range(NB):
            q0 = i * QW
            hs = i % 2
            dma(h2[hs][:], h2T_s[:, :, q0:q0 + QW], [("h2T_s", i)], [f"h2b{hs}"])
            for fc in range(22):
                s = fc % 2
                for (ps_, cb, nm) in ((pUg, fc * 128, "pUg"), (pUu, 2816 + fc * 128, "pUu")):
                    for kc in range(8):
                        O("pe", lambda e, ps_=ps_, s=s, kc=kc, cb=cb, hs=hs: e.matmul(ps_[s][:, 0:QW], lhsT=wup[:, kc, cb:cb + 128], rhs=h2[hs][:, kc, :], start=(kc == 0), stop=(kc == 7)),
                          reads=[("wup", 0 if nm == "pUg" else 2816, fc // 4), f"h2b{hs}"], writes=[f"{nm}{s}"])
                wg = lambda k: fcwt[:, fc * 3 + k:fc * 3 + k + 1]
                wu = lambda k: fcwt[:, (22 + fc) * 3 + k:(22 + fc) * 3 + k + 1]
                O("act", lambda e, s=s, w=wg(0): e.activation(out=cg[s][:], in_=pUg[s][:, 0:256], func=AF.Copy, scale=w), reads=[f"pUg{s}", "fcwt"], writes=[f"cg{s}"])
                for k in (1, 2):
                    O("dve", lambda e, s=s, k=k, w=wg(k): e.scalar_tensor_tensor(out=cg[s][:], in0=pUg[s][:, k:k + 256], scalar=w, in1=cg[s][:], op0=ALU.mult, op1=ALU.add),
                      reads=[f"pUg{s}", "fcwt", f"cg{s}"], writes=[f"cg{s}"])
                O("act", lambda e, s=s, fc=fc: e.activation(out=sg[s][:], in_=cg[s][:], func=AF.Silu, bias=fcbt[:, fc:fc + 1]), reads=[f"cg{s}", "fcbt"], writes=[f"sg{s}"])
                O("act", lambda e, s=s, w=wu(0): e.activation(out=cu[s][:], in_=pUu[s][:, 0:256], func=AF.Copy, scale=w), reads=[f"pUu{s}", "fcwt"], writes=[f"cu{s}"])
                for k in (1, 2):
                    O("dve", lambda e, s=s, k=k, w=wu(k): e.scalar_tensor_tensor(out=cu[s][:], in0=pUu[s][:, k:k + 256], scalar=w, in1=cu[s][:], op0=ALU.mult, op1=ALU.add),
                      reads=[f"pUu{s}", "fcwt", f"cu{s}"], writes=[f"cu{s}"])
                O("dve", lambda e, s=s, fc=fc: e.scalar_tensor_tensor(out=actT[:, fc, :], in0=cu[s][:], scalar=fcbt[:, 22 + fc:23 + fc], in1=sg[s][:], op0=ALU.add, op1=ALU.mult),
                  reads=[f"cu{s}", f"sg{s}", "fcbt"], writes=["actT"])
            for sub in range(2):
                xsl = (i * 2 + sub) % 2
                dma(xmt[xsl][:], xmid_s[i, 2 + sub * 128:2 + (sub + 1) * 128, :], [("xmid_s", i)], [f"xmt{xsl}"])
                for half in range(2):
                    for fc in range(22):
                        O("pe", lambda e, half=half, fc=fc, sub=sub: e.matmul(pDn[half][:, :], lhsT=actT[:, fc, sub * 128:(sub + 1) * 128], rhs=wdn[:, fc, half * 512:(half + 1) * 512],
                                                                            start=(fc == 0), stop=(fc == 21)),
                          reads=["actT", "wdn"], writes=[f"pDn{half}"])
                    O("dve", lambda e, half=half, xsl=xsl: e.tensor_tensor(out=xmt[xsl][:, half * 512:(half + 1) * 512], in0=xmt[xsl][:, half * 512:(half + 1) * 512], in1=pDn[half][:, :], op=ALU.add),
                      reads=[f"xmt{xsl}", f"pDn{half}"], writes=[f"xmt{xsl}"])
                O("act", lambda e, xsl=xsl, col=col: e.activation(out=junkF[:], in_=xmt[xsl][:], func=AF.Square, accum_out=stF[:, col:col + 1]),
                  reads=[f"xmt{xsl}", "stF"], writes=["junkF", "stF"])
                rsqrt_cols(stF[:, col:col + 1], stF[:, 32 + col:33 + col], 1.0 / 1024, ["stF"], ["stF"], stF[:, 16 + col:17 + col], "stF")
                O("dve", lambda e, xsl=xsl, col=col: e.scalar_tensor_tensor(out=xmt[xsl][:], in0=xmt[xsl][:], scalar=stF[:, 32 + col:33 + col], in1=gfr[:], op0=ALU.mult, op1=ALU.mult),
                  reads=[f"xmt{xsl}", "stF", "gfr"], writes=[f"xmt{xsl}"])
                r0 = i * 256 + sub * 128
                dma(y[r0:r0 + 128, :], xmt[xsl][:], [f"xmt{xsl}"], ["y"])
                col += 1
        S.barrier()
        S.flush()


def _host_consts():
    bf = ml_dtypes.bfloat16
    c = {}
    c["c_identb"] = np.eye(128, dtype=np.float32).astype(bf)
    c["c_identf"] = np.eye(128, dtype=np.float32)
    sel = np.zeros((8, 8, 128), np.float32)
    for h in range(8):
        sel[h, h, :] = 1.0
    c["c_sel"] = sel.reshape(8, 1024)
    c["c_sel16"] = np.concatenate([sel, sel], axis=0).reshape(16, 1024).astype(bf)
    p = np.arange(128)
    invf = (10000.0 ** (-(np.arange(0, 64, 2, dtype=np.float32)) / 64.0)).astype(np.float32)
    rope = np.zeros((128, 4), np.float32)
    rope[:, 0] = (invf[p % 32].astype(np.float64) / (2 * np.pi)).astype(np.float32)
    sgn = np.where((p % 64) < 32, -1.0, 1.0)
    rope[:, 1] = (2 * np.pi * sgn).astype(np.float32)
    rope[:, 2] = np.float32(2 * np.pi)
    c["c_rope"] = rope
    col = np.arange(QW)
    am = np.zeros((128, 3, QW), np.float32)
    for r in range(3):
        am[:, r, :] = (p[:, None] + 128 * r <= 126 + col[None, :]).astype(np.float32)
    c["c_amask"] = am.reshape(128, 3 * QW).astype(bf)
    i = np.arange(128)
    c["c_dmU"] = np.where(i[None, :] >= p[:, None], 0.0, NEG).astype(np.float32)
    c["c_dmL"] = np.where(i[None, :] < p[:, None], 0.0, -NEG).astype(np.float32)
    return c


_NC_CACHE = {}


def _core_inputs(b, c, x, positions, shared):
    d = dict(shared)
    if c == 0:
        xs = np.concatenate([np.zeros((256, 1024), np.float32), x[b, :3840]], axis=0)
        ps = np.concatenate([np.zeros((256,), np.int32), positions[b, :3840]], axis=0)
        kb = np.zeros((128, 32), np.float32)
        kb[:, 0:2] = NEG
    else:
        xs = x[b]
        ps = positions[b]
        kb = np.zeros((128, 32), np.float32)
    d["xs"] = np.ascontiguousarray(xs)
    d["pos"] = np.ascontiguousarray(ps.reshape(1, T).astype(np.int32))
    d["kbias"] = kb
    return d


def _shared_inputs(norm_mix, w_in, lambda_q1, lambda_k1, lambda_q2, lambda_k2, a_subln, w_a_out, conv_qkv, a_log, dt_bias,
                   b_onorm, w_b_out, w_o, norm_ffn, w_up, ffn_conv, ffn_conv_bias, w_down, norm_final):
    f = lambda a: np.ascontiguousarray(np.asarray(a, dtype=np.float32))
    d = _host_consts()
    d["w_in"] = f(w_in[0]); d["w_a"] = f(w_a_out[0]); d["w_b"] = f(w_b_out[0]); d["w_o"] = f(w_o[0])
    d["w_up"] = f(w_up[0]); d["w_dn"] = f(w_down[0])
    d["gmix"] = f(np.asarray(norm_mix[0]).reshape(8, 128).T)
    d["gffn"] = f(np.asarray(norm_ffn[0]).reshape(8, 128).T)
    d["gfin"] = f(np.asarray(norm_final).reshape(1, 1024))
    d["lam4"] = f(np.concatenate([np.asarray(lambda_q1[0]), np.asarray(lambda_k1[0]), np.asarray(lambda_q2[0]), np.asarray(lambda_k2[0])]).reshape(1, 256))
    d["asub"] = f(np.asarray(a_subln[0]).reshape(128, 1))
    d["bon"] = f(np.asarray(b_onorm[0]).reshape(128, 1))
    d["cw"] = f(np.asarray(conv_qkv[0]).reshape(4, 24, 128).transpose(2, 1, 0).reshape(128, 96))
    d["alog"] = f(np.asarray(a_log[0]).reshape(8, 1))
    d["dtb"] = f(np.asarray(dt_bias[0]).reshape(8, 1))
    d["fcw"] = f(np.asarray(ffn_conv[0]).reshape(3, 44, 128).transpose(2, 1, 0).reshape(128, 132))
    d["fcb"] = f(np.asarray(ffn_conv_bias[0]).reshape(44, 128).T)
    return d


def kernel(x, positions, norm_mix, w_in, lambda_q1, lambda_k1, lambda_q2, lambda_k2, a_subln, w_a_out, conv_qkv, a_log,
           dt_bias, b_onorm, w_b_out, w_o, norm_ffn, w_up, ffn_conv, ffn_conv_bias, w_down, norm_final):
    x = np.asarray(x, dtype=np.float32)
    positions = np.asarray(positions, dtype=np.int32)
    shared = _shared_inputs(norm_mix, w_in, lambda_q1, lambda_k1, lambda_q2, lambda_k2, a_subln, w_a_out, conv_qkv, a_log,
                            dt_bias, b_onorm, w_b_out, w_o, norm_ffn, w_up, ffn_conv, ffn_conv_bias, w_down, norm_final)
    nc = build_nc()
    in_maps = []
    for core in range(8):
        b, c = core // 2, core % 2
        in_maps.append(_core_inputs(b, c, x, positions, shared))
    res = run_bass_kernel_spmd(nc, in_maps, core_ids=list(range(8)))
    out = np.zeros((4, 4096, 1024), np.float32)
    for core in range(8):
        b, c = core // 2, core % 2
        yc = np.asarray(res.results[core]["y"]).reshape(NB, 256, 1024)
        for i in range(NB):
            g = 2 * i + c
            out[b, g * 256:(g + 1) * 256] = yc[i]
    return out
```

```python
import bisect
from contextlib import ExitStack
import numpy as np
import ml_dtypes
import concourse.bass as bass
import concourse.mybir as mybir
from concourse.bass_utils import run_bass_kernel_spmd

F32 = mybir.dt.float32
BF16 = mybir.dt.bfloat16
I32 = mybir.dt.int32
ALU = mybir.AluOpType
AF = mybir.ActivationFunctionType
AX = mybir.AxisListType

ENGS = ("pe", "act", "dve", "pool", "sp")
BLK = {"pe": "tensor", "act": "scalar", "dve": "vector", "pool": "gpsimd", "sp": "sync"}

T = 4096
NB = 8
QW = 258
MYT = NB * QW
EPS = 1e-6
NEG = -30000.0
NSLOT = 24


class Op:
    __slots__ = ("eng", "fn", "stream", "pos", "deps", "flag", "val")

    def __init__(self, eng, fn, stream, pos):
        self.eng, self.fn, self.stream, self.pos = eng, fn, stream, pos
        self.deps = []
        self.flag = False
        self.val = 0


class Sched:
    def __init__(self, nc, sems):
        self.nc = nc
        self.sems = sems
        self.pending = {e: [] for e in ENGS}
        self.streams = {}
        self.cum = {}
        self.emitted = {}
        self.flagpos = {}
        self.lastw = {}
        self.readers = {}
        self.waited = {e: {} for e in ENGS}
        self.nops = 0
        self.dmacount = {}

    def _resolve(self, d):
        if d.pos < self.emitted.get(d.stream, 0) and not d.flag:
            fp = self.flagpos[d.stream]
            i = bisect.bisect_left(fp, d.pos)
            return self.streams[d.stream][fp[i]]
        return d

    def op(self, eng, fn, reads=(), writes=(), dma=False):
        if dma:
            k = self.dmacount.get(eng, 0)
            self.dmacount[eng] = k + 1
            st = "d_%s_%d" % (eng, k % NSLOT)
        else:
            st = "c_" + eng
        lst = self.streams.setdefault(st, [])
        o = Op(eng, fn, st, len(lst))
        deps = []
        if dma:
            o.flag = True
            if lst:
                deps.append(lst[-1])
        lst.append(o)
        self.nops += 1
        for b in reads:
            w = self.lastw.get(b)
            if w is not None:
                deps.append(w)
        for b in writes:
            w = self.lastw.get(b)
            if w is not None:
                deps.append(w)
            deps.extend(self.readers.get(b, ()))
        wt = self.waited[eng]
        need = {}
        for d in deps:
            if d is o:
                continue
            if d.stream == "c_" + eng and not dma and eng == "pe":
                continue
            if wt.get(d.stream, -1) >= d.pos:
                continue
            if d.stream not in need or need[d.stream].pos < d.pos:
                need[d.stream] = d
        for s, d in need.items():
            d = self._resolve(d)
            wt[s] = max(wt.get(s, -1), d.pos)
            d.flag = True
            o.deps.append(d)
        for b in reads:
            self.readers.setdefault(b, []).append(o)
        for b in writes:
            self.lastw[b] = o
            self.readers[b] = []
        self.pending[eng].append(o)
        return o

    def barrier(self):
        lasts = [lst[-1] for lst in self.streams.values() if lst]
        for e in ENGS:
            o = Op(e, None, None, -1)
            wt = self.waited[e]
            for d in lasts:
                if d.stream == "c_" + e and e == "pe":
                    continue
                if wt.get(d.stream, -1) >= d.pos:
                    continue
                d = self._resolve(d)
                wt[d.stream] = max(wt.get(d.stream, -1), d.pos)
                d.flag = True
                o.deps.append(d)
            self.pending[e].append(o)

    def flush(self):
        for st, lst in self.streams.items():
            inc = 16 if st.startswith("d_") else 1
            c = self.cum.get(st, 0)
            e0 = self.emitted.get(st, 0)
            if len(lst) > e0:
                lst[-1].flag = True
            fp = self.flagpos.setdefault(st, [])
            for o in lst[e0:]:
                if o.flag:
                    c += inc
                    o.val = c
                    fp.append(o.pos)
            self.cum[st] = c
        nc, sems = self.nc, self.sems
        with nc.Block() as block:
            for e in ENGS:
                ops = self.pending[e]
                if not ops:
                    continue

                def body(eng, ops=ops):
                    for o in ops:
                        for d in o.deps:
                            assert d.val > 0, (d.stream, d.pos)
                            eng.wait_ge(sems[d.stream], d.val)
                        if o.fn is None:
                            continue
                        ins = o.fn(eng)
                        if o.flag:
                            ins.then_inc(sems[o.stream], 16 if o.stream.startswith("d_") else 1)

                getattr(block, BLK[e])(body)
        for st, lst in self.streams.items():
            self.emitted[st] = len(lst)
        self.pending = {e: [] for e in ENGS}


def build_nc(stop_after=None, dbg=False):
    nc = bass.Bass("TRN2", target_bir_lowering=False)
    din = lambda n, shp, dt=F32: nc.dram_tensor(n, shp, dt, kind="ExternalInput").ap()
    xs = din("xs", [T, 1024])
    pos = din("pos", [1, T], I32)
    w_in = din("w_in", [1024, 9232])
    w_a = din("w_a", [1024, 1024])
    w_b = din("w_b", [1024, 1024])
    w_o = din("w_o", [1024, 1024])
    w_up = din("w_up", [1024, 5632])
    w_dn = din("w_dn", [2816, 1024])
    gmix = din("gmix", [128, 8])
    gffn = din("gffn", [128, 8])
    gfin = din("gfin", [1, 1024])
    lam4 = din("lam4", [1, 256])
    asub = din("asub", [128, 1])
    bon = din("bon", [128, 1])
    cw = din("cw", [128, 24 * 4])
    alog = din("alog", [8, 1])
    dtb = din("dtb", [8, 1])
    fcw = din("fcw", [128, 44 * 3])
    fcb = din("fcb", [128, 44])
    kbias = din("kbias", [128, 32])
    c_identb = din("c_identb", [128, 128], BF16)
    c_identf = din("c_identf", [128, 128])
    c_sel = din("c_sel", [8, 8 * 128])
    c_sel16 = din("c_sel16", [16, 8 * 128], BF16)
    c_rope = din("c_rope", [128, 4])
    c_amask = din("c_amask", [128, 3 * QW], BF16)
    c_dmU = din("c_dmU", [128, 128])
    c_dmL = din("c_dmL", [128, 128])
    y = nc.dram_tensor("y", [NB * 256, 1024], F32, kind="ExternalOutput").ap()
    dscr = lambda n, shp, dt: (nc.dram_tensor(n, shp, dt, kind="ExternalOutput") if dbg else nc.dram_tensor(n, shp, dt)).ap()
    kT_s = dscr("kT_s", [8, 128, T], BF16)
    v_s = dscr("v_s", [T, 1024], BF16)
    qT_s = dscr("qT_s", [8, 128, MYT], BF16)
    dq_s = dscr("dq_s", [8, 128, T], BF16)
    dk_s = dscr("dk_s", [8, 128, T], BF16)
    dktm_s = dscr("dktm_s", [8, T, 128], BF16)
    dvtm_s = dscr("dvtm_s", [8, T, 128], BF16)
    z_s = dscr("z_s", [8, 128, MYT], BF16)
    gate_s = dscr("gate_s", [16, 128, MYT], BF16)
    bg_s = dscr("bg_s", [2, 8, T], F32)
    bg16_s = dscr("bg16_s", [24, T], BF16)
    xmid_s = dscr("xmid_s", [NB, QW, 1024], F32)
    h2T_s = dscr("h2T_s", [128, 8, MYT], BF16)
    dbg_out = {}

    top = ExitStack()
    with top:
        sems = {}
        for st in ["c_pe", "c_act", "c_dve", "c_pool", "c_sp"] + ["d_%s_%d" % (e, k) for e in ("sp", "pool") for k in range(NSLOT)]:
            sems[st] = top.enter_context(nc.semaphore(st))
        S = Sched(nc, sems)
        O = S.op

        def SB(es, n, shp, dt):
            return es.enter_context(nc.sbuf_tensor(n, shp, dt))

        def PS(es, n, shp, dt=F32):
            return es.enter_context(nc.psum_tensor(n, shp, dt))

        def dma(out, in_, r, w, eng="sp", **kw):
            return O(eng, lambda e: e.dma_start(out=out, in_=in_, **kw), reads=r, writes=w, dma=True)

        identb = SB(top, "identb", [128, 128], BF16)
        identf = SB(top, "identf", [128, 128], F32)
        onesb = SB(top, "onesb", [128, 128], BF16)
        cst = SB(top, "cst", [128, 8], F32)
        dma(identb[:], c_identb, [], ["identb"])
        dma(identf[:], c_identf, [], ["identf"])
        O("pool", lambda e: e.memset(onesb[:], 1.0), writes=["onesb"])
        O("pool", lambda e: e.memset(cst[:, 0:1], EPS), writes=["cst"])
        O("pool", lambda e: e.memset(cst[:, 1:2], 1e-30), writes=["cst"])
        O("pool", lambda e: e.memset(cst[:, 2:3], 0.0), writes=["cst"])
        O("pool", lambda e: e.memset(cst[:, 3:4], 1.0), writes=["cst"])
        eps_ap = cst[:, 0:1]

        def rsqrt_cols(src_ap, dst_ap, scale, r, w, tmp_ap, tmpname):
            if dst_ap.shape[-1] > 8:
                O("act", lambda e: e.activation(out=tmp_ap, in_=src_ap, func=AF.Ln, scale=scale, bias=cst[:tmp_ap.shape[0], 0:1]),
                  reads=r + ["cst"], writes=[tmpname])
                O("act", lambda e: e.activation(out=dst_ap, in_=tmp_ap, func=AF.Exp, scale=-0.5), reads=[tmpname], writes=w)
                return
            O("act", lambda e: e.activation(out=tmp_ap, in_=src_ap, func=AF.Sqrt, scale=scale, bias=cst[:tmp_ap.shape[0], 0:1]),
              reads=r + ["cst"], writes=[tmpname])
            O("dve", lambda e: e.reciprocal(out=dst_ap, in_=tmp_ap), reads=[tmpname], writes=w)

        esAB = ExitStack()
        with esAB:
            hT = SB(esAB, "hT", [128, 8, T], BF16)
            esA = ExitStack()
            with esA:
                gm = SB(esA, "gm", [128, 8], F32)
                dma(gm[:], gmix, [], ["gm"])
                xt = [SB(esA, f"xt{i}", [128, 1024], F32) for i in range(2)]
                xn = [SB(esA, f"xn{i}", [128, 1024], BF16) for i in range(2)]
                junk = SB(esA, "junkA", [128, 1024], BF16)
                stA = SB(esA, "stA", [128, 3 * 32], F32)
                psT = [PS(esA, f"psT{i}", [128, 8, 128], BF16) for i in range(2)]
                O("dve", lambda e: e.memset(stA[:], 0.0), writes=["stA"])
                for tt in range(32):
                    s = tt % 2
                    dma(xt[s][:], xs[tt * 128:(tt + 1) * 128, :], [], [f"xt{s}"])
                    O("act", lambda e, s=s, tt=tt: e.activation(out=junk[:], in_=xt[s][:], func=AF.Square, accum_out=stA[:, tt:tt + 1]),
                      reads=[f"xt{s}", "stA"], writes=["junkA", "stA"])
                    rsqrt_cols(stA[:, tt:tt + 1], stA[:, 64 + tt:65 + tt], 1.0 / 1024, ["stA"], ["stA"], stA[:, 32 + tt:33 + tt], "stA")
                    O("dve", lambda e, s=s, tt=tt: e.tensor_scalar(out=xn[s][:], in0=xt[s][:], scalar1=stA[:, 64 + tt:65 + tt], scalar2=None, op0=ALU.mult),
                      reads=[f"xt{s}", "stA"], writes=[f"xn{s}"])
                    for kc in range(8):
                        O("pe", lambda e, s=s, kc=kc: e.transpose(psT[s][:, kc, :], xn[s][:, kc * 128:(kc + 1) * 128], identb[:]),
                          reads=[f"xn{s}", "identb"], writes=[f"psT{s}"])
                    O("dve", lambda e, s=s, tt=tt: e.tensor_tensor(out=hT[:, :, tt * 128:(tt + 1) * 128], in0=psT[s][:],
                                                                  in1=gm[:].unsqueeze(2).to_broadcast([128, 8, 128]), op=ALU.mult),
                      reads=[f"psT{s}", "gm"], writes=[("hT", tt // 4)])
                S.barrier()
                S.flush()
            esB = ExitStack()
            with esB:
                wc = [SB(esB, f"wc{i}", [128, 8, 128], BF16) for i in range(3)]
                wr = [SB(esB, f"wr{i}", [128, 8, 128], BF16) for i in range(2)]
                pA = [PS(esB, f"pA{i}", [128, 512]) for i in range(3)]
                pB = [PS(esB, f"pB{i}", [128, 512]) for i in range(2)]
                pTm = [PS(esB, f"pTm{i}", [128, 4, 128], BF16) for i in range(2)]
                w_in_v = w_in.rearrange("(kc p) c -> p kc c", p=128)
                cnt = {"wc": 0, "wr": 0, "pA": 0, "pB": 0, "pTm": 0}

                def load_wc(c0, ncols=128):
                    s = cnt["wc"] % 3
                    cnt["wc"] += 1
                    dma(wc[s][:, :, 0:ncols], w_in_v[:, :, c0:c0 + ncols], [], [f"wc{s}"], eng="pool")
                    return s

                def load_wr(c0):
                    s = cnt["wr"] % 2
                    cnt["wr"] += 1
                    src = w_in_v[:, :, c0:c0 + 128].rearrange("p k (m t j) -> p k m t j", m=2, t=2, j=32)
                    dst = wr[s][:].rearrange("p k (m t j) -> p k m t j", m=2, t=2, j=32)
                    for m in range(2):
                        dma(dst[:, :, m, 0, :], src[:, :, m, 1, :], [], [f"wr{s}"], eng="pool")
                        dma(dst[:, :, m, 1, :], src[:, :, m, 0, :], [], [f"wr{s}"], eng="pool")
                    return s

                def proj(wt, wname, t0, n, M=128, wcols=(0, 128)):
                    s = cnt["pA"] % 3
                    cnt["pA"] += 1
                    for kc in range(8):
                        O("pe", lambda e, s=s, kc=kc: e.matmul(pA[s][0:M, 0:n], lhsT=wt[:, kc, wcols[0]:wcols[1]], rhs=hT[:, kc, t0:t0 + n],
                                                             start=(kc == 0), stop=(kc == 7)),
                          reads=[wname, ("hT", t0 // 512)], writes=[f"pA{s}"])
                    return s

                def projB(wt, wname, t0, n):
                    s = cnt["pB"] % 2
                    cnt["pB"] += 1
                    for kc in range(8):
                        O("pe", lambda e, s=s, kc=kc: e.matmul(pB[s][:, 0:n], lhsT=wt[:, kc, :], rhs=hT[:, kc, t0:t0 + n],
                                                             start=(kc == 0), stop=(kc == 7)),
                          reads=[wname, ("hT", t0 // 512)], writes=[f"pB{s}"])
                    return s

                all_tiles = [(tt * 512, 512, tt * 512) for tt in range(8)]
                my_tiles = [(512 * i + 254, QW, i * QW) for i in range(NB)]

                esR = ExitStack()
                with esR:
                    cosT = SB(esR, "cosT", [128, T], F32)
                    sinT = SB(esR, "sinT", [128, T], F32)
                    esRt = ExitStack()
                    with esRt:
                        rc = SB(esRt, "rc", [128, 4], F32)
                        dma(rc[:], c_rope, [], ["rc"])
                        posi = SB(esRt, "posi", [128, T], I32)
                        posf = SB(esRt, "posf", [128, T], F32)
                        t1 = SB(esRt, "t1", [128, T], F32)
                        t2 = SB(esRt, "t2", [128, T], F32)
                        dma(posi[:], pos.partition_broadcast(128), [], ["posi"])
                        O("dve", lambda e: e.tensor_copy(out=posf[:], in_=posi[:]), reads=["posi"], writes=["posf"])
                        O("dve", lambda e: e.tensor_scalar(out=t1[:], in0=posf[:], scalar1=rc[:, 0:1], scalar2=None, op0=ALU.mult),
                          reads=["posf", "rc"], writes=["t1"])
                        for which, dst, dname in ((0, sinT, "sinT"), (1, cosT, "cosT")):
                            if which == 1:
                                O("dve", lambda e: e.tensor_scalar(out=t1[:], in0=t1[:], scalar1=0.25, scalar2=None, op0=ALU.add),
                                  reads=["t1"], writes=["t1"])
                            O("dve", lambda e: e.tensor_copy(out=posi[:], in_=t1[:]), reads=["t1"], writes=["posi"])
                            O("dve", lambda e: e.tensor_copy(out=posf[:], in_=posi[:]), reads=["posi"], writes=["posf"])
                            O("dve", lambda e: e.tensor_tensor(out=t2[:], in0=t1[:], in1=posf[:], op=ALU.subtract), reads=["t1", "posf"], writes=["t2"])
                            O("dve", lambda e: e.tensor_scalar(out=posf[:], in0=t2[:], scalar1=0.5, scalar2=-1.0, op0=ALU.is_gt, op1=ALU.mult),
                              reads=["t2"], writes=["posf"])
                            O("dve", lambda e: e.tensor_tensor(out=t2[:], in0=t2[:], in1=posf[:], op=ALU.add), reads=["t2", "posf"], writes=["t2"])
                            sc_ap = rc[:, 1:2] if which == 0 else rc[:, 2:3]
                            O("act", lambda e, dst=dst, sc_ap=sc_ap: e.activation(out=dst[:], in_=t2[:], func=AF.Sin, scale=sc_ap),
                              reads=["t2", "rc"], writes=[dname])

                        S.barrier()
                    rt1 = [SB(esR, f"rt1{i}", [128, 512], F32) for i in range(2)]
                    rt2 = [SB(esR, f"rt2{i}", [128, 512], F32) for i in range(2)]
                    ko = [SB(esR, f"ko{i}", [128, T], BF16) for i in range(2)]
                    it = 0
                    for kind, cbase, tiles, dst_s in (("k", 1024, all_tiles, kT_s), ("q", 0, my_tiles, qT_s)):
                        for h in range(8):
                            sw = load_wc(cbase + h * 128)
                            sr = load_wr(cbase + h * 128)
                            so = (it // 1) % 2
                            it += 1
                            for (t0, n, o0) in tiles:
                                a = proj(wc[sw], f"wc{sw}", t0, n)
                                b = projB(wr[sr], f"wr{sr}", t0, n)
                                u = cnt["pB"] % 2
                                O("dve", lambda e, a=a, u=u, t0=t0, n=n: e.tensor_tensor(out=rt1[u][:, 0:n], in0=pA[a][:, 0:n], in1=cosT[:, t0:t0 + n], op=ALU.mult),
                                  reads=[f"pA{a}", "cosT"], writes=[f"rt1{u}"])
                                O("dve", lambda e, b=b, u=u, t0=t0, n=n: e.tensor_tensor(out=rt2[u][:, 0:n], in0=pB[b][:, 0:n], in1=sinT[:, t0:t0 + n], op=ALU.mult),
                                  reads=[f"pB{b}", "sinT"], writes=[f"rt2{u}"])
                                O("dve", lambda e, u=u, so=so, n=n, o0=o0: e.tensor_tensor(out=ko[so][:, o0:o0 + n], in0=rt1[u][:, 0:n], in1=rt2[u][:, 0:n], op=ALU.add),
                                  reads=[f"rt1{u}", f"rt2{u}"], writes=[f"ko{so}"])
                            ntot = tiles[-1][2] + tiles[-1][1]
                            dma(dst_s[h, :, :], ko[so][:, 0:ntot], [f"ko{so}"], [(kind + "T_s", h)])
                S.barrier()
                esV = ExitStack()
                with esV:
                    wv = SB(esV, "wv", [128, 8, 1024], BF16)
                    vt = [SB(esV, f"vt{i}", [128, 1024], BF16) for i in range(2)]
                    dma(wv[:], w_in_v[:, :, 2048:3072], [], ["wv"], eng="pool")
                    for tt in range(32):
                        s2 = tt % 2
                        for half in range(2):
                            s = cnt["pA"] % 3
                            cnt["pA"] += 1
                            for kc in range(8):
                                O("pe", lambda e, s=s, kc=kc, tt=tt, half=half: e.matmul(pA[s][:, :], lhsT=hT[:, kc, tt * 128:(tt + 1) * 128],
                                                                                       rhs=wv[:, kc, half * 512:(half + 1) * 512], start=(kc == 0), stop=(kc == 7)),
                                  reads=["wv", ("hT", tt // 4)], writes=[f"pA{s}"])
                            O("act", lambda e, s=s, s2=s2, half=half: e.activation(out=vt[s2][:, half * 512:(half + 1) * 512], in_=pA[s][:, :], func=AF.Copy),
                              reads=[f"pA{s}"], writes=[f"vt{s2}"])
                        dma(v_s[tt * 128:(tt + 1) * 128, :], vt[s2][:], [f"vt{s2}"], ["v_s"])
                S.barrier()
                esZ = ExitStack()
                with esZ:
                    zo = [SB(esZ, f"zo{i}", [128, MYT], BF16) for i in range(2)]
                    for j in range(24):
                        if j < 8:
                            c0, fn, dst = 6144 + j * 128, AF.Silu, z_s[j, :, :]
                        else:
                            c0, fn, dst = 7184 + (j - 8) * 128, AF.Sigmoid, gate_s[j - 8, :, :]
                        sw = load_wc(c0)
                        so = j % 2
                        for (t0, n, o0) in my_tiles:
                            a = proj(wc[sw], f"wc{sw}", t0, n)
                            O("act", lambda e, a=a, so=so, n=n, o0=o0, fn=fn: e.activation(out=zo[so][:, o0:o0 + n], in_=pA[a][:, 0:n], func=fn),
                              reads=[f"pA{a}"], writes=[f"zo{so}"])
                        dma(dst, zo[so][:], [f"zo{so}"], [("zg_s", j)])
                S.barrier()
                esG = ExitStack()
                with esG:
                    bT = SB(esG, "bT", [8, T], F32)
                    gA = SB(esG, "gA", [8, T], F32)
                    gB = SB(esG, "gB", [8, T], F32)
                    sm = SB(esG, "sm", [8, 4], F32)
                    dma(sm[:, 0:1], alog, [], ["sm"])
                    dma(sm[:, 1:2], dtb, [], ["sm"])
                    O("act", lambda e: e.activation(out=sm[:, 2:3], in_=sm[:, 0:1], func=AF.Exp), reads=["sm"], writes=["sm"])
                    O("dve", lambda e: e.tensor_scalar(out=sm[:, 3:4], in0=sm[:, 2:3], scalar1=-1.0, scalar2=None, op0=ALU.mult), reads=["sm"], writes=["sm"])
                    sw = load_wc(7168, 16)
                    for (t0, n, o0) in all_tiles:
                        a = proj(wc[sw], f"wc{sw}", t0, n, M=8, wcols=(0, 8))
                        O("act", lambda e, a=a, t0=t0: e.activation(out=bT[:, t0:t0 + 512], in_=pA[a][0:8, :], func=AF.Sigmoid),
                          reads=[f"pA{a}"], writes=["bT"])
                        a = proj(wc[sw], f"wc{sw}", t0, n, M=8, wcols=(8, 16))
                        O("act", lambda e, a=a, t0=t0: e.activation(out=gA[:, t0:t0 + 512], in_=pA[a][0:8, :], func=AF.Exp, bias=sm[:, 1:2]),
                          reads=[f"pA{a}", "sm"], writes=["gA"])
                    O("act", lambda e: e.activation(out=gB[:], in_=gA[:], func=AF.Ln, bias=cst[0:8, 3:4]), reads=["gA", "cst"], writes=["gB"])
                    O("dve", lambda e: e.tensor_scalar(out=gA[:], in0=gB[:], scalar1=sm[:, 3:4], scalar2=None, op0=ALU.mult), reads=["gB", "sm"], writes=["gA"])
                    cur, oth, cn, on = gA, gB, "gA", "gB"
                    sh = 1
                    while sh < 128:
                        c3 = cur[:].rearrange("p (n j) -> p n j", j=128)
                        o3 = oth[:].rearrange("p (n j) -> p n j", j=128)
                        O("dve", lambda e, c3=c3, o3=o3, sh=sh: e.tensor_tensor(out=o3[:, :, sh:], in0=c3[:, :, sh:], in1=c3[:, :, :128 - sh], op=ALU.add),
                          reads=[cn], writes=[on])
                        O("pool", lambda e, c3=c3, o3=o3, sh=sh: e.tensor_copy(out=o3[:, :, :sh], in_=c3[:, :, :sh]), reads=[cn], writes=[on])
                        cur, oth, cn, on = oth, cur, on, cn
                        sh *= 2
                    dma(bg_s[0, :, :], bT[:], ["bT"], ["bg_s"])
                    dma(bg_s[1, :, :], cur[:], [cn], ["bg_s"])
                    h16 = SB(esG, "h16", [8, T], BF16)
                    l16 = SB(esG, "l16", [8, T], BF16)
                    b16 = SB(esG, "b16", [8, T], BF16)
                    O("dve", lambda e, cur=cur: e.tensor_copy(out=h16[:], in_=cur[:]), reads=[cn], writes=["h16"])
                    O("dve", lambda e, oth=oth: e.tensor_copy(out=oth[:], in_=h16[:]), reads=["h16"], writes=[on])
                    O("dve", lambda e, cur=cur, oth=oth: e.tensor_tensor(out=oth[:], in0=cur[:], in1=oth[:], op=ALU.subtract), reads=[cn, on], writes=[on])
                    O("dve", lambda e, oth=oth: e.tensor_copy(out=l16[:], in_=oth[:]), reads=[on], writes=["l16"])
                    O("dve", lambda e: e.tensor_copy(out=b16[:], in_=bT[:]), reads=["bT"], writes=["b16"])
                    dma(bg16_s[0:8, :], h16[:], ["h16"], ["bg16_s"])
                    dma(bg16_s[8:16, :], l16[:], ["l16"], ["bg16_s"])
                    dma(bg16_s[16:24, :], b16[:], ["b16"], ["bg16_s"])
                S.barrier()
                esD = ExitStack()
                with esD:
                    cwt = SB(esD, "cwt", [128, 96], F32)
                    dma(cwt[:], cw, [], ["cwt"])
                    pc = [SB(esD, f"pc{i}", [128, T + 3], F32) for i in range(2)]
                    cvs = [SB(esD, f"cv{i}", [128, T], F32) for i in range(2)]
                    sqt = [SB(esD, f"sqt{i}", [128, 512], BF16) for i in range(2)]
                    nbs = [SB(esD, f"nbD{i}", [128, T], BF16) for i in range(2)]
                    rs = [SB(esD, f"rsD{i}", [128, 512], F32) for i in range(2)]
                    sd = [SB(esD, f"sdD{i}", [128, 512], F32) for i in range(2)]
                    tm = SB(esD, "tmD", [128, 32, 128], BF16)
                    for i in range(2):
                        O("pool", lambda e, i=i: e.memset(pc[i][:, 0:3], 0.0), writes=[f"pc{i}"])

                    for j in range(24):
                        kind, h = j // 8, j % 8
                        sw = load_wc(3072 + j * 128)
                        sp_ = j % 2
                        cv, cvn = cvs[sp_], f"cv{sp_}"
                        nb, nbn = nbs[sp_], f"nbD{sp_}"
                        scl = (128.0 ** -0.5) if kind == 0 else 1.0
                        for ti, (t0, n, o0) in enumerate(all_tiles):
                            a = proj(wc[sw], f"wc{sw}", t0, n)
                            pcw = (f"pc{sp_}", ti)
                            pcr = [(f"pc{sp_}", ti), (f"pc{sp_}", ti - 1)] if ti > 0 else [(f"pc{sp_}", ti), f"pc{sp_}"]
                            cvt = (cvn, ti)
                            if ti % 2 == 0:
                                O("act", lambda e, a=a, sp_=sp_, t0=t0: e.activation(out=pc[sp_][:, 3 + t0:3 + t0 + 512], in_=pA[a][:, :], func=AF.Copy),
                                  reads=[f"pA{a}"], writes=[pcw])
                            else:
                                O("dve", lambda e, a=a, sp_=sp_, t0=t0: e.tensor_copy(out=pc[sp_][:, 3 + t0:3 + t0 + 512], in_=pA[a][:, :]),
                                  reads=[f"pA{a}"], writes=[pcw])
                            O("act", lambda e, cv=cv, sp_=sp_, j=j, t0=t0: e.activation(out=cv[:, t0:t0 + 512], in_=pc[sp_][:, 3 + t0:3 + t0 + 512], func=AF.Copy, scale=cwt[:, j * 4 + 3:j * 4 + 4]),
                              reads=pcr + ["cwt"], writes=[cvt])
                            for k in (2, 1, 0):
                                O("dve", lambda e, cv=cv, sp_=sp_, j=j, k=k, t0=t0: e.scalar_tensor_tensor(out=cv[:, t0:t0 + 512], in0=pc[sp_][:, k + t0:k + t0 + 512], scalar=cwt[:, j * 4 + k:j * 4 + k + 1],
                                                                                                 in1=cv[:, t0:t0 + 512], op0=ALU.mult, op1=ALU.add),
                                  reads=pcr + ["cwt", cvt], writes=[cvt])
                        for ti, (t0, n, o0) in enumerate(all_tiles):
                            cvt = (cvn, ti)
                            if kind < 2:
                                O("act", lambda e, cv=cv, t0=t0: e.activation(out=cv[:, t0:t0 + 512], in_=cv[:, t0:t0 + 512], func=AF.Silu), reads=[cvt], writes=[cvt])
                            else:
                                O("act", lambda e, cv=cv, nb=nb, t0=t0: e.activation(out=nb[:, t0:t0 + 512], in_=cv[:, t0:t0 + 512], func=AF.Silu), reads=[cvt], writes=[(nbn, ti)])
                        if kind < 2:
                            for ti, (t0, n, o0) in enumerate(all_tiles):
                                cvt = (cvn, ti)
                                sqs = cnt["pB"] % 2
                                O("pool", lambda e, cv=cv, t0=t0, sqs=sqs: e.tensor_tensor(out=sqt[sqs][:], in0=cv[:, t0:t0 + 512], in1=cv[:, t0:t0 + 512], op=ALU.mult), reads=[cvt], writes=[f"sqt{sqs}"])
                                s = cnt["pB"] % 2
                                cnt["pB"] += 1
                                O("pe", lambda e, s=s, sqs=sqs: e.matmul(pB[s][:, :], lhsT=onesb[:], rhs=sqt[sqs][:], start=True, stop=True),
                                  reads=["onesb", f"sqt{sqs}"], writes=[f"pB{s}"])
                                rsqrt_cols(pB[s][:, :], rs[s][:], 1.0, [f"pB{s}"], [f"rsD{s}"], sd[s][:], f"sdD{s}")
                                O("dve", lambda e, cv=cv, nb=nb, s=s, t0=t0, scl=scl: e.scalar_tensor_tensor(out=nb[:, t0:t0 + 512], in0=cv[:, t0:t0 + 512], scalar=scl, in1=rs[s][:],
                                                                                             op0=ALU.mult, op1=ALU.mult),
                                  reads=[cvt, f"rsD{s}"], writes=[(nbn, ti)])
                        nball = [(nbn, ti) for ti in range(8)]
                        if kind < 2:
                            dma((dq_s if kind == 0 else dk_s)[h, :, :], nb[:], nball, [("dqk_s", kind, h)])
                        if kind >= 1:
                            for g in range(8):
                                s = cnt["pTm"] % 2
                                cnt["pTm"] += 1
                                for q in range(4):
                                    tt = g * 4 + q
                                    O("pe", lambda e, cv=cv, nb=nb, s=s, q=q, tt=tt: e.transpose(pTm[s][:, q, :], nb[:, tt * 128:(tt + 1) * 128], identb[:]),
                                      reads=[(nbn, tt // 4), "identb"], writes=[f"pTm{s}"])
                                eng = "act" if g % 2 == 0 else "dve"
                                if eng == "act":
                                    O("act", lambda e, cv=cv, nb=nb, s=s, g=g: e.activation(out=tm[:, g * 4:(g + 1) * 4, :], in_=pTm[s][:], func=AF.Copy), reads=[f"pTm{s}"], writes=["tmD"])
                                else:
                                    O("dve", lambda e, cv=cv, nb=nb, s=s, g=g: e.tensor_copy(out=tm[:, g * 4:(g + 1) * 4, :], in_=pTm[s][:]), reads=[f"pTm{s}"], writes=["tmD"])
                            dst = (dktm_s if kind == 1 else dvtm_s)[h].rearrange("(n p) d -> p n d", p=128)
                            dma(dst, tm[:], ["tmD"], [("dtm_s", kind, h)])
                S.barrier()
                S.flush()
        if stop_after == "B":
            return finish(nc, S, top, y, dbg_out)

        esCE = ExitStack()
        with esCE:
            oaT = SB(esCE, "oaT", [128, 8, MYT], BF16)
            obT = SB(esCE, "obT", [128, 8, MYT], BF16)
            esC = ExitStack()
            with esC:
                kT = [SB(esC, f"kT{i}", [128, T], BF16) for i in range(2)]
                vh = [SB(esC, f"vh{i}", [128, 32, 128], BF16) for i in range(2)]
                qT = [SB(esC, f"qT{i}", [128, MYT], BF16) for i in range(2)]
                kb_t = SB(esC, "kb_t", [128, 32], F32)
                am = SB(esC, "am", [128, 3 * QW], BF16)
                pP = [SB(esC, f"pP{i}", [128, 2, QW], BF16) for i in range(3)]
                lamt = SB(esC, "lamt", [128, 256], F32)
                lt = SB(esC, "lt", [128, 128], F32)
                ls = SB(esC, "ls", [128, 8], F32)
                asb = SB(esC, "asb", [128, 2], F32)
                rr = [SB(esC, f"rr{m}", [128, QW], F32) for m in range(2)]
                tt_ = [SB(esC, f"ttC{m}", [128, QW], F32) for m in range(2)]
                osb = SB(esC, "osb", [128, QW], F32)
                sqc = SB(esC, "sqc", [128, QW], BF16)
                sdc = SB(esC, "sdc", [128, QW], F32)
                rsc = SB(esC, "rsc", [128, QW], F32)
                pS = [PS(esC, f"pS{i}", [128, 2, 512]) for i in range(2)]
                pO = [PS(esC, f"pO{m}", [128, 512]) for m in range(2)]
                pLt = PS(esC, "pLt", [128, 2, 512])
                pL = [pLt[:, m, :] for m in range(2)]
                rrt = SB(esC, "rrt", [128, 2, QW], F32)
                dma(kb_t[:], kbias, [], ["kb_t"])
                dma(am[:], c_amask, [], ["am"])
                dma(lamt[:], lam4.partition_broadcast(128), [], ["lamt"])
                dma(asb[:, 0:1], asub, [], ["asb"])
                O("dve", lambda e: e.tensor_scalar(out=asb[:, 1:2], in0=asb[:, 0:1], scalar1=0.8, scalar2=None, op0=ALU.mult), reads=["asb"], writes=["asb"])
                O("dve", lambda e: e.tensor_tensor(out=lt[:, 0:64], in0=lamt[:, 0:64], in1=lamt[:, 64:128], op=ALU.mult), reads=["lamt"], writes=["lt"])
                O("dve", lambda e: e.tensor_tensor(out=lt[:, 64:128], in0=lamt[:, 128:192], in1=lamt[:, 192:256], op=ALU.mult), reads=["lamt"], writes=["lt"])
                O("dve", lambda e: e.reduce_sum(out=ls[:, 0:2], in_=lt[:].rearrange("p (a b) -> p a b", a=2), axis=AX.X), reads=["lt"], writes=["ls"])
                O("act", lambda e: e.activation(out=ls[:, 2:4], in_=ls[:, 0:2], func=AF.Exp), reads=["ls"], writes=["ls"])
                O("dve", lambda e: e.tensor_tensor(out=ls[:, 4:5], in0=ls[:, 3:4], in1=ls[:, 2:3], op=ALU.subtract), reads=["ls"], writes=["ls"])
                O("dve", lambda e: e.tensor_scalar(out=ls[:, 5:6], in0=ls[:, 4:5], scalar1=-0.2, scalar2=None, op0=ALU.add), reads=["ls"], writes=["ls"])
                nlam = ls[:, 5:6]
                pcount = 0
                for h in range(8):
                    hs = h % 2
                    dma(kT[hs][:], kT_s[h, :, :], [("kT_s", h)], [f"kT{hs}"])
                    dma(vh[hs][:], v_s[:, h * 128:(h + 1) * 128].rearrange("(n p) d -> p n d", p=128), ["v_s"], [f"vh{hs}"])
                    dma(qT[hs][:], qT_s[h, :, :], [("qT_s", h)], [f"qT{hs}"])
                    for i in range(NB):
                        q0 = i * QW
                        nkb = 4 * i + 4
                        def emit_S(kb, hs=hs, q0=q0):
                            ss = kb % 2
                            for m in range(2):
                                O("pe", lambda e, ss=ss, m=m, hs=hs, kb=kb, q0=q0: e.matmul(pS[ss][:, m, 0:QW], lhsT=kT[hs][m * 64:(m + 1) * 64, kb * 128:(kb + 1) * 128],
                                                                                           rhs=qT[hs][m * 64:(m + 1) * 64, q0:q0 + QW], start=True, stop=True),
                                  reads=[f"kT{hs}", f"qT{hs}"], writes=[f"pS{ss}"])
                        emit_S(0)
                        for kb in range(nkb):
                            ss = kb % 2
                            pp = pcount % 3
                            pcount += 1
                            if kb + 1 < nkb:
                                emit_S(kb + 1)
                            O("act", lambda e, ss=ss, pp=pp, kb=kb: e.activation(out=pP[pp][:], in_=pS[ss][:, :, 0:QW], func=AF.Exp, scale=0.125, bias=kb_t[:, kb:kb + 1]),
                              reads=[f"pS{ss}", "kb_t"], writes=[f"pP{pp}"])
                            r = kb - (4 * i + 1)
                            if r >= 0:
                                O("dve", lambda e, pp=pp, r=r: e.tensor_tensor(out=pP[pp][:], in0=pP[pp][:], in1=am[:, r * QW:(r + 1) * QW].unsqueeze(1).to_broadcast([128, 2, QW]), op=ALU.mult),
                                  reads=[f"pP{pp}", "am"], writes=[f"pP{pp}"])
                            for m in range(2):
                                O("pe", lambda e, m=m, pp=pp, hs=hs, kb=kb, nkb=nkb: e.matmul(pO[m][:, 0:QW], lhsT=vh[hs][:, kb, :], rhs=pP[pp][:, m, :], start=(kb == 0), stop=(kb == nkb - 1)),
                                  reads=[f"vh{hs}", f"pP{pp}"], writes=[f"pO{m}"])
                                O("pe", lambda e, m=m, pp=pp, kb=kb, nkb=nkb: e.matmul(pL[m][:, 0:QW], lhsT=onesb[:], rhs=pP[pp][:, m, :], start=(kb == 0), stop=(kb == nkb - 1)),
                                  reads=["onesb", f"pP{pp}"], writes=[f"pL{m}"])
                        O("act", lambda e: e.activation(out=rrt[:], in_=pLt[:, :, 0:QW], func=AF.Ln, bias=cst[:, 1:2]), reads=["pL0", "pL1", "cst"], writes=["rrt"])
                        O("act", lambda e: e.activation(out=rrt[:], in_=rrt[:], func=AF.Exp, scale=-1.0), reads=["rrt"], writes=["rrt"])
                        for m in range(2):
                            O("dve", lambda e, m=m: e.tensor_tensor(out=tt_[m][:], in0=pO[m][:, 0:QW], in1=rrt[:, m, :], op=ALU.mult), reads=[f"pO{m}", "rrt"], writes=[f"ttC{m}"])
                        O("dve", lambda e: e.scalar_tensor_tensor(out=osb[:], in0=tt_[1][:], scalar=nlam, in1=tt_[0][:], op0=ALU.mult, op1=ALU.add),
                          reads=["ttC0", "ttC1", "ls"], writes=["osb"])
                        O("act", lambda e: e.activation(out=sqc[:], in_=osb[:], func=AF.Square), reads=["osb"], writes=["sqc"])
                        O("pe", lambda e: e.matmul(pS[0][:, 0, 0:QW], lhsT=onesb[:], rhs=sqc[:], start=True, stop=True), reads=["onesb", "sqc"], writes=["pS0"])
                        rsqrt_cols(pS[0][:, 0, 0:QW], rsc[:], 1.0 / 128, ["pS0"], ["rsc"], sdc[:], "sdc")
                        O("dve", lambda e, h=h, q0=q0: e.scalar_tensor_tensor(out=oaT[:, h, q0:q0 + QW], in0=osb[:], scalar=asb[:, 1:2], in1=rsc[:], op0=ALU.mult, op1=ALU.mult),
                          reads=["osb", "asb", "rsc"], writes=[("oaT", i)])
                S.barrier()
                S.flush()
            if stop_after == "C":
                if dbg:
                    dbg_out["oaT"] = nc.dram_tensor("dbg_oaT", [128, 8 * MYT], BF16, kind="ExternalOutput").ap()
                    dma(dbg_out["oaT"], oaT[:].rearrange("p h t -> p (h t)"), [("oaT", i) for i in range(NB)], ["dbg"])
                return finish(nc, S, top, y, dbg_out)
            build_delta(nc, S, O, SB, PS, dma, cst, identb, identf, onesb, rsqrt_cols, obT,
                        dq_s, dk_s, dktm_s, dvtm_s, z_s, bg_s, bon, c_sel, c_dmU, c_dmL, bg16_s, c_sel16)
            if stop_after == "D":
                if dbg:
                    dbg_out["obT"] = nc.dram_tensor("dbg_obT", [128, 8 * MYT], BF16, kind="ExternalOutput").ap()
                    dma(dbg_out["obT"], obT[:].rearrange("p h t -> p (h t)"), [("obT", i) for i in range(NB)], ["dbg"])
                return finish(nc, S, top, y, dbg_out)
            esE = ExitStack()
            with esE:
                h2T = SB(esE, "h2T", [128, 8, MYT], BF16)
                wa = SB(esE, "wa", [128, 8, 1024], BF16)
                wb_ = SB(esE, "wb_", [128, 8, 1024], BF16)
                wo = SB(esE, "wo", [128, 8, 1024], BF16)
                gf = SB(esE, "gf", [128, 8], F32)
                for cb in range(4):
                    dma(wa[:, :, cb * 256:(cb + 1) * 256], w_a.rearrange("(k p) c -> p k c", p=128)[:, :, cb * 256:(cb + 1) * 256], [], [("wa", cb)], eng="pool")
                    dma(wb_[:, :, cb * 256:(cb + 1) * 256], w_b.rearrange("(k p) c -> p k c", p=128)[:, :, cb * 256:(cb + 1) * 256], [], [("wb_", cb)], eng="pool")
                dma(wo[:], w_o.rearrange("(k p) c -> p k c", p=128), [], ["wo"], eng="pool")
                dma(gf[:], gffn, [], ["gf"])
                gab = [SB(esE, f"gab{i}", [128, 2, QW], BF16) for i in range(2)]
                me1 = [SB(esE, f"me1{i}", [128, QW], F32) for i in range(2)]
                me2 = [SB(esE, f"me2{i}", [128, QW], F32) for i in range(2)]
                mT = SB(esE, "mT", [128, 8, QW], BF16)
                xin = [SB(esE, f"xin{i}", [128, 1024], F32) for i in range(2)]
                xm = [SB(esE, f"xm{i}", [128, 1024], F32) for i in range(2)]
                xn2 = [SB(esE, f"xn2{i}", [128, 1024], BF16) for i in range(2)]
                junkE = SB(esE, "junkE", [128, 1024], BF16)
                stE = SB(esE, "stE", [128, 3 * 32], F32)
                pY = [[PS(esE, f"pY{i}{m}", [128, 512]) for m in range(2)] for i in range(2)]
                pD = [PS(esE, f"pD{m}", [128, 512]) for m in range(2)]
                pT2 = [PS(esE, f"pT2{i}", [128, 8, 128], BF16) for i in range(2)]
                O("dve", lambda e: e.memset(stE[:], 0.0), writes=["stE"])
                n_sub = 0
                for i in range(NB):
                    q0 = i * QW
                    for c in range(8):
                        sy = c % 2
                        sg = c % 2
                        dma(gab[sg][:, 0, :], gate_s[c, :, q0:q0 + QW], [("zg_s", 8 + c)], [f"gab{sg}"])
                        dma(gab[sg][:, 1, :], gate_s[8 + c, :, q0:q0 + QW], [("zg_s", 16 + c)], [f"gab{sg}"])
                        for (m, wt, wn, src, sn) in ((0, wa, "wa", oaT, "oaT"), (1, wb_, "wb_", obT, "obT")):
                            for hh in range(8):
                                O("pe", lambda e, sy=sy, m=m, wt=wt, src=src, hh=hh, c=c, q0=q0: e.matmul(pY[sy][m][:, 0:QW], lhsT=wt[:, hh, c * 128:(c + 1) * 128],
                                                                                                       rhs=src[:, hh, q0:q0 + QW], start=(hh == 0), stop=(hh == 7)),
                                  reads=[(wn, c // 2), (sn, i)], writes=[f"pY{sy}{m}"])
                        O("dve", lambda e, sy=sy, sg=sg: e.tensor_tensor(out=me1[sy][:], in0=pY[sy][0][:, 0:QW], in1=gab[sg][:, 0, :], op=ALU.mult),
                          reads=[f"pY{sy}0", f"gab{sg}"], writes=[f"me1{sy}"])
                        O("dve", lambda e, sy=sy, sg=sg: e.tensor_tensor(out=me2[sy][:], in0=pY[sy][1][:, 0:QW], in1=gab[sg][:, 1, :], op=ALU.mult),
                          reads=[f"pY{sy}1", f"gab{sg}"], writes=[f"me2{sy}"])
                        O("pool", lambda e, sy=sy, c=c: e.tensor_tensor(out=mT[:, c, :], in0=me1[sy][:], in1=me2[sy][:], op=ALU.add),
                          reads=[f"me1{sy}", f"me2{sy}"], writes=["mT"])
                    for (a0, M) in ((0, 2), (2, 128), (130, 128)):
                        sx = n_sub % 2
                        col = n_sub
                        n_sub += 1
                        tok0 = 512 * i + 254 + a0
                        dma(xin[sx][0:M, :], xs[tok0:tok0 + M, :], [], [f"xin{sx}"])
                        for half in range(2):
                            for c in range(8):
                                O("pe", lambda e, half=half, c=c, a0=a0, M=M: e.matmul(pD[half][0:M, :], lhsT=mT[:, c, a0:a0 + M], rhs=wo[:, c, half * 512:(half + 1) * 512],
                                                                                      start=(c == 0), stop=(c == 7)),
                                  reads=["mT", "wo"], writes=[f"pD{half}"])
                            O("dve", lambda e, half=half, sx=sx, M=M: e.tensor_tensor(out=xm[sx][0:M, half * 512:(half + 1) * 512], in0=pD[half][0:M, :],
                                                                                     in1=xin[sx][0:M, half * 512:(half + 1) * 512], op=ALU.add),
                              reads=[f"pD{half}", f"xin{sx}"], writes=[f"xm{sx}"])
                        dma(xmid_s[i, a0:a0 + M, :], xm[sx][0:M, :], [f"xm{sx}"], [("xmid_s", i)])
                        O("act", lambda e, sx=sx, M=M, col=col: e.activation(out=junkE[0:M, :], in_=xm[sx][0:M, :], func=AF.Square, accum_out=stE[0:M, col:col + 1]),
                          reads=[f"xm{sx}", "stE"], writes=["junkE", "stE"])
                        rsqrt_cols(stE[0:M, col:col + 1], stE[0:M, 64 + col:65 + col], 1.0 / 1024, ["stE"], ["stE"], stE[0:M, 32 + col:33 + col], "stE")
                        O("dve", lambda e, sx=sx, M=M, col=col: e.tensor_scalar(out=xn2[sx][0:M, :], in0=xm[sx][0:M, :], scalar1=stE[0:M, 64 + col:65 + col], scalar2=None, op0=ALU.mult),
                          reads=[f"xm{sx}", "stE"], writes=[f"xn2{sx}"])
                        for kc in range(8):
                            O("pe", lambda e, sx=sx, kc=kc, M=M: e.transpose(pT2[sx][:, kc, 0:M], xn2[sx][0:M, kc * 128:(kc + 1) * 128], identb[0:M, 0:M]),
                              reads=[f"xn2{sx}", "identb"], writes=[f"pT2{sx}"])
                        O("dve", lambda e, sx=sx, M=M, q0=q0, a0=a0: e.tensor_tensor(out=h2T[:, :, q0 + a0:q0 + a0 + M], in0=pT2[sx][:, :, 0:M],
                                                                                    in1=gf[:].unsqueeze(2).to_broadcast([128, 8, M]), op=ALU.mult),
                          reads=[f"pT2{sx}", "gf"], writes=[("h2T", i)])
                    dma(h2T_s[:, :, q0:q0 + QW], h2T[:, :, q0:q0 + QW], [("h2T", i)], [("h2T_s", i)])
                S.barrier()
                S.flush()
        if stop_after == "E":
            return finish(nc, S, top, y, dbg_out)
        build_ffn(nc, S, O, SB, PS, dma, cst, rsqrt_cols, h2T_s, xmid_s, w_up, w_dn, fcw, fcb, gfin, y)
        return finish(nc, S, top, y, dbg_out)


def finish(nc, S, top, y, dbg_out):
    S.barrier()
    S.flush()
    return nc


def build_delta(nc, S, O, SB, PS, dma, cst, identb, identf, onesb, rsqrt_cols, obT,
                dq_s, dk_s, dktm_s, dvtm_s, z_s, bg_s, bon, c_sel, c_dmU, c_dmL, bg16_s, c_sel16):
    es = ExitStack()
    with es:
        sel = SB(es, "sel", [8, 8 * 128], F32)
        dmU = SB(es, "dmU", [128, 128], F32)
        dmL = SB(es, "dmL", [128, 128], F32)
        bont = SB(es, "bont", [128, 1], F32)
        gct = [SB(es, f"gct{i}", [8, 512], F32) for i in range(2)]
        bet = [SB(es, f"bet{i}", [8, 512], F32) for i in range(2)]
        glast = SB(es, "glast", [8, 32], F32)
        gctm = SB(es, "gctm", [128, 32, 8], F32)
        betm = SB(es, "betm", [128, 32, 8], F32)
        gch = SB(es, "gch", [128, 32], F32)
        beh = SB(es, "beh", [128, 32], F32)
        cb1 = SB(es, "cb1", [128, 32], F32)
        cb2 = SB(es, "cb2", [128, 32], F32)
        egl = SB(es, "egl", [128, 32], F32)
        qTd = SB(es, "qTd", [128, T], BF16)
        kTd = SB(es, "kTd", [128, T], BF16)
        ktm = SB(es, "ktm", [128, 32, 128], BF16)
        vtm = SB(es, "vtm", [128, 32, 128], BF16)
        qgT = SB(es, "qgT", [128, T], BF16)
        kbT = SB(es, "kbT", [128, T], BF16)
        vb = SB(es, "vb", [128, 32, 128], BF16)
        kbg = SB(es, "kbg", [128, 32, 128], BF16)
        kg = SB(es, "kg", [128, 32, 128], BF16)
        oTm = SB(es, "oTm", [128, NB, QW], F32)
        eg = SB(es, "eg", [128, 512], F32)
        e1 = SB(es, "e1", [128, 4, 128], F32)
        e2 = SB(es, "e2", [128, 4, 128], F32)
        DmI = SB(es, "DmI", [128, 4, 128], F32)
        DmS = SB(es, "DmS", [128, 4, 128], F32)
        Dp = SB(es, "Dp", [128, 4, 128], F32)
        qkT = [SB(es, f"qkT{i}", [128, 4, 128], BF16) for i in range(2)]
        Bb = [SB(es, f"Bb{i}", [128, 4, 128], BF16) for i in range(2)]
        BTb = [SB(es, f"BTb{i}", [128, 4, 128], BF16) for i in range(2)]
        Xb = [SB(es, f"Xb{i}", [128, 4, 128], BF16) for i in range(2)]
        u_sb = [SB(es, f"u_sb{i}", [128, 4, 128], F32) for i in range(2)]
        wT_sb = [SB(es, f"wT_sb{i}", [128, 4, 128], BF16) for i in range(2)]
        S_f = SB(es, "S_f", [128, 128], F32)
        S_b = SB(es, "S_b", [128, 128], BF16)
        vnew = [SB(es, f"vnew{i}", [128, 128], BF16) for i in range(2)]
        sqd = SB(es, "sqd", [128, QW], BF16)
        sdd = SB(es, "sdd", [128, QW], F32)
        rsd = SB(es, "rsd", [128, QW], F32)
        ztl = [SB(es, f"ztl{i}", [128, QW], BF16) for i in range(2)]
        tno = SB(es, "tno", [128, QW], F32)
        pGR = PS(es, "pGR", [128, 512])
        pBR = PS(es, "pBR", [128, 512])
        pG1 = PS(es, "pG1", [128, 4, 128])
        pG2 = PS(es, "pG2", [128, 4, 128])
        pG3 = PS(es, "pG3", [128, 4, 128])
        pWS = PS(es, "pWS", [128, 512])
        pOT = PS(es, "pOT", [128, 512])
        pSU = PS(es, "pSU", [128, 512])
        dma(sel[:], c_sel, [], ["sel"])
        sel16 = SB(es, "sel16", [16, 8 * 128], BF16)
        dma(sel16[:], c_sel16, [], ["sel16"])
        g16 = [SB(es, f"g16{i}", [16, 512], BF16) for i in range(2)]
        be16 = [SB(es, f"be16{i}", [8, 512], BF16) for i in range(2)]
        dma(dmU[:], c_dmU, [], ["dmU"])
        dma(dmL[:], c_dmL, [], ["dmL"])
        dma(bont[:], bon, [], ["bont"])
        dma(glast[:], bg_s[1].rearrange("h (n j) -> h n j", j=128)[:, :, 127], ["bg_s"], ["glast"], allow_slow_non_contiguous=True)
        for g in range(8):
            s = g % 2
            dma(gct[s][:], bg_s[1, :, g * 512:(g + 1) * 512], ["bg_s"], [f"gct{s}"])
            dma(bet[s][:], bg_s[0, :, g * 512:(g + 1) * 512], ["bg_s"], [f"bet{s}"])
            for q in range(4):
                n = g * 4 + q
                O("pe", lambda e, s=s, q=q, n=n: e.matmul(pGR[:, n * 8:(n + 1) * 8], lhsT=gct[s][:, q * 128:(q + 1) * 128], rhs=identf[0:8, 0:8], start=True, stop=True),
                  reads=[f"gct{s}", "identf"], writes=["pGR"])
                O("pe", lambda e, s=s, q=q, n=n: e.matmul(pBR[:, n * 8:(n + 1) * 8], lhsT=bet[s][:, q * 128:(q + 1) * 128], rhs=identf[0:8, 0:8], start=True, stop=True),
                  reads=[f"bet{s}", "identf"], writes=["pBR"])
        O("dve", lambda e: e.tensor_copy(out=gctm[:].rearrange("p n h -> p (n h)"), in_=pGR[:, 0:256]), reads=["pGR"], writes=["gctm"])
        O("dve", lambda e: e.tensor_copy(out=betm[:].rearrange("p n h -> p (n h)"), in_=pBR[:, 0:256]), reads=["pBR"], writes=["betm"])
        for h in range(8):
            selh = sel[:, h * 128:(h + 1) * 128]
            dma(qTd[:], dq_s[h, :, :], [("dqk_s", 0, h)], ["qTd"])
            dma(kTd[:], dk_s[h, :, :], [("dqk_s", 1, h)], ["kTd"])
            dma(ktm[:], dktm_s[h].rearrange("(n p) d -> p n d", p=128), [("dtm_s", 1, h)], ["ktm"])
            dma(vtm[:], dvtm_s[h].rearrange("(n p) d -> p n d", p=128), [("dtm_s", 2, h)], ["vtm"])
            O("dve", lambda e, h=h: e.tensor_copy(out=gch[:], in_=gctm[:, :, h]), reads=["gctm"], writes=["gch"])
            O("dve", lambda e, h=h: e.tensor_copy(out=beh[:], in_=betm[:, :, h]), reads=["betm"], writes=["beh"])
            O("pe", lambda e, selh=selh: e.matmul(pWS[:, 0:32], lhsT=selh, rhs=glast[:], start=True, stop=True), reads=["sel", "glast"], writes=["pWS"])
            O("act", lambda e: e.activation(out=egl[:], in_=pWS[:, 0:32], func=AF.Exp), reads=["pWS"], writes=["egl"])
            O("dve", lambda e: e.tensor_tensor(out=cb2[:], in0=pWS[:, 0:32], in1=gch[:], op=ALU.subtract), reads=["pWS", "gch"], writes=["cb2"])
            O("act", lambda e: e.activation(out=cb2[:], in_=cb2[:], func=AF.Exp), reads=["cb2"], writes=["cb2"])
            O("act", lambda e: e.activation(out=cb1[:], in_=gch[:], func=AF.Exp), reads=["gch"], writes=["cb1"])
            O("dve", lambda e: e.tensor_tensor(out=cb1[:], in0=cb1[:], in1=beh[:], op=ALU.mult), reads=["cb1", "beh"], writes=["cb1"])
            O("dve", lambda e: e.tensor_tensor(out=vb[:], in0=vtm[:], in1=beh[:].unsqueeze(2).to_broadcast([128, 32, 128]), op=ALU.mult), reads=["vtm", "beh"], writes=["vb"])
            O("dve", lambda e: e.tensor_tensor(out=kbg[:], in0=ktm[:], in1=cb1[:].unsqueeze(2).to_broadcast([128, 32, 128]), op=ALU.mult), reads=["ktm", "cb1"], writes=["kbg"])
            O("dve", lambda e: e.tensor_tensor(out=kg[:], in0=ktm[:], in1=cb2[:].unsqueeze(2).to_broadcast([128, 32, 128]), op=ALU.mult), reads=["ktm", "cb2"], writes=["kg"])
            O("dve", lambda e: e.memset(S_f[:], 0.0), writes=["S_f"])
            O("pool", lambda e: e.memset(S_b[:], 0.0), writes=["S_b"])
            def gen_inv(g, h=h, selh=selh):
                s = g % 2
                gs = g % 2
                t0 = g * 512
                dma(g16[s][:], bg16_s[0:16, t0:t0 + 512], ["bg16_s"], [f"g16{s}"])
                dma(be16[s][:], bg16_s[16:24, t0:t0 + 512], ["bg16_s"], [f"be16{s}"])
                O("pe", lambda e, s=s, h=h: e.matmul(pGR[:, :], lhsT=sel16[:, h * 128:(h + 1) * 128], rhs=g16[s][:], start=True, stop=True), reads=["sel16", f"g16{s}"], writes=["pGR"])
                O("pe", lambda e, s=s, h=h: e.matmul(pBR[:, :], lhsT=sel16[0:8, h * 128:(h + 1) * 128], rhs=be16[s][:], start=True, stop=True), reads=["sel16", f"be16{s}"], writes=["pBR"])
                yield
                O("act", lambda e: e.activation(out=eg[:], in_=pGR[:, :], func=AF.Exp), reads=["pGR"], writes=["eg"])
                O("dve", lambda e, t0=t0: e.tensor_tensor(out=kbT[:, t0:t0 + 512], in0=kTd[:, t0:t0 + 512], in1=pBR[:, :], op=ALU.mult), reads=["kTd", "pBR"], writes=[("kbT", g)])
                for c in range(4):
                    n = g * 4 + c
                    O("dve", lambda e, c=c, n=n: e.scalar_tensor_tensor(out=e1[:, c, :], in0=pGR[:, c * 128:(c + 1) * 128], scalar=gch[:, n:n + 1], in1=dmU[:], op0=ALU.subtract, op1=ALU.add),
                      reads=["pGR", "gch", "dmU"], writes=["e1"])
                    O("dve", lambda e, c=c, n=n: e.scalar_tensor_tensor(out=e2[:, c, :], in0=pGR[:, c * 128:(c + 1) * 128], scalar=gch[:, n:n + 1], in1=dmL[:], op0=ALU.subtract, op1=ALU.add),
                      reads=["pGR", "gch", "dmL"], writes=["e2"])
                yield
                O("act", lambda e: e.activation(out=DmI[:], in_=e1[:], func=AF.Exp), reads=["e1"], writes=["DmI"])
                O("act", lambda e: e.activation(out=Dp[:], in_=e2[:], func=AF.Exp, scale=-1.0), reads=["e2"], writes=["Dp"])
                O("dve", lambda e, t0=t0: e.tensor_tensor(out=qgT[:, t0:t0 + 512], in0=qTd[:, t0:t0 + 512], in1=eg[:], op=ALU.mult), reads=["qTd", "eg"], writes=[("qgT", g)])
                O("pool", lambda e: e.tensor_tensor(out=DmS[:], in0=DmI[:], in1=identf[:].unsqueeze(1).to_broadcast([128, 4, 128]), op=ALU.subtract), reads=["DmI", "identf"], writes=["DmS"])
                for c in range(4):
                    cs = slice(t0 + c * 128, t0 + (c + 1) * 128)
                    O("pe", lambda e, c=c, cs=cs: e.matmul(pG1[:, c, :], lhsT=kTd[:, cs], rhs=kbT[:, cs], start=True, stop=True), reads=["kTd", ("kbT", g)], writes=["pG1"])
                    O("pe", lambda e, c=c, cs=cs: e.matmul(pG2[:, c, :], lhsT=kbT[:, cs], rhs=kTd[:, cs], start=True, stop=True), reads=["kTd", ("kbT", g)], writes=["pG2"])
                    O("pe", lambda e, c=c, cs=cs: e.matmul(pG3[:, c, :], lhsT=kTd[:, cs], rhs=qTd[:, cs], start=True, stop=True), reads=["kTd", "qTd"], writes=["pG3"])
                yield
                O("dve", lambda e: e.scalar_tensor_tensor(out=Bb[0][:], in0=pG1[:], scalar=-1.0, in1=DmS[:], op0=ALU.mult, op1=ALU.mult), reads=["pG1", "DmS"], writes=["Bb0"])
                O("dve", lambda e: e.scalar_tensor_tensor(out=BTb[0][:], in0=pG2[:], scalar=-1.0, in1=Dp[:], op0=ALU.mult, op1=ALU.mult), reads=["pG2", "Dp"], writes=["BTb0"])
                O("dve", lambda e, gs=gs: e.tensor_tensor(out=qkT[gs][:], in0=pG3[:], in1=DmI[:], op=ALU.mult), reads=["pG3", "DmI"], writes=[f"qkT{gs}"])
                O("pool", lambda e: e.tensor_tensor(out=Xb[0][:], in0=Bb[0][:], in1=identf[:].unsqueeze(1).to_broadcast([128, 4, 128]), op=ALU.add), reads=["Bb0", "identf"], writes=["Xb0"])
                yield
                cur = 0
                for k in range(1, 7):
                    nx = 1 - cur
                    for c in range(4):
                        if k < 6:
                            O("pe", lambda e, c=c, cur=cur: e.matmul(pG1[:, c, :], lhsT=BTb[cur][:, c, :], rhs=Bb[cur][:, c, :], start=True, stop=True),
                              reads=[f"BTb{cur}", f"Bb{cur}"], writes=["pG1"])
                        O("pe", lambda e, c=c, cur=cur: e.matmul(pG2[:, c, :], lhsT=Bb[cur][:, c, :], rhs=BTb[cur][:, c, :], start=True, stop=True),
                          reads=[f"BTb{cur}", f"Bb{cur}"], writes=["pG2"])
                    yield
                    if k < 6:
                        O("act", lambda e, nx=nx: e.activation(out=Bb[nx][:], in_=pG1[:], func=AF.Copy), reads=["pG1"], writes=[f"Bb{nx}"])
                    O("act", lambda e, nx=nx: e.activation(out=BTb[nx][:], in_=pG2[:], func=AF.Copy), reads=["pG2"], writes=[f"BTb{nx}"])
                    for c in range(4):
                        O("pe", lambda e, c=c, cur=cur, nx=nx: e.matmul(pG3[:, c, :], lhsT=BTb[nx][:, c, :], rhs=Xb[cur][:, c, :], start=True, stop=True),
                          reads=[f"BTb{nx}", f"Xb{cur}"], writes=["pG3"])
                    yield
                    O("dve", lambda e, nx=nx, cur=cur: e.tensor_tensor(out=Xb[nx][:], in0=pG3[:], in1=Xb[cur][:], op=ALU.add), reads=["pG3", f"Xb{cur}"], writes=[f"Xb{nx}"])
                    cur = nx
                for c in range(4):
                    n = g * 4 + c
                    O("pe", lambda e, c=c, n=n, cur=cur: e.matmul(pG1[:, c, :], lhsT=Xb[cur][:, c, :], rhs=vb[:, n, :], start=True, stop=True), reads=[f"Xb{cur}", "vb"], writes=["pG1"])
                    O("pe", lambda e, c=c, n=n, cur=cur: e.matmul(pG2[:, c, :], lhsT=kbg[:, n, :], rhs=Xb[cur][:, c, :], start=True, stop=True), reads=[f"Xb{cur}", "kbg"], writes=["pG2"])
                yield
                O("act", lambda e, gs=gs: e.activation(out=u_sb[gs][:], in_=pG1[:], func=AF.Copy), reads=["pG1"], writes=[f"u_sb{gs}"])
                O("dve", lambda e, gs=gs: e.tensor_copy(out=wT_sb[gs][:], in_=pG2[:]), reads=["pG2"], writes=[f"wT_sb{gs}"])
                yield

            def gen_scan(g, h=h):
                gs = g % 2
                t0 = g * 512
                for c in range(4):
                    n = g * 4 + c
                    vs = n % 2
                    cs = slice(t0 + c * 128, t0 + (c + 1) * 128)
                    O("pe", lambda e, c=c, gs=gs: e.matmul(pWS[:, 0:128], lhsT=wT_sb[gs][:, c, :], rhs=S_b[:], start=True, stop=True), reads=[f"wT_sb{gs}", "S_b"], writes=["pWS"])
                    yield
                    O("dve", lambda e, c=c, vs=vs, gs=gs: e.tensor_tensor(out=vnew[vs][:], in0=u_sb[gs][:, c, :], in1=pWS[:, 0:128], op=ALU.subtract), reads=[f"u_sb{gs}", "pWS"], writes=[f"vnew{vs}"])
                    if n % 4 != 0:
                        O("pe", lambda e, cs=cs: e.matmul(pOT[:, 0:128], lhsT=S_b[:], rhs=qgT[:, cs], start=True, stop=False), reads=["S_b", ("qgT", g)], writes=["pOT"])
                    yield
                    O("pe", lambda e, n=n, vs=vs: e.matmul(pSU[:, 0:128], lhsT=kg[:, n, :], rhs=vnew[vs][:], start=True, stop=True), reads=["kg", f"vnew{vs}"], writes=["pSU"])
                    if n % 4 != 0:
                        O("pe", lambda e, c=c, vs=vs, gs=gs: e.matmul(pOT[:, 0:128], lhsT=vnew[vs][:], rhs=qkT[gs][:, c, :], start=False, stop=True), reads=[f"vnew{vs}", f"qkT{gs}"], writes=["pOT"])
                        blk = n // 4
                        if n % 4 == 1:
                            dsl, ssl = slice(0, 2), slice(126, 128)
                        elif n % 4 == 2:
                            dsl, ssl = slice(2, 130), slice(0, 128)
                        else:
                            dsl, ssl = slice(130, 258), slice(0, 128)
                    yield
                    O("dve", lambda e, n=n: e.scalar_tensor_tensor(out=S_f[:], in0=S_f[:], scalar=egl[:, n:n + 1], in1=pSU[:, 0:128], op0=ALU.mult, op1=ALU.add),
                      reads=["S_f", "egl", "pSU"], writes=["S_f"])
                    O("act", lambda e: e.activation(out=S_b[:], in_=S_f[:], func=AF.Copy), reads=["S_f"], writes=["S_b"])
                    if n % 4 != 0:
                        O("act", lambda e, blk=blk, dsl=dsl, ssl=ssl: e.activation(out=oTm[:, blk, dsl], in_=pOT[:, ssl], func=AF.Copy), reads=["pOT"], writes=["oTm"])
                    yield

            for _ in gen_inv(0):
                pass
            for g in range(8):
                gi = gen_inv(g + 1) if g + 1 < 8 else iter(())
                gsn = gen_scan(g)
                done_i = done_s = False
                while not (done_i and done_s):
                    if not done_s:
                        try:
                            next(gsn)
                        except StopIteration:
                            done_s = True
                    if not done_i:
                        try:
                            next(gi)
                        except StopIteration:
                            done_i = True
            for blk in range(NB):
                q0 = blk * QW
                zs = blk % 2
                dma(ztl[zs][:], z_s[h, :, q0:q0 + QW], [("zg_s", h)], [f"ztl{zs}"])
                O("act", lambda e, blk=blk: e.activation(out=sqd[:], in_=oTm[:, blk, :], func=AF.Square), reads=["oTm"], writes=["sqd"])
                O("pe", lambda e: e.matmul(pGR[:, 0:QW], lhsT=onesb[:], rhs=sqd[:], start=True, stop=True), reads=["onesb", "sqd"], writes=["pGR"])
                rsqrt_cols(pGR[:, 0:QW], rsd[:], 1.0 / 128, ["pGR"], ["rsd"], sdd[:], "sdd")
                O("dve", lambda e, blk=blk: e.scalar_tensor_tensor(out=tno[:], in0=oTm[:, blk, :], scalar=bont[:, 0:1], in1=rsd[:], op0=ALU.mult, op1=ALU.mult),
                  reads=["oTm", "bont", "rsd"], writes=["tno"])
                O("dve", lambda e, h=h, q0=q0, zs=zs: e.tensor_tensor(out=obT[:, h, q0:q0 + QW], in0=tno[:], in1=ztl[zs][:], op=ALU.mult),
                  reads=["tno", f"ztl{zs}"], writes=[("obT", blk)])
        S.barrier()
        S.flush()


def build_ffn(nc, S, O, SB, PS, dma, cst, rsqrt_cols, h2T_s, xmid_s, w_up, w_dn, fcw, fcb, gfin, y):
    es = ExitStack()
    with es:
        wup = SB(es, "wup", [128, 8, 5632], BF16)
        wdn = SB(es, "wdn", [128, 22, 1024], BF16)
        fcwt = SB(es, "fcwt", [128, 132], F32)
        fcbt = SB(es, "fcbt", [128, 44], F32)
        gfr = SB(es, "gfr", [128, 1024], F32)
        h2 = [SB(es, f"h2b{i}", [128, 8, QW], BF16) for i in range(2)]
        actT = SB(es, "actT", [128, 22, 256], BF16)
        cg = [SB(es, f"cg{i}", [128, 256], F32) for i in range(2)]
        cu = [SB(es, f"cu{i}", [128, 256], F32) for i in range(2)]
        ctmp = [SB(es, f"ctmp{i}", [128, 256], F32) for i in range(2)]
        sg = [SB(es, f"sg{i}", [128, 256], F32) for i in range(2)]
        uu = [SB(es, f"uu{i}", [128, QW], F32) for i in range(2)]
        xmt = [SB(es, f"xmt{i}", [128, 1024], F32) for i in range(2)]
        junkF = SB(es, "junkF", [128, 1024], BF16)
        stF = SB(es, "stF", [128, 3 * 16], F32)
        pUg = [PS(es, f"pUg{i}", [128, 512]) for i in range(2)]
        pUu = [PS(es, f"pUu{i}", [128, 512]) for i in range(2)]
        pDn = [PS(es, f"pDn{i}", [128, 512]) for i in range(2)]
        w_up_v = w_up.rearrange("(k p) c -> p k c", p=128)
        for cb in range(6):
            for base in (0, 2816):
                c0 = base + cb * 512
                c1 = min(base + (cb + 1) * 512, base + 2816)
                dma(wup[:, :, c0:c1], w_up_v[:, :, c0:c1], [], [("wup", base, cb)], eng="pool")
        dma(wdn[:], w_dn.rearrange("(f p) c -> p f c", p=128), [], ["wdn"], eng="pool")
        dma(fcwt[:], fcw, [], ["fcwt"])
        dma(fcbt[:], fcb, [], ["fcbt"])
        dma(gfr[:], gfin.partition_broadcast(128), [], ["gfr"])
        O("dve", lambda e: e.memset(stF[:], 0.0), writes=["stF"])
        col = 0
        for i in range(NB):
            q0 = i * QW
            hs = i % 2
            dma(h2[hs][:], h2T_s[:, :, q0:q0 + QW], [("h2T_s", i)], [f"h2b{hs}"])
            for fc in range(22):
                s = fc % 2
                for (ps_, cb, nm) in ((pUg, fc * 128, "pUg"), (pUu, 2816 + fc * 128, "pUu")):
                    for kc in range(8):
                        O("pe", lambda e, ps_=ps_, s=s, kc=kc, cb=cb, hs=hs: e.matmul(ps_[s][:, 0:QW], lhsT=wup[:, kc, cb:cb + 128], rhs=h2[hs][:, kc, :], start=(kc == 0), stop=(kc == 7)),
                          reads=[("wup", 0 if nm == "pUg" else 2816, fc // 4), f"h2b{hs}"], writes=[f"{nm}{s}"])
                wg = lambda k: fcwt[:, fc * 3 + k:fc * 3 + k + 1]
                wu = lambda k: fcwt[:, (22 + fc) * 3 + k:(22 + fc) * 3 + k + 1]
                O("act", lambda e, s=s, w=wg(0): e.activation(out=cg[s][:], in_=pUg[s][:, 0:256], func=AF.Copy, scale=w), reads=[f"pUg{s}", "fcwt"], writes=[f"cg{s}"])
                for k in (1, 2):
                    O("dve", lambda e, s=s, k=k, w=wg(k): e.scalar_tensor_tensor(out=cg[s][:], in0=pUg[s][:, k:k + 256], scalar=w, in1=cg[s][:], op0=ALU.mult, op1=ALU.add),
                      reads=[f"pUg{s}", "fcwt", f"cg{s}"], writes=[f"cg{s}"])
                O("act", lambda e, s=s, fc=fc: e.activation(out=sg[s][:], in_=cg[s][:], func=AF.Silu, bias=fcbt[:, fc:fc + 1]), reads=[f"cg{s}", "fcbt"], writes=[f"sg{s}"])
                O("act", lambda e, s=s, w=wu(0): e.activation(out=cu[s][:], in_=pUu[s][:, 0:256], func=AF.Copy, scale=w), reads=[f"pUu{s}", "fcwt"], writes=[f"cu{s}"])
                for k in (1, 2):
                    O("dve", lambda e, s=s, k=k, w=wu(k): e.scalar_tensor_tensor(out=cu[s][:], in0=pUu[s][:, k:k + 256], scalar=w, in1=cu[s][:], op0=ALU.mult, op1=ALU.add),
                      reads=[f"pUu{s}", "fcwt", f"cu{s}"], writes=[f"cu{s}"])
                O("dve", lambda e, s=s, fc=fc: e.scalar_tensor_tensor(out=actT[:, fc, :], in0=cu[s][:], scalar=fcbt[:, 22 + fc:23 + fc], in1=sg[s][:], op0=ALU.add, op1=ALU.mult),
                  reads=[f"cu{s}", f"sg{s}", "fcbt"], writes=["actT"])
            for sub in range(2):
                xsl = (i * 2 + sub) % 2
                dma(xmt[xsl][:], xmid_s[i, 2 + sub * 128:2 + (sub + 1) * 128, :], [("xmid_s", i)], [f"xmt{xsl}"])
                for half in range(2):
                    for fc in range(22):
                        O("pe", lambda e, half=half, fc=fc, sub=sub: e.matmul(pDn[half][:, :], lhsT=actT[:, fc, sub * 128:(sub + 1) * 128], rhs=wdn[:, fc, half * 512:(half + 1) * 512],
                                                                            start=(fc == 0), stop=(fc == 21)),
                          reads=["actT", "wdn"], writes=[f"pDn{half}"])
                    O("dve", lambda e, half=half, xsl=xsl: e.tensor_tensor(out=xmt[xsl][:, half * 512:(half + 1) * 512], in0=xmt[xsl][:, half * 512:(half + 1) * 512], in1=pDn[half][:, :], op=ALU.add),
                      reads=[f"xmt{xsl}", f"pDn{half}"], writes=[f"xmt{xsl}"])
                O("act", lambda e, xsl=xsl, col=col: e.activation(out=junkF[:], in_=xmt[xsl][:], func=AF.Square, accum_out=stF[:, col:col + 1]),
                  reads=[f"xmt{xsl}", "stF"], writes=["junkF", "stF"])
                rsqrt_cols(stF[:, col:col + 1], stF[:, 32 + col:33 + col], 1.0 / 1024, ["stF"], ["stF"], stF[:, 16 + col:17 + col], "stF")
                O("dve", lambda e, xsl=xsl, col=col: e.scalar_tensor_tensor(out=xmt[xsl][:], in0=xmt[xsl][:], scalar=stF[:, 32 + col:33 + col], in1=gfr[:], op0=ALU.mult, op1=ALU.mult),
                  reads=[f"xmt{xsl}", "stF", "gfr"], writes=[f"xmt{xsl}"])
                r0 = i * 256 + sub * 128
                dma(y[r0:r0 + 128, :], xmt[xsl][:], [f"xmt{xsl}"], ["y"])
                col += 1
        S.barrier()
        S.flush()


def _host_consts():
    bf = ml_dtypes.bfloat16
    c = {}
    c["c_identb"] = np.eye(128, dtype=np.float32).astype(bf)
    c["c_identf"] = np.eye(128, dtype=np.float32)
    sel = np.zeros((8, 8, 128), np.float32)
    for h in range(8):
        sel[h, h, :] = 1.0
    c["c_sel"] = sel.reshape(8, 1024)
    c["c_sel16"] = np.concatenate([sel, sel], axis=0).reshape(16, 1024).astype(bf)
    p = np.arange(128)
    invf = (10000.0 ** (-(np.arange(0, 64, 2, dtype=np.float32)) / 64.0)).astype(np.float32)
    rope = np.zeros((128, 4), np.float32)
    rope[:, 0] = (invf[p % 32].astype(np.float64) / (2 * np.pi)).astype(np.float32)
    sgn = np.where((p % 64) < 32, -1.0, 1.0)
    rope[:, 1] = (2 * np.pi * sgn).astype(np.float32)
    rope[:, 2] = np.float32(2 * np.pi)
    c["c_rope"] = rope
    col = np.arange(QW)
    am = np.zeros((128, 3, QW), np.float32)
    for r in range(3):
        am[:, r, :] = (p[:, None] + 128 * r <= 126 + col[None, :]).astype(np.float32)
    c["c_amask"] = am.reshape(128, 3 * QW).astype(bf)
    i = np.arange(128)
    c["c_dmU"] = np.where(i[None, :] >= p[:, None], 0.0, NEG).astype(np.float32)
    c["c_dmL"] = np.where(i[None, :] < p[:, None], 0.0, -NEG).astype(np.float32)
    return c


_NC_CACHE = {}


def _core_inputs(b, c, x, positions, shared):
    d = dict(shared)
    if c == 0:
        xs = np.concatenate([np.zeros((256, 1024), np.float32), x[b, :3840]], axis=0)
        ps = np.concatenate([np.zeros((256,), np.int32), positions[b, :3840]], axis=0)
        kb = np.zeros((128, 32), np.float32)
        kb[:, 0:2] = NEG
    else:
        xs = x[b]
        ps = positions[b]
        kb = np.zeros((128, 32), np.float32)
    d["xs"] = np.ascontiguousarray(xs)
    d["pos"] = np.ascontiguousarray(ps.reshape(1, T).astype(np.int32))
    d["kbias"] = kb
    return d


def _shared_inputs(norm_mix, w_in, lambda_q1, lambda_k1, lambda_q2, lambda_k2, a_subln, w_a_out, conv_qkv, a_log, dt_bias,
                   b_onorm, w_b_out, w_o, norm_ffn, w_up, ffn_conv, ffn_conv_bias, w_down, norm_final):
    f = lambda a: np.ascontiguousarray(np.asarray(a, dtype=np.float32))
    d = _host_consts()
    d["w_in"] = f(w_in[0]); d["w_a"] = f(w_a_out[0]); d["w_b"] = f(w_b_out[0]); d["w_o"] = f(w_o[0])
    d["w_up"] = f(w_up[0]); d["w_dn"] = f(w_down[0])
    d["gmix"] = f(np.asarray(norm_mix[0]).reshape(8, 128).T)
    d["gffn"] = f(np.asarray(norm_ffn[0]).reshape(8, 128).T)
    d["gfin"] = f(np.asarray(norm_final).reshape(1, 1024))
    d["lam4"] = f(np.concatenate([np.asarray(lambda_q1[0]), np.asarray(lambda_k1[0]), np.asarray(lambda_q2[0]), np.asarray(lambda_k2[0])]).reshape(1, 256))
    d["asub"] = f(np.asarray(a_subln[0]).reshape(128, 1))
    d["bon"] = f(np.asarray(b_onorm[0]).reshape(128, 1))
    d["cw"] = f(np.asarray(conv_qkv[0]).reshape(4, 24, 128).transpose(2, 1, 0).reshape(128, 96))
    d["alog"] = f(np.asarray(a_log[0]).reshape(8, 1))
    d["dtb"] = f(np.asarray(dt_bias[0]).reshape(8, 1))
    d["fcw"] = f(np.asarray(ffn_conv[0]).reshape(3, 44, 128).transpose(2, 1, 0).reshape(128, 132))
    d["fcb"] = f(np.asarray(ffn_conv_bias[0]).reshape(44, 128).T)
    return d


def kernel(x, positions, norm_mix, w_in, lambda_q1, lambda_k1, lambda_q2, lambda_k2, a_subln, w_a_out, conv_qkv, a_log,
           dt_bias, b_onorm, w_b_out, w_o, norm_ffn, w_up, ffn_conv, ffn_conv_bias, w_down, norm_final):
    x = np.asarray(x, dtype=np.float32)
    positions = np.asarray(positions, dtype=np.int32)
    shared = _shared_inputs(norm_mix, w_in, lambda_q1, lambda_k1, lambda_q2, lambda_k2, a_subln, w_a_out, conv_qkv, a_log,
                            dt_bias, b_onorm, w_b_out, w_o, norm_ffn, w_up, ffn_conv, ffn_conv_bias, w_down, norm_final)
    nc = build_nc()
    in_maps = []
    for core in range(8):
        b, c = core // 2, core % 2
        in_maps.append(_core_inputs(b, c, x, positions, shared))
    res = run_bass_kernel_spmd(nc, in_maps, core_ids=list(range(8)))
    out = np.zeros((4, 4096, 1024), np.float32)
    for core in range(8):
        b, c = core // 2, core % 2
        yc = np.asarray(res.results[core]["y"]).reshape(NB, 256, 1024)
        for i in range(NB):
            g = 2 * i + c
            out[b, g * 256:(g + 1) * 256] = yc[i]
    return out
```
